# Optimizing a Trainium2 kernel written in Bass

```python
import math
import jax
import jax.numpy as jnp
from jax import lax
import numpy as np

D_MODEL = 2048
BATCH = 8
SEQ = 2048
DEPTH = 4

CTX_LEN = 256
GRID_W = 64
HEAD_DIM = 128
N_HEAD_SLOTS = D_MODEL // HEAD_DIM
GDN_HEADS = N_HEAD_SLOTS // 2
GDN_DK = HEAD_DIM
GDN_DV = HEAD_DIM
GDN_CONV = 3
GDN_CHUNK = 64
DIFF_HEADS = N_HEAD_SLOTS // 2
DIFF_DH = HEAD_DIM // 2
Q_BLOCK = 128
NA_HEADS = N_HEAD_SLOTS // 2
NA_ROWS = 8
NA_COLS = 16
NA_QCOLS = 16
NA_KSPAN = 32
WIN_HEADS = N_HEAD_SLOTS // 2
WIN_KV_HEADS = 2
WIN = 128
WIN_BLOCK = 128
FFN_HIDDEN = -(-8 * D_MODEL // (3 * 256)) * 256
ROPE_BASE = 10000.0
NORM_EPS = 1e-6

GDN_QKV = GDN_HEADS * (2 * GDN_DK + GDN_DV)
EVEN_WIDTHS = (GDN_QKV, GDN_HEADS * GDN_DV, 2 * GDN_HEADS, 2 * GDN_HEADS,
               DIFF_HEADS * 2 * DIFF_DH, DIFF_HEADS * 2 * DIFF_DH, DIFF_HEADS * 2 * DIFF_DH)
ODD_WIDTHS = (NA_HEADS * HEAD_DIM, NA_HEADS * HEAD_DIM, NA_HEADS * HEAD_DIM,
              WIN_HEADS * HEAD_DIM, WIN_KV_HEADS * HEAD_DIM, WIN_KV_HEADS * HEAD_DIM)

kernel_name = 'hybrid_gdn_diff_natten_swa_dit_trunk'


def rmsnorm(x, g):
    xf = x.astype(jnp.float32)
    y = xf * lax.rsqrt(jnp.mean(xf * xf, axis=-1, keepdims=True) + NORM_EPS)
    return (y * g.astype(jnp.float32)).astype(x.dtype)


def l2norm(x):
    xf = x.astype(jnp.float32)
    return (xf * lax.rsqrt(jnp.sum(xf * xf, axis=-1, keepdims=True) + NORM_EPS)).astype(x.dtype)


def split_cols(p, widths):
    return jnp.split(p, np.cumsum(widths)[:-1].tolist(), axis=-1)


def _rope_axis(x, pos):
    n = x.shape[-1]
    inv = ROPE_BASE ** (-jnp.arange(0, n, 2, dtype=jnp.float32) / n)
    ang = pos.astype(jnp.float32)[:, None] * inv[None, :]
    bshape = (pos.shape[0],) + (1,) * (x.ndim - 3) + (n // 2,)
    cos = jnp.cos(ang).reshape(bshape).astype(x.dtype)
    sin = jnp.sin(ang).reshape(bshape).astype(x.dtype)
    x1, x2 = x[..., : n // 2], x[..., n // 2:]
    return jnp.concatenate([x1 * cos - x2 * sin, x2 * cos + x1 * sin], axis=-1)


def rope2d(x, rows, cols):
    d = x.shape[-1]
    return jnp.concatenate([_rope_axis(x[..., : d // 2], rows), _rope_axis(x[..., d // 2:], cols)], axis=-1)


def short_conv(x, w):
    k = w.shape[0]
    return lax.conv_general_dilated(x, w[:, None, :].astype(x.dtype), window_strides=(1,),
                                    padding=[(k // 2, k // 2)], dimension_numbers=('NWC', 'WIO', 'NWC'),
                                    feature_group_count=x.shape[-1])


def gated_delta_chunked(q, k, v, g, beta, s0):
    bsz, n, h, dk = q.shape
    dv = v.shape[-1]
    c = GDN_CHUNK
    nc = n // c

    def chunks(a):
        a = a.astype(jnp.float32).reshape((bsz, nc, c, h) + a.shape[3:])
        return jnp.moveaxis(a, 3, 1)

    qc = chunks(q) * dk ** -0.5
    kc = chunks(k)
    vc = chunks(v)
    bc = chunks(beta)
    gc = jnp.cumsum(chunks(g), axis=-1)
    tril = jnp.asarray(np.tril(np.ones((c, c), bool)))
    strict = jnp.asarray(np.tril(np.ones((c, c), bool), -1))
    decay = jnp.exp(jnp.where(tril, gc[..., :, None] - gc[..., None, :], -jnp.inf))
    kb = kc * bc[..., None]
    a_mat = jnp.where(strict, jnp.einsum('bhnid,bhnjd->bhnij', kb, kc) * decay, 0.0)
    eye = jnp.eye(c, dtype=jnp.float32)
    t_mat = lax.linalg.triangular_solve(eye + a_mat, jnp.broadcast_to(eye, a_mat.shape),
                                        left_side=True, lower=True)
    u = jnp.einsum('bhnij,bhnjd->bhnid', t_mat, vc * bc[..., None])
    w = jnp.einsum('bhnij,bhnjd->bhnid', t_mat, kb * jnp.exp(gc)[..., None])
    attn = jnp.where(tril, jnp.einsum('bhnid,bhnjd->bhnij', qc, kc) * decay, 0.0)
    qg = qc * jnp.exp(gc)[..., None]
    kdec = kc * jnp.exp(gc[..., -1:] - gc)[..., None]
    g_last = jnp.exp(gc[..., -1])

    def step(s, xs):
        qg_i, kdec_i, u_i, w_i, attn_i, gl_i = xs
        v_new = u_i - jnp.einsum('bhcd,bhde->bhce', w_i, s)
        o_i = jnp.einsum('bhcd,bhde->bhce', qg_i, s) + jnp.einsum('bhij,bhje->bhie', attn_i, v_new)
        s = s * gl_i[..., None, None] + jnp.einsum('bhcd,bhce->bhde', kdec_i, v_new)
        return s, o_i

    xs = tuple(jnp.moveaxis(a, 2, 0) for a in (qg, kdec, u, w, attn, g_last))
    s_fin, o = lax.scan(step, s0.astype(jnp.float32), xs)
    o = jnp.moveaxis(o, 0, 2).reshape(bsz, h, n, dv).transpose(0, 2, 1, 3)
    return o.astype(v.dtype), s_fin


def gdn_bidirectional(q, k, v, g, beta, qc, kc, vc, gc, betac):
    bsz, _, h, dk = q.shape
    s0 = jnp.zeros((bsz, h, dk, v.shape[-1]), jnp.float32)
    flip = lambda a: jnp.flip(a, axis=1)
    oc_f, sc_f = gated_delta_chunked(qc, kc, vc, gc[:, :, 0], betac[:, :, 0], s0)
    o_f, _ = gated_delta_chunked(q, k, v, g[:, :, 0], beta[:, :, 0], sc_f)
    oc_b, sc_b = gated_delta_chunked(flip(qc), flip(kc), flip(vc), flip(gc[:, :, 1]), flip(betac[:, :, 1]), s0)
    o_b, _ = gated_delta_chunked(flip(q), flip(k), flip(v), flip(g[:, :, 1]), flip(beta[:, :, 1]), sc_b)
    return o_f + flip(o_b), oc_f + flip(oc_b)


def diff_attention(q, k, v, qc, kc, vc, lam, rows, cols, need_ctx):
    bsz, n, h = q.shape[:3]
    q = rope2d(q, rows, cols)
    k = rope2d(k, rows, cols)
    k_all = jnp.concatenate([k, kc], axis=1)
    v_all = jnp.concatenate([v, vc], axis=1)
    scale = q.shape[-1] ** -0.5

    def attend(qb, kk, vv):
        s = jnp.einsum('bqhcd,bkhcd->bhcqk', qb, kk).astype(jnp.float32) * scale
        p = jax.nn.softmax(s, axis=-1)
        a = (p[:, :, 0] - lam * p[:, :, 1]).astype(vv.dtype)
        return jnp.einsum('bhqk,bkhe->bqhe', a, vv)

    qb = jnp.moveaxis(q.reshape((bsz, n // Q_BLOCK, Q_BLOCK) + q.shape[2:]), 1, 0)
    o = lax.map(lambda blk: attend(blk, k_all, v_all), qb)
    o = jnp.moveaxis(o, 0, 1).reshape(bsz, n, h, v.shape[-1])
    oc = attend(qc, kc, vc) if need_ctx else None
    return o, oc


def neighbourhood_attention(q, k, v, kc, vc, rpb):
    bsz, n, h, d = q.shape
    n_rows = n // GRID_W
    kr = min(NA_ROWS, n_rows)
    scale = d ** -0.5
    ncb = GRID_W // NA_QCOLS
    q_cols = np.arange(GRID_W).reshape(ncb, NA_QCOLS)
    win_c = np.clip(q_cols - NA_COLS // 2, 0, GRID_W - NA_COLS)
    span0 = np.clip(np.arange(ncb) * NA_QCOLS - NA_COLS // 2, 0, GRID_W - NA_KSPAN)
    key_cols = span0[:, None] + np.arange(NA_KSPAN)
    kcol = key_cols[:, None, :]
    col_ok = (kcol >= win_c[..., None]) & (kcol < win_c[..., None] + NA_COLS)
    dc_idx = np.clip(kcol - q_cols[..., None] + NA_COLS - 1, 0, 2 * NA_COLS - 2)
    nl = kr * NA_KSPAN
    mask = jnp.asarray(np.broadcast_to(col_ok[:, :, None, :], (ncb, NA_QCOLS, kr, NA_KSPAN)).reshape(ncb, NA_QCOLS, nl))
    rpb_c = rpb[:, :, dc_idx]
    kg = k.reshape(bsz, n_rows, GRID_W, h, d)
    vg = v.reshape(bsz, n_rows, GRID_W, h, d)

    def row_block(args):
        r, q_r = args
        start = jnp.clip(r - kr // 2, 0, n_rows - kr)

        def gather(a):
            a = lax.dynamic_slice_in_dim(a, start, kr, axis=1)[:, :, key_cols]
            return a.transpose(0, 2, 1, 3, 4, 5).reshape(bsz, ncb, nl, h, d)

        k_r, v_r = gather(kg), gather(vg)
        q_b = q_r.reshape(bsz, ncb, NA_QCOLS, h, d)
        dr_idx = start + jnp.arange(kr) - r + NA_ROWS - 1
        bias = rpb_c[:, dr_idx].transpose(0, 2, 3, 1, 4).reshape(h, ncb, NA_QCOLS, nl)
        s_loc = jnp.einsum('bnqhd,bnkhd->bhnqk', q_b, k_r).astype(jnp.float32) * scale + bias
        s_loc = jnp.where(mask, s_loc, -jnp.inf)
        s_ctx = jnp.einsum('bnqhd,bkhd->bhnqk', q_b, kc).astype(jnp.float32) * scale
        p = jax.nn.softmax(jnp.concatenate([s_loc, s_ctx], axis=-1), axis=-1).astype(v.dtype)
        o = (jnp.einsum('bhnqk,bnkhd->bnqhd', p[..., :nl], v_r)
             + jnp.einsum('bhnqk,bkhd->bnqhd', p[..., nl:], vc))
        return o.reshape(bsz, GRID_W, h, d)

    qg = jnp.moveaxis(q.reshape(bsz, n_rows, GRID_W, h, d), 1, 0)
    o = lax.map(row_block, (jnp.arange(n_rows), qg))
    return jnp.moveaxis(o, 0, 1).reshape(bsz, n, h, d)


def context_attention(qc, kc, vc):
    s = jnp.einsum('bqhd,bkhd->bhqk', qc, kc).astype(jnp.float32) * qc.shape[-1] ** -0.5
    p = jax.nn.softmax(s, axis=-1).astype(vc.dtype)
    return jnp.einsum('bhqk,bkhd->bqhd', p, vc)


def window_sink_attention(q, k, v, kc, vc, sink, rows, cols):
    bsz, n, h, d = q.shape
    kvh = k.shape[2]
    grp = h // kvh
    q = rope2d(q, rows, cols)
    k = rope2d(k, rows, cols)
    scale = d ** -0.5
    nb = n // WIN_BLOCK

    def band(a):
        ap = jnp.pad(a, ((0, 0), (WIN_BLOCK, WIN_BLOCK), (0, 0), (0, 0))).reshape(bsz, nb + 2, WIN_BLOCK, kvh, d)
        return jnp.concatenate([ap[:, :-2], ap[:, 1:-1], ap[:, 2:]], axis=2)

    kb, vb = band(k), band(v)
    qb = q.reshape(bsz, nb, WIN_BLOCK, kvh, grp, d)
    qpos = jnp.arange(n).reshape(nb, WIN_BLOCK)
    kpos = (jnp.arange(nb) * WIN_BLOCK - WIN_BLOCK)[:, None] + jnp.arange(3 * WIN_BLOCK)[None, :]
    kp = kpos[:, None, :]
    valid = (kp >= 0) & (kp < n) & (jnp.abs(qpos[:, :, None] - kp) <= WIN)
    s_loc = jnp.where(valid, jnp.einsum('bnqhgd,bnkhd->bhgnqk', qb, kb).astype(jnp.float32) * scale, -jnp.inf)
    s_ctx = jnp.einsum('bnqhgd,bkhd->bhgnqk', qb, kc).astype(jnp.float32) * scale
    sink_l = jnp.broadcast_to(sink.astype(jnp.float32).reshape(1, kvh, grp, 1, 1, 1), s_ctx.shape[:-1] + (1,))
    p = jax.nn.softmax(jnp.concatenate([s_loc, s_ctx, sink_l], axis=-1), axis=-1).astype(v.dtype)
    nl = 3 * WIN_BLOCK
    o = (jnp.einsum('bhgnqk,bnkhd->bnqhgd', p[..., :nl], vb)
         + jnp.einsum('bhgnqk,bkhd->bnqhgd', p[..., nl:-1], vc))
    return o.reshape(bsz, n, h, d)


def context_sink_attention(qc, kc, vc, sink):
    bsz, m, h, d = qc.shape
    kvh = kc.shape[2]
    grp = h // kvh
    qg = qc.reshape(bsz, m, kvh, grp, d)
    s = jnp.einsum('bqhgd,bkhd->bhgqk', qg, kc).astype(jnp.float32) * d ** -0.5
    sink_l = jnp.broadcast_to(sink.astype(jnp.float32).reshape(1, kvh, grp, 1, 1), s.shape[:-1] + (1,))
    p = jax.nn.softmax(jnp.concatenate([s, sink_l], axis=-1), axis=-1).astype(vc.dtype)[..., :-1]
    return jnp.einsum('bhgqk,bkhd->bqhgd', p, vc).reshape(bsz, m, h, d)


def swiglu(u, wg, wu, wd):
    return (jax.nn.silu(u @ wg) * (u @ wu)) @ wd


def even_mixer(u, uc, w_in, conv_w, a_log, dt_bias, gdn_g, lam_q1, lam_k1, lam_q2, lam_k2, subln_g,
               lambda_init, rows, cols, need_ctx):
    def project(t):
        bsz, n = t.shape[:2]
        qkv, z, a, b, dq, dk, dv = split_cols(t @ w_in, EVEN_WIDTHS)
        qkv = jax.nn.silu(short_conv(qkv, conv_w))
        gq, gk, gv = split_cols(qkv, (GDN_HEADS * GDN_DK, GDN_HEADS * GDN_DK, GDN_HEADS * GDN_DV))
        gq = l2norm(gq.reshape(bsz, n, GDN_HEADS, GDN_DK))
        gk = l2norm(gk.reshape(bsz, n, GDN_HEADS, GDN_DK))
        gv = gv.reshape(bsz, n, GDN_HEADS, GDN_DV)
        g = -jnp.exp(a_log.astype(jnp.float32)) * jax.nn.softplus(
            a.reshape(bsz, n, 2, GDN_HEADS).astype(jnp.float32) + dt_bias.astype(jnp.float32))
        beta = jax.nn.sigmoid(b.reshape(bsz, n, 2, GDN_HEADS).astype(jnp.float32))
        gdn = (gq, gk, gv, g, beta, z.reshape(bsz, n, GDN_HEADS, GDN_DV))
        diff = (dq.reshape(bsz, n, DIFF_HEADS, 2, DIFF_DH), dk.reshape(bsz, n, DIFF_HEADS, 2, DIFF_DH),
                dv.reshape(bsz, n, DIFF_HEADS, 2 * DIFF_DH))
        return gdn, diff

    (q, k, v, g, beta, z), (dq, dk, dv) = project(u)
    (qc, kc, vc, gc, betac, zc), (dqc, dkc, dvc) = project(uc)
    o, oc = gdn_bidirectional(q, k, v, g, beta, qc, kc, vc, gc, betac)
    lam = (jnp.exp(jnp.sum(lam_q1.astype(jnp.float32) * lam_k1.astype(jnp.float32)))
           - jnp.exp(jnp.sum(lam_q2.astype(jnp.float32) * lam_k2.astype(jnp.float32))) + lambda_init)
    d_o, d_oc = diff_attention(dq, dk, dv, dqc, dkc, dvc, lam, rows, cols, need_ctx)

    def merge(o_gdn, zz, o_diff):
        bsz, n = o_gdn.shape[:2]
        a_out = rmsnorm(o_gdn, gdn_g) * jax.nn.silu(zz)
        b_out = rmsnorm(o_diff, subln_g) * (1.0 - lambda_init)
        return jnp.concatenate([a_out.reshape(bsz, n, -1), b_out.reshape(bsz, n, -1)], axis=-1)

    y = merge(o, z, d_o)
    yc = merge(oc, zc, d_oc) if need_ctx else None
    return y, yc


def odd_mixer(u, uc, w_in, rpb, sink, rows, cols, need_ctx):
    def project(t):
        bsz, n = t.shape[:2]
        nq, nk, nv, wq, wk, wv = split_cols(t @ w_in, ODD_WIDTHS)
        hd = lambda a, nh: a.reshape(bsz, n, nh, HEAD_DIM)
        return ((hd(nq, NA_HEADS), hd(nk, NA_HEADS), hd(nv, NA_HEADS)),
                (hd(wq, WIN_HEADS), hd(wk, WIN_KV_HEADS), hd(wv, WIN_KV_HEADS)))

    (nq, nk, nv), (wq, wk, wv) = project(u)
    (nqc, nkc, nvc), (wqc, wkc, wvc) = project(uc)
    bsz, n = u.shape[:2]
    c_out = neighbourhood_attention(nq, nk, nv, nkc, nvc, rpb)
    d_out = window_sink_attention(wq, wk, wv, wkc, wvc, sink, rows, cols)
    y = jnp.concatenate([c_out.reshape(bsz, n, -1), d_out.reshape(bsz, n, -1)], axis=-1)
    yc = None
    if need_ctx:
        m = uc.shape[1]
        cc = context_attention(nqc, nkc, nvc)
        dc = context_sink_attention(wqc, wkc, wvc, sink)
        yc = jnp.concatenate([cc.reshape(bsz, m, -1), dc.reshape(bsz, m, -1)], axis=-1)
    return y, yc


def setup_inputs(seed: int = 0) -> dict:
    key = jax.random.key(seed)
    ks = list(jax.random.split(key, 28))
    n_even = (DEPTH + 1) // 2
    n_odd = DEPTH // 2
    f32 = jnp.float32
    d = D_MODEL

    def nrm(k, shape, std):
        return jax.random.normal(k, shape, f32) * std

    dt = jnp.exp(jax.random.uniform(ks[9], (n_even, 2, GDN_HEADS), f32, math.log(1e-3), math.log(1e-1)))
    dt_bias = dt + jnp.log(-jnp.expm1(-dt))
    return {
        'x': nrm(ks[0], (BATCH, SEQ, d), 1.0),
        'c': nrm(ks[1], (BATCH, d), 1.0),
        'ctx': nrm(ks[2], (BATCH, CTX_LEN, d), 1.0),
        'c_ctx': nrm(ks[3], (d,), 1.0),
        'ada_w': nrm(ks[4], (DEPTH, d, 6 * d), 0.5 * d ** -0.5),
        'ada_b': nrm(ks[5], (DEPTH, 6 * d), 0.02),
        'norm_mix_g': 1.0 + nrm(ks[6], (DEPTH, d), 0.02),
        'norm_ffn_g': 1.0 + nrm(ks[7], (DEPTH, d), 0.02),
        'w_in_even': nrm(ks[8], (n_even, d, sum(EVEN_WIDTHS)), d ** -0.5),
        'gdn_conv_w': nrm(ks[10], (n_even, GDN_CONV, GDN_QKV), GDN_CONV ** -0.5),
        'gdn_a_log': jnp.log(jax.random.uniform(ks[11], (n_even, 2, GDN_HEADS), f32, 1.0, 16.0)),
        'gdn_dt_bias': dt_bias,
        'gdn_norm_g': 1.0 + nrm(ks[12], (n_even, GDN_DV), 0.02),
        'diff_lambda_q1': nrm(ks[13], (n_even, DIFF_DH), 0.1),
        'diff_lambda_k1': nrm(ks[14], (n_even, DIFF_DH), 0.1),
        'diff_lambda_q2': nrm(ks[15], (n_even, DIFF_DH), 0.1),
        'diff_lambda_k2': nrm(ks[16], (n_even, DIFF_DH), 0.1),
        'diff_subln_g': 1.0 + nrm(ks[17], (n_even, 2 * DIFF_DH), 0.02),
        'w_in_odd': nrm(ks[18], (n_odd, d, sum(ODD_WIDTHS)), d ** -0.5),
        'na_rpb': nrm(ks[19], (n_odd, NA_HEADS, 2 * NA_ROWS - 1, 2 * NA_COLS - 1), 0.1),
        'win_sink': nrm(ks[20], (n_odd, WIN_HEADS), 0.5),
        'w_out': nrm(ks[21], (DEPTH, d, d), d ** -0.5),
        'ffn_w_gate': nrm(ks[22], (DEPTH, d, FFN_HIDDEN), d ** -0.5),
        'ffn_w_up': nrm(ks[23], (DEPTH, d, FFN_HIDDEN), d ** -0.5),
        'ffn_w_down': nrm(ks[24], (DEPTH, FFN_HIDDEN, d), FFN_HIDDEN ** -0.5),
        'final_norm_g': 1.0 + nrm(ks[25], (d,), 0.02),
    }


def reference(x, c, ctx, c_ctx, ada_w, ada_b, norm_mix_g, norm_ffn_g,
              w_in_even, gdn_conv_w, gdn_a_log, gdn_dt_bias, gdn_norm_g,
              diff_lambda_q1, diff_lambda_k1, diff_lambda_q2, diff_lambda_k2, diff_subln_g,
              w_in_odd, na_rpb, win_sink, w_out, ffn_w_gate, ffn_w_up, ffn_w_down, final_norm_g):
    n = x.shape[1]
    pos = jnp.arange(n)
    rows, cols = pos // GRID_W, pos % GRID_W
    h, hc = x, ctx
    c_act = jax.nn.silu(c)
    cc_act = jax.nn.silu(c_ctx)
    for l in range(DEPTH):
        need_ctx = l < DEPTH - 1
        mod = jnp.split((c_act @ ada_w[l] + ada_b[l])[:, None, :], 6, axis=-1)
        modc = jnp.split(cc_act @ ada_w[l] + ada_b[l], 6, axis=-1)
        u = rmsnorm(h, norm_mix_g[l]) * (1.0 + mod[1]) + mod[0]
        uc = rmsnorm(hc, norm_mix_g[l]) * (1.0 + modc[1]) + modc[0]
        i = l // 2
        if l % 2 == 0:
            y, yc = even_mixer(u, uc, w_in_even[i], gdn_conv_w[i], gdn_a_log[i], gdn_dt_bias[i], gdn_norm_g[i],
                               diff_lambda_q1[i], diff_lambda_k1[i], diff_lambda_q2[i], diff_lambda_k2[i],
                               diff_subln_g[i], 0.8 - 0.6 * math.exp(-0.3 * l), rows, cols, need_ctx)
        else:
            y, yc = odd_mixer(u, uc, w_in_odd[i], na_rpb[i], win_sink[i], rows, cols, need_ctx)
        h = h + mod[2] * (y @ w_out[l])
        u2 = rmsnorm(h, norm_ffn_g[l]) * (1.0 + mod[4]) + mod[3]
        h = h + mod[5] * swiglu(u2, ffn_w_gate[l], ffn_w_up[l], ffn_w_down[l])
        if need_ctx:
            hc = hc + modc[2] * (yc @ w_out[l])
            u2c = rmsnorm(hc, norm_ffn_g[l]) * (1.0 + modc[4]) + modc[3]
            hc = hc + modc[5] * swiglu(u2c, ffn_w_gate[l], ffn_w_up[l], ffn_w_down[l])
    return rmsnorm(h, final_norm_g)
```

```python
import math
from contextlib import ExitStack

import numpy as np

import concourse.bass as bass
import concourse.mybir as mybir
from concourse.bass_utils import run_bass_kernel_spmd

F32 = mybir.dt.float32
BF16 = mybir.dt.bfloat16
AF = mybir.ActivationFunctionType
ALU = mybir.AluOpType
AX = mybir.AxisListType

D = 2048
KC = 16
T = 2048
C = 256
NT = T + C
NTILE = NT // 128
FF = 5632
FC = FF // 128
DEPTH = 4
EPS = 1e-6
TT = [(0, 512), (512, 512), (1024, 512), (1536, 512), (2048, 256)]
W_EVEN = 7200
W_ODD = 4608

ENGS = ("pe", "act", "dve", "pool", "sp")


class _Op:
    __slots__ = ("eng", "emit", "deps", "is_dma", "sem", "semval", "need_inc")

    def __init__(self, eng, emit, is_dma):
        self.eng = eng
        self.emit = emit
        self.deps = []
        self.is_dma = is_dma
        self.sem = None
        self.semval = 0
        self.need_inc = False


class Prog:
    NDMASEM = 8

    def __init__(self, nc, same_eng_sync=True):
        self.nc = nc
        self.same_eng_sync = same_eng_sync
        self.stack = ExitStack()
        self.esem = {e: self.stack.enter_context(nc.semaphore("s_" + e)) for e in ENGS}
        self.ecount = {e: 0 for e in ENGS}
        self.dsem, self.dcount, self.drr, self.dlast = {}, {}, {}, {}
        for q in ("sp", "pool", "act"):
            self.dsem[q] = [self.stack.enter_context(nc.semaphore("d_%s%d" % (q, i))) for i in range(self.NDMASEM)]
            self.dcount[q] = [0] * self.NDMASEM
            self.drr[q] = 0
            self.dlast[q] = [None] * self.NDMASEM
        self.ops = {e: [] for e in ENGS}
        self.lastw = {}
        self.readers = {}
        self.waited = {e: {} for e in ENGS}
        self.n_inst = 0
        self._cur_barrier = None

    def op(self, eng, emit, reads=(), writes=(), dma=False):
        o = _Op(eng, emit, dma)
        deps = []
        psr = [k for k in reads if isinstance(k, tuple) and k and k[0] == "ps"]
        if psr:
            writes = list(writes) + [k for k in psr if k not in writes]
        for k in reads:
            w = self.lastw.get(k)
            if w is not None:
                deps.append(w)
        for k in writes:
            w = self.lastw.get(k)
            if w is not None:
                deps.append(w)
            deps.extend(self.readers.get(k, ()))
        if dma:
            q = eng
            i = self.drr[q]
            self.drr[q] = (i + 1) % self.NDMASEM
            prev = self.dlast[q][i]
            if prev is not None:
                deps.append(prev)
            self.dcount[q][i] += 16
            o.sem = self.dsem[q][i]
            o.semval = self.dcount[q][i]
            o.need_inc = True
            self.dlast[q][i] = o
        seen = set()
        for d in deps:
            if d is o or id(d) in seen:
                continue
            seen.add(id(d))
            if (not d.is_dma) and d.eng == eng and (eng == "pe" or not self.same_eng_sync):
                continue
            o.deps.append(d)
            d.need_inc = True
        for k in writes:
            self.lastw[k] = o
            self.readers[k] = []
        for k in reads:
            self.readers.setdefault(k, []).append(o)
        self.ops[eng].append(o)
        return o

    def stage_end(self):
        lasts = []
        for e in ENGS:
            if self.ops[e]:
                lasts.append(self.ops[e][-1])
        for q in self.dlast:
            for d in self.dlast[q]:
                if d is not None:
                    lasts.append(d)
        for o in lasts:
            o.need_inc = True
        self.flush()
        cur = self._cur_barrier or {}
        self._cur_barrier = {e: list(lasts) + list(cur.get(e) or []) for e in ENGS}
        self.lastw.clear()
        self.readers.clear()

    def flush(self):
        nc = self.nc
        for e in ENGS:
            for o in self.ops[e]:
                if o.is_dma:
                    continue
                if o.need_inc and o.sem is None:
                    self.ecount[e] += 1
                    o.sem = self.esem[e]
                    o.semval = self.ecount[e]
        pend = self._cur_barrier
        engmap = {"pe": "tensor", "act": "scalar", "dve": "vector", "pool": "gpsimd", "sp": "sync"}
        if not any(self.ops[e] for e in ENGS):
            return
        with nc.Block() as block:
            for e in ENGS:
                ops = self.ops[e]
                if not ops:
                    continue

                def body(engine, ops=ops, e=e):
                    waited = self.waited[e]
                    first = True
                    for o in ops:
                        deps = o.deps
                        if first and pend and pend.get(e):
                            deps = list(deps) + [d for d in pend[e] if d is not o]
                            pend[e] = None
                        first = False
                        need = {}
                        for d in deps:
                            sid = id(d.sem)
                            if waited.get(sid, 0) >= d.semval:
                                continue
                            if sid not in need or need[sid][1] < d.semval:
                                need[sid] = (d.sem, d.semval)
                        for sid, (s, v) in need.items():
                            engine.wait_ge(s, v)
                            waited[sid] = v
                            self.n_inst += 1
                        ins = o.emit(engine)
                        self.n_inst += 1
                        if o.need_inc:
                            ins.then_inc(o.sem, 16 if o.is_dma else 1)

                getattr(block, engmap[e])(body)
        for e in ENGS:
            self.ops[e] = []

    def final_wait(self, eng="sp"):
        self.stage_end()
        self.op(eng, lambda en: en.nop())
        self.flush()

    def close(self):
        self.stack.close()


def _rope_tables(kind):
    pos = np.arange(T)
    rows, cols = pos // 64, pos % 64
    cos = np.zeros((128, T), np.float32)
    sin = np.zeros((128, T), np.float32)
    perm = np.zeros((128, 128), np.float32)
    if kind == "d":
        blocks = [(0, 32, rows), (32, 32, cols), (64, 32, rows), (96, 32, cols)]
    else:
        blocks = [(0, 64, rows), (64, 64, cols)]
    for base, n, p in blocks:
        half = n // 2
        inv = (10000.0 ** (-np.arange(0, n, 2, dtype=np.float32) / n)).astype(np.float32)
        ang = p.astype(np.float32)[None, :] * inv[:, None]
        c, s = np.cos(ang).astype(np.float32), np.sin(ang).astype(np.float32)
        cos[base:base + half] = c
        cos[base + half:base + n] = c
        sin[base:base + half] = -s
        sin[base + half:base + n] = s
        for i in range(half):
            perm[base + half + i, base + i] = 1.0
            perm[base + i, base + half + i] = 1.0
    return cos, sin, perm


def _na_geometry():
    masks, index, plan = [], {}, {}
    for a in range(16):
        plan[a] = []
        qrow = np.repeat(np.array([2 * a, 2 * a + 1]), 64)
        qcol = np.tile(np.arange(64), 2)
        start = np.clip(qrow - 4, 0, 24)
        winc = np.clip(qcol - 8, 0, 48)
        for t in range(16):
            krow = np.repeat(np.array([2 * t, 2 * t + 1]), 64)
            kcol = np.tile(np.arange(64), 2)
            ok = ((krow[:, None] >= start[None, :]) & (krow[:, None] < start[None, :] + 8)
                  & (kcol[:, None] >= winc[None, :]) & (kcol[:, None] < winc[None, :] + 16))
            if not ok.any():
                continue
            key = (t - a, ok.tobytes())
            if key not in index:
                index[key] = len(masks)
                masks.append((t - a, ok.astype(np.float32)))
            plan[a].append((t, index[key]))
    return masks, plan


_NA_MASKS, _NA_PLAN = _na_geometry()
N_NA_MASK = len(_NA_MASKS)


def _win_masks():
    m = np.zeros((6, 128, 512), np.float32)
    for r in range(6):
        j = (r - 1) * 128 + np.arange(128)[:, None]
        i = np.arange(512)[None, :]
        m[r] = (np.abs(i - j) <= 128).astype(np.float32)
    return m


def _consts():
    c = {}
    c["ident"] = np.eye(128, dtype=np.float32)
    m = np.arange(128)[:, None]
    i = np.arange(128)[None, :]
    c["tri"] = np.stack([(m <= i), (m >= i), (m > i), (m < i)]).astype(np.float32)
    cd, sd, pd = _rope_tables("d")
    cw, sw, pw = _rope_tables("w")
    c["rope_d"] = np.stack([cd, sd])
    c["rope_w"] = np.stack([cw, sw])
    c["perm"] = np.stack([pd, pw])
    c["na_mask"] = np.stack([mk for _, mk in _NA_MASKS])
    c["win_mask"] = _win_masks()
    return c


class Builder:
    def __init__(self, nc, debug=False, layers=DEPTH, ndt=F32):
        self.nc = nc
        self.debug = debug
        self.layers = layers
        self.ndt = ndt
        self.P = Prog(nc)
        self.inp = {}
        self.scr = {}
        self.uid = 0

    def din(self, name, shape, dtype=F32):
        t = self.nc.dram_tensor(name, list(shape), dtype, kind="ExternalInput").ap()
        self.inp[name] = t
        return t

    def dscr(self, name, shape, dtype):
        kind = "ExternalOutput" if self.debug else "Internal"
        t = self.nc.dram_tensor(name, list(shape), dtype, kind=kind).ap()
        self.scr[name] = t
        return t

    def sb(self, es, name, shape, dtype):
        self.uid += 1
        return es.enter_context(self.nc.sbuf_tensor("%s_%d" % (name, self.uid), list(shape), dtype))

    def mm(self, out, lhsT, rhs, start, stop, reads, writes):
        self.P.op("pe", lambda e: e.matmul(out, lhsT=lhsT, rhs=rhs, start=start, stop=stop), reads, writes)

    def act(self, out, in_, func, reads, writes, bias=None, scale=None, accum_out=None):
        kw = {}
        if bias is not None:
            kw["bias"] = bias
        if scale is not None:
            kw["scale"] = scale
        if accum_out is not None:
            kw["accum_out"] = accum_out
        self.P.op("act", lambda e: e.activation(out=out, in_=in_, func=func, **kw), reads, writes)

    def tt(self, eng, out, in0, in1, op, reads, writes):
        self.P.op(eng, lambda e: e.tensor_tensor(out=out, in0=in0, in1=in1, op=op), reads, writes)

    def ts(self, eng, out, in0, s1, s2, op0, op1, reads, writes):
        if op1 is None:
            self.P.op(eng, lambda e: e.tensor_scalar(out=out, in0=in0, scalar1=s1, scalar2=None, op0=op0), reads, writes)
        else:
            self.P.op(eng, lambda e: e.tensor_scalar(out=out, in0=in0, scalar1=s1, scalar2=s2, op0=op0, op1=op1), reads, writes)

    def stt(self, out, in0, scalar, in1, op0, op1, reads, writes):
        self.P.op("dve", lambda e: e.scalar_tensor_tensor(out=out, in0=in0, scalar=scalar, in1=in1, op0=op0, op1=op1), reads, writes)

    def cp(self, eng, out, in_, reads, writes):
        if eng == "act":
            self.P.op("act", lambda e: e.copy(out=out, in_=in_), reads, writes)
        else:
            self.P.op(eng, lambda e: e.tensor_copy(out=out, in_=in_), reads, writes)

    def recip(self, out, in_, reads, writes):
        self.P.op("dve", lambda e: e.reciprocal(out=out, in_=in_), reads, writes)

    def dma(self, q, out, in_, reads, writes, **kw):
        self.P.op(q, lambda e: e.dma_start(out=out, in_=in_, **kw), reads, writes, dma=True)

    def rsqrt(self, out, in_, mult, tmp, tmpkey, reads, writes):
        self.act(tmp, in_, AF.Sqrt, reads, [tmpkey], bias=self.eps_col[:, 0:1], scale=mult)
        self.recip(out, tmp, [tmpkey], writes)

    def declare(self):
        di = self.din
        self.x = di("x", [T, D])
        self.ctx = di("ctx", [C, D])
        self.c_fm = di("c_fm", [128, KC, 2])
        self.ada_w = di("ada_w", [DEPTH, D, 6 * D])
        self.ada_b = di("ada_b_fm", [DEPTH, 128, 96])
        self.nmg = di("nmg_fm", [DEPTH, 128, KC])
        self.nfg = di("nfg_fm", [DEPTH, 128, KC])
        self.fng = di("fng_fm", [128, KC])
        self.w_in_even = di("w_in_even", [2, D, W_EVEN])
        self.w_in_odd = di("w_in_odd", [2, D, W_ODD])
        self.w_out = di("w_out", [DEPTH, D, D])
        self.w_gate = di("ffn_w_gate", [DEPTH, D, FF])
        self.w_up = di("ffn_w_up", [DEPTH, D, FF])
        self.w_down = di("ffn_w_down", [DEPTH, FF, D])
        self.conv_fm = di("conv_fm", [2, 128, 24, 3])
        self.gdn_vec = di("gdn_vec", [2, 128, 32])
        self.gng = di("gng_rep", [2, 128, 128])
        self.lam = di("lam_rep", [2, 128, 4, 64])
        self.subln = di("subln_fm", [2, 128, 1])
        self.na_bias = di("na_bias", [2, 8, 128, N_NA_MASK, 128])
        self.sink = di("sink_rep", [2, 128, 8])
        self.c_ident = di("c_ident", [128, 128])
        self.c_tri = di("c_tri", [4, 128, 128])
        self.c_rope_d = di("c_rope_d", [2, 128, T])
        self.c_rope_w = di("c_rope_w", [2, 128, T])
        self.c_perm = di("c_perm", [2, 128, 128])
        self.c_win = di("c_win_mask", [6, 128, 512])
        self.out = self.nc.dram_tensor("out", [T, D], F32, kind="ExternalOutput").ap()
        ds = self.dscr
        self.hT = ds("hT", [KC, 128, NT], F32)
        self.yT = ds("yT", [KC, 128, NT], BF16)
        self.qT_a = ds("qT_a", [8, 128, NT], BF16)
        self.kT_a = ds("kT_a", [8, 128, NT], BF16)
        self.v_a = ds("v_a", [NT, 1024], BF16)
        self.k_tm = ds("k_tm", [NT, 1024], BF16)
        self.qT_b = ds("qT_b", [8, 128, NT], BF16)
        self.kT_b = ds("kT_b", [8, 128, NT], BF16)
        self.v_b = ds("v_b", [NT, 1024], BF16)
        self.gz = ds("gz", [NT, 1024], BF16)
        self.gab = ds("gab", [NT, 32], F32)
        self.go = ds("go", [2, NT, 1024], F32)

    def setup(self, es):
        nc = self.nc
        sb = lambda n, s, d: self.sb(es, n, s, d)
        self.ps = es.enter_context(nc.psum_tensor("ps", [128, 8, 512], F32))
        self.ident_f = sb("ident_f", [128, 128], F32)
        self.ident_b = sb("ident_b", [128, 128], BF16)
        self.ones_b = sb("ones_b", [128, 128], BF16)
        self.ones_f = sb("ones_f", [128, 128], F32)
        self.eps_col = sb("eps_col", [128, 1], F32)
        self.cact = sb("cact", [128, KC, 2], BF16)
        self.modv = sb("modv", [128, 96, 2], F32)
        self.G1 = sb("G1", [128, KC, 2], F32)
        self.G2 = sb("G2", [128, KC, 2], F32)
        self.nrm2 = sb("nrm2", [128, 64], F32)
        self.mbias = sb("mbias", [128, 64], F32)
        P = self.P
        with ExitStack() as e1:
            craw = self.sb(e1, "craw", [128, KC, 2], F32)
            self.dma("sp", self.ident_f[:], self.c_ident, [], ["ident_f"])
            self.dma("sp", craw[:], self.c_fm, [], ["craw"])
            self.cp("dve", self.ident_b[:], self.ident_f[:], ["ident_f"], ["ident_b"])
            P.op("pool", lambda e: e.memset(self.ones_b[:], 1.0), [], ["ones_b"])
            P.op("pool", lambda e: e.memset(self.ones_f[:], 1.0), [], ["ones_f"])
            P.op("pool", lambda e: e.memset(self.eps_col[:], EPS), [], ["eps"])
            self.act(self.cact[:], craw[:], AF.Silu, ["craw"], ["cact"])
            P.stage_end()

    def shift(self, which):
        return 0 if which == 0 else 48

    def stage_load_inputs(self):
        P = self.P
        with ExitStack() as es:
            xin = [self.sb(es, "xin", [128, D], F32) for _ in range(2)]
            hst = [self.sb(es, "hst", [128, KC, 128], F32) for _ in range(2)]
            hv = self.hT.rearrange("k p t -> p k t")
            for m in range(NTILE):
                src = self.x[m * 128:(m + 1) * 128, :] if m < 16 else self.ctx[(m - 16) * 128:(m - 15) * 128, :]
                r = m % 2
                self.dma("sp", xin[r][:], src, [], [("xin", r)])
                for g in range(4):
                    b = (m * 4 + g) % 8
                    for j in range(4):
                        k = g * 4 + j
                        self.mm(self.ps[:, b, j * 128:(j + 1) * 128], xin[r][:, k * 128:(k + 1) * 128], self.ident_f[:],
                                True, True, [("xin", r)], [("ps", b)])
                    dst = hst[r][:, g * 4:(g + 1) * 4, :]
                    srcp = self.ps[:, b, :].rearrange("p (a b) -> p a b", a=4)
                    self.cp("act" if g % 2 else "dve", dst, srcp, [("ps", b)], [("hst", r, g)])
                self.dma("pool", hv[:, :, m * 128:(m + 1) * 128], hst[r][:], [("hst", r, g) for g in range(4)], [("hT", m)])
            P.stage_end()

    def stage_mod(self, l):
        P = self.P
        with ExitStack() as es:
            wb = [self.sb(es, "wb", [128, KC, 512], BF16) for _ in range(2)]
            adab = self.sb(es, "adab", [128, 96], F32)
            nmg = self.sb(es, "nmg", [128, KC], F32)
            nfg = self.sb(es, "nfg", [128, KC], F32)
            self.dma("sp", adab[:], self.ada_b[l], [], ["adab"])
            self.dma("sp", nmg[:], self.nmg[l], [], ["nmg"])
            self.dma("sp", nfg[:], self.nfg[l], [], ["nfg"])
            wv = self.ada_w[l].rearrange("(k p) n -> p k n", p=128)

            def load(cg):
                self.dma("pool", wb[cg % 2][:], wv[:, :, cg * 512:(cg + 1) * 512], [], [("wb", cg % 2)])
            load(0)
            for cg in range(24):
                if cg + 1 < 24:
                    load(cg + 1)
                for j in range(4):
                    ch = cg * 4 + j
                    for k in range(KC):
                        self.mm(self.ps[:, 0, ch * 2:ch * 2 + 2], wb[cg % 2][:, k, j * 128:(j + 1) * 128], self.cact[:, k, :],
                                k == 0, k == KC - 1, [("wb", cg % 2), "cact"], [("ps", 0)])
            self.tt("dve", self.modv[:], self.ps[:, 0, 0:192].rearrange("p (a b) -> p a b", b=2),
                    adab[:].unsqueeze(2).to_broadcast([128, 96, 2]), ALU.add, [("ps", 0), "adab"], ["modv"])
            self.stt(self.G1[:], self.modv[:, 16:32, :], 1.0, nmg[:].unsqueeze(2).to_broadcast([128, KC, 2]), ALU.add, ALU.mult,
                     ["modv", "nmg"], ["G1"])
            self.stt(self.G2[:], self.modv[:, 64:80, :], 1.0, nfg[:].unsqueeze(2).to_broadcast([128, KC, 2]), ALU.add, ALU.mult,
                     ["modv", "nfg"], ["G2"])
            if self.debug:
                dm = self.dscr("dbg_mod%d" % l, [128, 192], F32)
                self.dma("sp", dm, self.modv[:].rearrange("p a b -> p (a b)"), ["modv"], ["dbgm"])
            P.stage_end()

    def norm_tiles(self, es, tiles, which, uT, base, tag):
        G = self.G1 if which == 0 else self.G2
        sh = 0 if which == 0 else 48
        hv = self.hT.rearrange("k p t -> p k t")
        hb = [self.sb(es, "hb", [128, KC, 256], F32) for _ in range(2)]
        sq = [self.sb(es, "sq", [128, KC, 256], BF16) for _ in range(2)]
        rs = [self.sb(es, "rs", [128, 256], F32) for _ in range(2)]
        rt = self.sb(es, "rt", [128, 256], F32)
        sub = []
        for (t0, n) in tiles:
            for o in range(0, n, 256):
                sub.append((t0 + o, min(256, n - o)))
        pend = None
        for i, (t0, n) in enumerate(sub):
            r = i % 2
            s = 1 if t0 >= T else 0
            b = 7
            self.dma("sp", hb[r][:, :, 0:n], hv[:, :, t0:t0 + n], [("hT", "all")], [("hb", r)])
            self.tt("pool", sq[r][:, :, 0:n], hb[r][:, :, 0:n], hb[r][:, :, 0:n], ALU.mult, [("hb", r)], [("sq", r)])
            for k in range(KC):
                self.mm(self.ps[:, b, 0:n], self.ones_b[:], sq[r][:, k, 0:n], k == 0, k == KC - 1, [("sq", r), "ones_b"], [("ps", b)])
            self.rsqrt(rs[r][:, 0:n], self.ps[:, b, 0:n], 1.0 / D, rt[:, 0:n], "rt", [("ps", b), "eps"], [("rs", r)])
            self.tt("dve", hb[r][:, :, 0:n], hb[r][:, :, 0:n], rs[r][:, 0:n].unsqueeze(1).to_broadcast([128, KC, n]), ALU.mult,
                    [("hb", r), ("rs", r)], [("hb", r)])

            def second(r=r, s=s, t0=t0, n=n):
                for k in range(KC):
                    self.act(uT[:, k, t0 - base:t0 - base + n], hb[r][:, k, 0:n], AF.Identity, [("hb", r), "G1", "G2", "modv"],
                             [(tag, k, t0)], scale=G[:, k, s:s + 1], bias=self.modv[:, sh + k, s:s + 1])
            if pend is not None:
                pend()
            pend = second
        if pend is not None:
            pend()

    def resid_update(self, hrow, c, t0, n, ps_ap, gate_chunk_base, rkeys, wkey):
        s = 1 if t0 >= T else 0
        self.stt(hrow[:, t0:t0 + n], ps_ap, self.modv[:, gate_chunk_base + c, s:s + 1], hrow[:, t0:t0 + n], ALU.mult, ALU.add,
                 rkeys + ["modv"], [wkey])

    def stats_row(self, orow, okeys, slot0, split, sqrow, mx, sn="sqrow", presq=False):
        if not presq:
            self.act(sqrow[:], orow[:], AF.Square, okeys, [sn])
        parts = [(0, 64), (64, 128)] if split else [(0, 128)]
        for pi, (p0, p1) in enumerate(parts):
            for ti, (t0, n) in enumerate(TT):
                b = 4 + (self.auxr % 3)
                self.auxr += 1
                self.mm(self.ps[:, b, 0:n], self.ones_b[p0:p1, :], sqrow[p0:p1, t0:t0 + n], True, True, [sn, "ones_b"], [("ps", b)])
                self.P.op("dve", lambda e, b=b, n=n, ti=ti: e.tensor_reduce(out=mx[:, ti:ti + 1], in_=self.ps[:, b, 0:n], axis=AX.X, op=ALU.max),
                          [("ps", b)], [("mx", ti)])
            sl = slot0 + pi
            self.P.op("dve", lambda e, sl=sl: e.tensor_reduce(out=self.nrm2[:, sl:sl + 1], in_=mx[:, 0:5], axis=AX.X, op=ALU.max),
                      [("mx", ti) for ti in range(5)], [("nrm2", sl)])

    def stage_inproj(self, l):
        P = self.P
        even = (l % 2 == 0)
        i2 = l // 2
        w = (self.w_in_even if even else self.w_in_odd)[i2]
        wv = w.rearrange("(k p) n -> p k n", p=128)
        if even:
            jobs = [("fm", "gq", 0, 1024), ("fm", "gk", 1024, 1024), ("fm", "gv", 2048, 1024), ("tm", "z", 3072, 1024),
                    ("tm", "ab", 4096, 32), ("fm", "dq", 4128, 1024), ("fm", "dk", 5152, 1024), ("tm", "dv", 6176, 1024)]
        else:
            jobs = [("fm", "nq", 0, 1024), ("fm", "nk", 1024, 1024), ("tm", "nv", 2048, 1024), ("fm", "wq", 3072, 1024),
                    ("fm", "wk", 4096, 256), ("tm", "wv", 4352, 256)]
        groups = []
        for mode, kind, c0, nc_ in jobs:
            for g0 in range(0, nc_, 512):
                groups.append((mode, kind, c0, g0, min(512, nc_ - g0)))
        with ExitStack() as eo:
            uT = self.sb(eo, "uT", [128, KC, NT], BF16)
            with ExitStack() as es:
                self.norm_tiles(es, TT, 0, uT, 0, "uT")
                P.stage_end()
                if self.debug:
                    du = self.dscr("dbg_uT%d" % l, [128, KC, NT], BF16)
                    self.dma("sp", du, uT[:], [], ["dbgu"])
                    P.stage_end()
            with ExitStack() as es:
                sb = lambda n, s, d: self.sb(es, n, s, d)
                wb = [sb("wb", [128, KC, 512], BF16) for _ in range(2)]
                xrows = [sb("xrow", [128, NT], F32) for _ in range(2)]
                yas = [sb("ya", [128, NT], F32) for _ in range(2)]
                orow0s = [sb("orow0", [128, NT], BF16) for _ in range(2)]
                orow = [sb("orow", [128, NT], BF16) for _ in range(2)]
                sqrows = [sb("sqrow", [128, NT], BF16) for _ in range(2)]
                rsr = sb("rsr", [128, 512], F32)
                rst = sb("rst", [128, 512], F32)
                t1 = sb("t1", [128, 512], F32)
                t2 = sb("t2", [128, 512], F32)
                mx = sb("mx", [128, 8], F32)
                tms = [sb("tms", [128, NTILE, 128], BF16) for _ in range(1)]
                gst = [sb("gst", [128, 512], BF16) for _ in range(3)]
                abst = sb("abst", [128, NTILE, 32], F32)
                rope = sb("rope", [128, 2, T], F32)
                permb = sb("permb", [128, 128], BF16)
                permf = sb("permf", [128, 128], F32)
                cw = sb("cw", [128, 24, 3], F32)
                self.auxr = 0
                self.dma("sp", rope[:], (self.c_rope_d if even else self.c_rope_w).rearrange("a p t -> p a t"), [], ["rope"])
                self.dma("sp", permf[:], self.c_perm[0 if even else 1], [], ["permf"])
                self.cp("dve", permb[:], permf[:], ["permf"], ["permb"])
                if even:
                    self.dma("sp", cw[:], self.conv_fm[i2], [], ["cw"])

                def load(gi):
                    mode, kind, c0, g0, gw = groups[gi]
                    self.dma("pool", wb[gi % 2][:, :, 0:gw], wv[:, :, c0 + g0:c0 + g0 + gw], [], [("wb", gi % 2)])

                load(0)
                mainr = 0
                orr = 0
                self.tmr = 0
                gsr = 0
                self.pipe = []
                self.pipe_depth = 1
                for gi, (mode, kind, c0, g0, gw) in enumerate(groups):
                    if gi + 1 < len(groups):
                        load(gi + 1)
                    wbg = wb[gi % 2]
                    wkey = ("wb", gi % 2)
                    if mode == "tm":
                        if kind == "ab":
                            for m in range(NTILE):
                                b = mainr % 4
                                mainr += 1
                                for k in range(KC):
                                    self.mm(self.ps[:, b, 0:gw], uT[:, k, m * 128:(m + 1) * 128], wbg[:, k, 0:gw], k == 0, k == KC - 1,
                                            [wkey], [("ps", b)])
                                self.cp("act", abst[:, m, :], self.ps[:, b, 0:gw], [("ps", b)], [("abst", m)])
                            self.dma("pool", self.gab.rearrange("(t p) c -> p t c", p=128), abst[:], [("abst", m) for m in range(NTILE)], ["gab"])
                            continue
                        dst = {"z": self.gz, "dv": self.v_b, "nv": self.v_a, "wv": self.v_b}[kind]
                        for m in range(NTILE):
                            b = mainr % 4
                            mainr += 1
                            for k in range(KC):
                                self.mm(self.ps[:, b, 0:gw], uT[:, k, m * 128:(m + 1) * 128], wbg[:, k, 0:gw], k == 0, k == KC - 1,
                                        [wkey], [("ps", b)])
                            r = gsr % 3
                            gsr += 1
                            self.cp("act" if m % 2 else "dve", gst[r][:, 0:gw], self.ps[:, b, 0:gw], [("ps", b)], [("gst", r)])
                            self.dma("pool", dst[m * 128:(m + 1) * 128, g0:g0 + gw], gst[r][:, 0:gw], [("gst", r)], [(kind, m, g0)])
                        continue
                    for j in range(gw // 128):
                        ch = (g0 // 128) + j
                        o_r = orr % 2
                        orr += 1
                        ob = orow[o_r]
                        okey = ("orow", o_r)
                        xrow, ya, sqrow, ob0 = xrows[o_r], yas[o_r], sqrows[o_r], orow0s[o_r]
                        xn, yn, sn, o0n = ("xrow", o_r), ("ya", o_r), ("sqrow", o_r), ("orow0", o_r)
                        plain = kind in ("nq", "nk")
                        roped = kind in ("dq", "dk", "wq", "wk")
                        gdn = kind in ("gq", "gk", "gv")
                        for ti, (t0, n) in enumerate(TT):
                            b = mainr % 4
                            mainr += 1
                            for k in range(KC):
                                self.mm(self.ps[:, b, 0:n], wbg[:, k, j * 128:(j + 1) * 128], uT[:, k, t0:t0 + n], k == 0, k == KC - 1,
                                        [wkey], [("ps", b)])
                            if plain or (roped and t0 >= T):
                                self.cp("act", ob[:, t0:t0 + n], self.ps[:, b, 0:n], [("ps", b)], [(okey, ti)])
                            elif roped:
                                self.cp("act", ob0[:, t0:t0 + n], self.ps[:, b, 0:n], [("ps", b)], [(o0n, ti)])
                            else:
                                self.cp("act", xrow[:, t0:t0 + n], self.ps[:, b, 0:n], [("ps", b)], [(xn, ti)])

                        okeys_now = [(okey, ti) for ti in range(5)]
                        sqst = sqrows[o_r]
                        if gdn:
                            cch = {"gq": 0, "gk": 8, "gv": 16}[kind] + ch
                            xk = [(xn, ti) for ti in range(5)]
                            self.ts("dve", ya[:], xrow[:], cw[:, cch, 1:2], None, ALU.mult, None, xk + ["cw"], [yn])
                            for (s0, s1) in ((0, T), (T, NT)):
                                self.stt(ya[:, s0 + 1:s1], xrow[:, s0:s1 - 1], cw[:, cch, 0:1], ya[:, s0 + 1:s1], ALU.mult, ALU.add,
                                         xk + ["cw", yn], [yn])
                                self.stt(ya[:, s0:s1 - 1], xrow[:, s0 + 1:s1], cw[:, cch, 2:3], ya[:, s0:s1 - 1], ALU.mult, ALU.add,
                                         xk + ["cw", yn], [yn])
                            if kind == "gv":
                                self.act(ob[:], ya[:], AF.Silu, [yn], okeys_now)
                            else:
                                self.act(ya[:], ya[:], AF.Silu, [yn], [yn])
                                self.act(sqrow[:], ya[:], AF.Square, [yn], [sn])
                        elif not roped:
                            self.act(sqrow[:], ob[:], AF.Square, okeys_now, [sn])

                        def epi(kind=kind, ch=ch, ob=ob, okey=okey, xrow=xrow, ya=ya, sqrow=sqrow, ob0=ob0, xn=xn, yn=yn, sn=sn, o0n=o0n,
                                roped=roped, gdn=gdn):
                            okeys = [(okey, ti) for ti in range(5)]
                            if roped:
                                for ti, (t0, n) in enumerate(TT[:4]):
                                    ba = 4 + (self.auxr % 3)
                                    self.auxr += 1
                                    self.mm(self.ps[:, ba, 0:n], permb[:], ob0[:, t0:t0 + n], True, True, [(o0n, ti), "permb"], [("ps", ba)])
                                    self.tt("dve", t1[:, 0:n], ob0[:, t0:t0 + n], rope[:, 0, t0:t0 + n], ALU.mult, [(o0n, ti), "rope"], ["t1"])
                                    self.tt("dve", t2[:, 0:n], self.ps[:, ba, 0:n], rope[:, 1, t0:t0 + n], ALU.mult, [("ps", ba), "rope"], ["t2"])
                                    self.tt("dve", ob[:, t0:t0 + n], t1[:, 0:n], t2[:, 0:n], ALU.add, ["t1", "t2"], [(okey, ti)])
                                self.act(sqrow[:], ob[:], AF.Square, okeys, [sn])
                            if kind in ("gq", "gk"):
                                for ti, (t0, n) in enumerate(TT):
                                    ba = 4 + (self.auxr % 3)
                                    self.auxr += 1
                                    self.mm(self.ps[:, ba, 0:n], self.ones_b[:], sqrow[:, t0:t0 + n], True, True, [sn, "ones_b"], [("ps", ba)])
                                    self.rsqrt(rsr[:, 0:n], self.ps[:, ba, 0:n], 1.0, rst[:, 0:n], "rst", [("ps", ba), "eps"], ["rsr"])
                                    sc = (128.0 ** -0.5) if kind == "gq" else 1.0
                                    self.stt(ob[:, t0:t0 + n], ya[:, t0:t0 + n], sc, rsr[:, 0:n], ALU.mult, ALU.mult, [yn, "rsr"], [(okey, ti)])
                            if kind in ("dq", "dk"):
                                self.stats_row(ob, okeys, (0 if kind == "dq" else 16) + 2 * ch, True, sqrow, mx, sn, presq=True)
                            elif kind in ("nq", "nk"):
                                self.stats_row(ob, okeys, (0 if kind == "nq" else 8) + ch, False, sqrow, mx, sn, presq=True)
                            elif kind in ("wq", "wk"):
                                self.stats_row(ob, okeys, (16 if kind == "wq" else 24) + ch, False, sqrow, mx, sn, presq=True)
                            fm_dst = {"gq": self.qT_a, "gk": self.kT_a, "dq": self.qT_b, "dk": self.kT_b, "nq": self.qT_a, "nk": self.kT_a,
                                      "wq": self.qT_b, "wk": self.kT_b}.get(kind)
                            if fm_dst is not None:
                                self.dma("pool", fm_dst[ch], ob[:], okeys, [(kind, "fm", ch)])
                            if kind in ("gk", "gv"):
                                tr = 0
                                for m4 in range(0, NTILE, 4):
                                    ba = 4 + (self.auxr % 3)
                                    self.auxr += 1
                                    nm = min(4, NTILE - m4)
                                    for mm_ in range(nm):
                                        m = m4 + mm_
                                        self.mm(self.ps[:, ba, mm_ * 128:(mm_ + 1) * 128], ob[:, m * 128:(m + 1) * 128], self.ident_b[:], True, True,
                                                okeys + ["ident_b"], [("ps", ba)])
                                    self.cp("act" if (m4 // 4) % 2 else "dve", tms[tr][:, m4:m4 + nm, :],
                                            self.ps[:, ba, 0:nm * 128].rearrange("p (a b) -> p a b", b=128), [("ps", ba)], [("tms", tr, m4)])
                                tdst = self.k_tm if kind == "gk" else self.v_a
                                self.dma("pool", tdst[:, ch * 128:(ch + 1) * 128].rearrange("(t p) d -> p t d", p=128), tms[tr][:],
                                         [("tms", tr, m4) for m4 in range(0, NTILE, 4)], [(kind, "tm", ch)])
                        self.pipe_push(epi)
                self.pipe_flush()
                P.stage_end()

    def stage_outproj(self, l, tiles):
        P = self.P
        wv = self.w_out[l].rearrange("(k p) n -> p k n", p=128)
        hv = self.hT
        with ExitStack() as es:
            sb = lambda n, s, d: self.sb(es, n, s, d)
            yT = sb("yTs", [128, KC, NT], BF16)
            wb = [sb("wb", [128, KC, 512], BF16) for _ in range(2)]
            hrow = [sb("hrow", [128, NT], F32) for _ in range(3)]
            for k in range(KC):
                self.dma("sp", yT[:, k, :], self.yT[k], [], [("yT", k)])

            def load(g):
                self.dma("pool", wb[g % 2][:], wv[:, :, g * 512:(g + 1) * 512], [], [("wb", g % 2)])
            load(0)
            tmax = max(t0 + n for t0, n in tiles)
            mainr = 0
            for g in range(4):
                if g + 1 < 4:
                    load(g + 1)
                for j in range(4):
                    c = g * 4 + j
                    hr = c % 3
                    self.dma("sp", hrow[hr][:, 0:tmax], hv[c][:, 0:tmax], [], [("hrow", hr)])
                    for (t0, n) in tiles:
                        b = mainr % 8
                        mainr += 1
                        for k in range(KC):
                            self.mm(self.ps[:, b, 0:n], wb[g % 2][:, k, j * 128:(j + 1) * 128], yT[:, k, t0:t0 + n], k == 0, k == KC - 1,
                                    [("wb", g % 2), ("yT", k)], [("ps", b)])
                        self.resid_update(hrow[hr], c, t0, n, self.ps[:, b, 0:n], 32, [("ps", b), ("hrow", hr)], ("hrow", hr))
                    self.dma("pool", hv[c][:, 0:tmax], hrow[hr][:, 0:tmax], [("hrow", hr)], [("hTc", c)])
            P.stage_end()

    def stage_ffn(self, l, tiles):
        P = self.P
        wg = self.w_gate[l].rearrange("(k p) n -> p k n", p=128)
        wu = self.w_up[l].rearrange("(k p) n -> p k n", p=128)
        wd = self.w_down[l].rearrange("(f p) n -> p f n", p=128)
        hv = self.hT
        halves = [tiles[:2], tiles[2:]]
        for half in halves:
            if not half:
                continue
            base = half[0][0]
            ntok = sum(n for _, n in half)
            with ExitStack() as eo:
                actT = self.sb(eo, "actT", [128, FC, ntok], BF16)
                with ExitStack() as es:
                    sb = lambda n, s, d: self.sb(es, n, s, d)
                    u2 = sb("u2", [128, KC, ntok], BF16)
                    with ExitStack() as en:
                        self.norm_tiles(en, half, 1, u2, base, "u2")
                        P.stage_end()
                    wgb = [sb("wgb", [128, KC, 256], BF16) for _ in range(2)]
                    wub = [sb("wub", [128, KC, 256], BF16) for _ in range(2)]
                    sg = sb("sg", [128, 512], F32)

                    def load(g):
                        self.dma("pool", wgb[g % 2][:], wg[:, :, g * 256:(g + 1) * 256], [], [("wgb", g % 2)])
                        self.dma("pool", wub[g % 2][:], wu[:, :, g * 256:(g + 1) * 256], [], [("wub", g % 2)])
                    load(0)
                    r = 0
                    for g in range(FC // 2):
                        if g + 1 < FC // 2:
                            load(g + 1)
                        for j in range(2):
                            f = g * 2 + j
                            for (t0, n) in half:
                                bg = (r % 4) * 2
                                bu = bg + 1
                                r += 1
                                for k in range(KC):
                                    self.mm(self.ps[:, bg, 0:n], wgb[g % 2][:, k, j * 128:(j + 1) * 128], u2[:, k, t0 - base:t0 - base + n],
                                            k == 0, k == KC - 1, [("wgb", g % 2)], [("ps", bg)])
                                for k in range(KC):
                                    self.mm(self.ps[:, bu, 0:n], wub[g % 2][:, k, j * 128:(j + 1) * 128], u2[:, k, t0 - base:t0 - base + n],
                                            k == 0, k == KC - 1, [("wub", g % 2)], [("ps", bu)])
                                self.act(sg[:, 0:n], self.ps[:, bg, 0:n], AF.Silu, [("ps", bg)], ["sg"])
                                self.tt("dve", actT[:, f, t0 - base:t0 - base + n], sg[:, 0:n], self.ps[:, bu, 0:n], ALU.mult,
                                        ["sg", ("ps", bu)], [("actT", f, t0)])
                    P.stage_end()
                with ExitStack() as es:
                    sb = lambda n, s, d: self.sb(es, n, s, d)
                    wdb = [sb("wdb", [128, FC, 256], BF16) for _ in range(2)]
                    hrow = [sb("hrow", [128, ntok], F32) for _ in range(3)]

                    def loadd(g):
                        self.dma("pool", wdb[g % 2][:], wd[:, :, g * 256:(g + 1) * 256], [], [("wdb", g % 2)])
                    loadd(0)
                    r = 0
                    for g in range(8):
                        if g + 1 < 8:
                            loadd(g + 1)
                        for j in range(2):
                            c = g * 2 + j
                            hr = c % 3
                            self.dma("sp", hrow[hr][:], hv[c][:, base:base + ntok], [], [("hrow", hr)])
                            for (t0, n) in half:
                                b = r % 8
                                r += 1
                                for f in range(FC):
                                    self.mm(self.ps[:, b, 0:n], wdb[g % 2][:, f, j * 128:(j + 1) * 128], actT[:, f, t0 - base:t0 - base + n],
                                            f == 0, f == FC - 1, [("wdb", g % 2)], [("ps", b)])
                                s = 1 if t0 >= T else 0
                                self.stt(hrow[hr][:, t0 - base:t0 - base + n], self.ps[:, b, 0:n], self.modv[:, 80 + c, s:s + 1],
                                         hrow[hr][:, t0 - base:t0 - base + n], ALU.mult, ALU.add, [("ps", b), ("hrow", hr)], [("hrow", hr)])
                            self.dma("pool", hv[c][:, base:base + ntok], hrow[hr][:], [("hrow", hr)], [("hTc", c)])
                    P.stage_end()

    def stage_final(self):
        P = self.P
        hv = self.hT.rearrange("k p t -> p k t")
        with ExitStack() as es:
            sb = lambda n, s, d: self.sb(es, n, s, d)
            hb = [sb("hb", [128, KC, 512], F32) for _ in range(2)]
            sq = sb("sq", [128, KC, 512], BF16)
            rs = sb("rs", [128, 512], F32)
            rt = sb("rt", [128, 512], F32)
            fg = sb("fg", [128, KC], F32)
            ot = [sb("ot", [128, D], F32) for _ in range(2)]
            self.dma("sp", fg[:], self.fng, [], ["fg"])
            orr = 0
            self.fbank = 0
            for i, (t0, n) in enumerate(TT[:4]):
                r = i % 2
                self.dma("sp", hb[r][:], hv[:, :, t0:t0 + n], [], [("hb", r)])
                self.act(sq[:], hb[r][:], AF.Square, [("hb", r)], ["sq"])
                for k in range(KC):
                    self.mm(self.ps[:, 7, 0:n], self.ones_b[:], sq[:, k, :], k == 0, k == KC - 1, ["sq", "ones_b"], [("ps", 7)])
                self.rsqrt(rs[:], self.ps[:, 7, 0:n], 1.0 / D, rt[:], "rt", [("ps", 7), "eps"], ["rs"])
                self.tt("dve", hb[r][:], hb[r][:], rs[:].unsqueeze(1).to_broadcast([128, KC, n]), ALU.mult, [("hb", r), "rs"], [("hb", r)])
                self.tt("dve", hb[r][:], hb[r][:], fg[:].unsqueeze(2).to_broadcast([128, KC, n]), ALU.mult, [("hb", r), "fg"], [("hb", r)])
                for m in range(n // 128):
                    o = orr % 2
                    orr += 1
                    for g in range(4):
                        b = self.fbank % 7
                        self.fbank += 1
                        for j in range(4):
                            k = g * 4 + j
                            self.mm(self.ps[:, b, j * 128:(j + 1) * 128], hb[r][:, k, m * 128:(m + 1) * 128], self.ident_f[:], True, True,
                                    [("hb", r), "ident_f"], [("ps", b)])
                        self.cp("act" if g % 2 else "dve", ot[o][:, g * 512:(g + 1) * 512], self.ps[:, b, :], [("ps", b)], [("ot", o, g)])
                    tok = t0 + m * 128
                    self.dma("pool", self.out[tok:tok + 128, :], ot[o][:], [("ot", o, g) for g in range(4)], [("out", tok)])
            P.stage_end()

    def pipe_push(self, fn):
        self.pipe.append(fn)
        while len(self.pipe) > self.pipe_depth:
            self.pipe.pop(0)()

    def pipe_flush(self):
        while self.pipe:
            self.pipe.pop(0)()

    def attn_keys(self, pT, q_ap, n, ktiles, bias_ap, scale, slot, qkeys):
        last = len(ktiles) - 1
        for idx, (k_ap, v_ap, mask_ap, rk) in enumerate(ktiles):
            bs = self.sr % 3
            self.sr += 1
            self.mm(self.ps[:, bs, 0:n], k_ap, q_ap, True, True, qkeys + rk, [("ps", bs)])
            pr = self.pr % 4
            self.pr += 1
            self.act(pT[pr][:, 0:n], self.ps[:, bs, 0:n], AF.Exp, [("ps", bs), "mbias"], [("pT", pr)], bias=bias_ap, scale=scale)
            if mask_ap is not None:
                self.tt("dve", pT[pr][:, 0:n], pT[pr][:, 0:n], mask_ap, ALU.mult, [("pT", pr), "mask"], [("pT", pr)])

            def second(idx=idx, pr=pr, v_ap=v_ap, rk=rk):
                self.mm(self.ps[:, 3 + slot, 0:n], v_ap, pT[pr][:, 0:n], idx == 0, idx == last, [("pT", pr)] + rk, [("ps", 3 + slot)])
                self.mm(self.ps[:, 5 + slot, 0:n], self.ones_b[:], pT[pr][:, 0:n], idx == 0, idx == last, [("pT", pr)], [("ps", 5 + slot)])
            self.pipe_push(second)

    def score_bounds(self, es, pairs, scale):
        tmp = self.sb(es, "mtmp", [128, 64], F32)
        for slot, qs, ks in pairs:
            self.tt("dve", tmp[:, slot:slot + 1], self.nrm2[:, qs:qs + 1], self.nrm2[:, ks:ks + 1], ALU.mult, [], [("mtmp", slot)])
            self.act(tmp[:, slot:slot + 1], tmp[:, slot:slot + 1], AF.Sqrt, [("mtmp", slot)], [("mtmp", slot)])
            self.ts("dve", self.mbias[:, slot:slot + 1], tmp[:, slot:slot + 1], -scale, None, ALU.mult, None, [("mtmp", slot)], ["mbias"])

    def stage_diff(self, l, need_ctx):
        P = self.P
        i2 = l // 2
        lam_init = 0.8 - 0.6 * math.exp(-0.3 * l)
        scale = 64.0 ** -0.5
        with ExitStack() as es:
            sb = lambda n, s, d: self.sb(es, n, s, d)
            qT = [sb("qT", [128, NT], BF16) for _ in range(2)]
            qz = [[sb("qz", [128, NT], BF16) for _ in range(2)] for _ in range(2)]
            kT = [sb("kT", [128, NT], BF16) for _ in range(2)]
            V = [sb("V", [128, NTILE, 128], BF16) for _ in range(2)]
            pT = [sb("pT", [128, 512], BF16) for _ in range(4)]
            ybuf = [sb("ybuf", [128, NT], BF16) for _ in range(2)]
            r0 = sb("r0", [128, 512], F32)
            o0 = sb("o0", [128, 512], F32)
            r1 = sb("r1", [128, 512], F32)
            o1 = sb("o1", [128, 512], F32)
            od = sb("od", [128, 512], F32)
            sq = sb("sqd", [128, 512], BF16)
            rs = sb("rsd", [128, 512], F32)
            rt = sb("rtd", [128, 512], F32)
            lamv = sb("lamv", [128, 4, 64], F32)
            lp = sb("lp", [128, 2, 64], F32)
            le = sb("le", [128, 2], F32)
            lamc = sb("lamc", [128, 1], F32)
            sgc = sb("sgc", [128, 1], F32)
            self.sr = 0
            self.pr = 0
            self.pipe = []
            self.pipe_depth = 2
            self.score_bounds(es, [(h * 2 + c, h * 2 + c, 16 + h * 2 + c) for h in range(8) for c in range(2)], scale)
            self.dma("sp", lamv[:], self.lam[i2], [], ["lamv"])
            self.dma("sp", sgc[:], self.subln[i2], [], ["sgc"])
            self.tt("dve", lp[:, 0, :], lamv[:, 0, :], lamv[:, 1, :], ALU.mult, ["lamv"], ["lp"])
            self.tt("dve", lp[:, 1, :], lamv[:, 2, :], lamv[:, 3, :], ALU.mult, ["lamv", "lp"], ["lp"])
            P.op("dve", lambda e: e.tensor_reduce(out=le[:], in_=lp[:], axis=AX.X, op=ALU.add), ["lp"], ["le"])
            self.act(le[:], le[:], AF.Exp, ["le"], ["le"])
            self.tt("dve", lamc[:], le[:, 0:1], le[:, 1:2], ALU.subtract, ["le"], ["lamc"])
            self.ts("dve", lamc[:], lamc[:], lam_init, None, ALU.add, None, ["lamc"], ["lamc"])
            self.ts("dve", sgc[:], sgc[:], 1.0 - lam_init, None, ALU.mult, None, ["sgc"], ["sgc"])
            for c in range(2):
                for r in range(2):
                    P.op("pool", lambda e, c=c, r=r: e.memset(qz[r][c][:], 0.0), [], [("qz", r, c)])
            qtiles = TT if need_ctx else TT[:4]
            for h in range(8):
                r = h % 2
                self.dma("sp", qT[r][:], self.qT_b[h], [], [("qT", r)])
                self.dma("sp", kT[r][:], self.kT_b[h], [], [("kT", r)])
                self.dma("sp", V[r][:], self.v_b[:, h * 128:(h + 1) * 128].rearrange("(t p) d -> p t d", p=128), [], [("V", r)])
                for c in range(2):
                    self.cp("pool", qz[r][c][c * 64:(c + 1) * 64, :], qT[r][c * 64:(c + 1) * 64, :], [("qT", r)], [("qz", r, c)])
                for (t0, n) in qtiles:
                    kts = list(range(NTILE)) if t0 < T else [16, 17]
                    for c in range(2):
                        kl = [(kT[r][:, m * 128:(m + 1) * 128], V[r][:, m, :], None, [("kT", r), ("V", r)]) for m in kts]
                        self.attn_keys(pT, qz[r][c][:, t0:t0 + n], n, kl, self.mbias[:, h * 2 + c:h * 2 + c + 1], scale, c, [("qz", r, c)])

                    def epi(t0=t0, n=n, r=r):
                        self.recip(r0[:, 0:n], self.ps[:, 5, 0:n], [("ps", 5)], ["r0"])
                        self.tt("dve", o0[:, 0:n], self.ps[:, 3, 0:n], r0[:, 0:n], ALU.mult, [("ps", 3), "r0"], ["o0"])
                        self.recip(r1[:, 0:n], self.ps[:, 6, 0:n], [("ps", 6)], ["r1"])
                        self.ts("dve", r1[:, 0:n], r1[:, 0:n], lamc[:, 0:1], None, ALU.mult, None, ["r1", "lamc"], ["r1"])
                        self.tt("dve", o1[:, 0:n], self.ps[:, 4, 0:n], r1[:, 0:n], ALU.mult, [("ps", 4), "r1"], ["o1"])
                        self.tt("dve", od[:, 0:n], o0[:, 0:n], o1[:, 0:n], ALU.subtract, ["o0", "o1"], ["od"])
                        self.act(sq[:, 0:n], od[:, 0:n], AF.Square, ["od"], ["sqd"])
                        self.mm(self.ps[:, 7, 0:n], self.ones_b[:], sq[:, 0:n], True, True, ["sqd"], [("ps", 7)])
                        self.rsqrt(rs[:, 0:n], self.ps[:, 7, 0:n], 1.0 / 128, rt[:, 0:n], "rtd", [("ps", 7)], ["rsd"])
                        self.stt(ybuf[r][:, t0:t0 + n], od[:, 0:n], sgc[:, 0:1], rs[:, 0:n], ALU.mult, ALU.mult, ["od", "rsd", "sgc"], [("ybuf", r)])
                    self.pipe_push(epi)
                tmax = NT if need_ctx else T

                def store(h=h, r=r, tmax=tmax):
                    self.dma("pool", self.yT[8 + h][:, 0:tmax], ybuf[r][:, 0:tmax], [("ybuf", r)], [("yT", 8 + h)])
                self.pipe_push(store)
            self.pipe_flush()
            P.stage_end()

    def stage_na(self, l, need_ctx):
        P = self.P
        i2 = l // 2
        scale = 128.0 ** -0.5
        with ExitStack() as es:
            sb = lambda n, s, d: self.sb(es, n, s, d)
            qT = [sb("qT", [128, NT], BF16) for _ in range(2)]
            kT = [sb("kT", [128, NT], BF16) for _ in range(2)]
            V = [sb("V", [128, NTILE, 128], BF16) for _ in range(2)]
            Bf = sb("Bf", [128, N_NA_MASK, 128], F32)
            E = [sb("E", [128, N_NA_MASK, 128], BF16) for _ in range(2)]
            pT = [sb("pT", [128, 512], BF16) for _ in range(4)]
            ybuf = [sb("ybuf", [128, NT], BF16) for _ in range(2)]
            rr = sb("rr", [128, 512], F32)
            self.sr = 0
            self.pr = 0
            self.pipe = []
            self.pipe_depth = 2
            self.score_bounds(es, [(h, h, 8 + h) for h in range(8)], scale)
            for h in range(8):
                r = h % 2
                self.dma("sp", qT[r][:], self.qT_a[h], [], [("qT", r)])
                self.dma("sp", kT[r][:], self.kT_a[h], [], [("kT", r)])
                self.dma("sp", V[r][:], self.v_a[:, h * 128:(h + 1) * 128].rearrange("(t p) d -> p t d", p=128), [], [("V", r)])
                self.dma("sp", Bf[:], self.na_bias[i2, h], [], ["Bf"])
                self.act(E[r][:], Bf[:], AF.Exp, ["Bf"], [("E", r)])
                blocks = [(a * 128, 128, a) for a in range(16)]
                if need_ctx:
                    blocks.append((T, C, None))
                for bi, (t0, n, a) in enumerate(blocks):
                    rk = [("kT", r), ("V", r)]
                    if a is None:
                        kl = [(kT[r][:, m * 128:(m + 1) * 128], V[r][:, m, :], None, rk) for m in (16, 17)]
                    else:
                        kl = [(kT[r][:, t * 128:(t + 1) * 128], V[r][:, t, :], E[r][:, mid, :], rk + [("E", r)]) for (t, mid) in _NA_PLAN[a]]
                        kl += [(kT[r][:, m * 128:(m + 1) * 128], V[r][:, m, :], None, rk) for m in (16, 17)]
                    slot = bi % 2
                    self.attn_keys(pT, qT[r][:, t0:t0 + n], n, kl, self.mbias[:, h:h + 1], scale, slot, [("qT", r)])

                    def epi(t0=t0, n=n, r=r, slot=slot):
                        self.recip(rr[:, 0:n], self.ps[:, 5 + slot, 0:n], [("ps", 5 + slot)], ["rr"])
                        self.tt("dve", ybuf[r][:, t0:t0 + n], self.ps[:, 3 + slot, 0:n], rr[:, 0:n], ALU.mult, [("ps", 3 + slot), "rr"], [("ybuf", r)])
                    self.pipe_push(epi)
                tmax = NT if need_ctx else T

                def store(h=h, r=r, tmax=tmax):
                    self.dma("pool", self.yT[h][:, 0:tmax], ybuf[r][:, 0:tmax], [("ybuf", r)], [("yT", h)])
                self.pipe_push(store)
            self.pipe_flush()
            P.stage_end()

    def stage_win(self, l, need_ctx):
        P = self.P
        i2 = l // 2
        scale = 128.0 ** -0.5
        with ExitStack() as es:
            sb = lambda n, s, d: self.sb(es, n, s, d)
            qT = [sb("qT", [128, NT], BF16) for _ in range(2)]
            kT = [sb("kT", [128, NT], BF16) for _ in range(2)]
            V = [sb("V", [128, NTILE, 128], BF16) for _ in range(2)]
            wmf = sb("wmf", [128, 6, 512], F32)
            wm = sb("wm", [128, 6, 512], BF16)
            pT = [sb("pT", [128, 512], BF16) for _ in range(4)]
            ybuf = [sb("ybuf", [128, NT], BF16) for _ in range(2)]
            rr = sb("rr", [128, 512], F32)
            sk = sb("sk", [128, 8], F32)
            esk = sb("esk", [128, 8], F32)
            self.sr = 0
            self.pr = 0
            self.pipe = []
            self.pipe_depth = 2
            self.score_bounds(es, [(16 + h, 16 + h, 24 + h // 4) for h in range(8)], scale)
            self.dma("sp", wmf[:], self.c_win.rearrange("r p q -> p r q"), [], ["wmf"])
            self.cp("dve", wm[:], wmf[:], ["wmf"], ["mask"])
            self.dma("sp", sk[:], self.sink[i2], [], ["sk"])
            for h in range(8):
                r = h % 2
                kv = h // 4
                self.dma("sp", qT[r][:], self.qT_b[h], [], [("qT", r)])
                self.dma("sp", kT[r][:], self.kT_b[kv], [], [("kT", r)])
                self.dma("sp", V[r][:], self.v_b[:, kv * 128:(kv + 1) * 128].rearrange("(t p) d -> p t d", p=128), [], [("V", r)])
                self.act(esk[:, h:h + 1], sk[:, h:h + 1], AF.Exp, ["sk", "mbias"], [("esk", h)], bias=self.mbias[:, 16 + h:17 + h])
                blocks = [(b4 * 512, 512, b4) for b4 in range(4)]
                if need_ctx:
                    blocks.append((T, C, None))
                for bi, (t0, n, b4) in enumerate(blocks):
                    rk = [("kT", r), ("V", r)]
                    kl = []
                    if b4 is not None:
                        for t in range(4 * b4 - 1, 4 * b4 + 5):
                            if 0 <= t < 16:
                                kl.append((kT[r][:, t * 128:(t + 1) * 128], V[r][:, t, :], wm[:, t - 4 * b4 + 1, :], rk))
                    kl += [(kT[r][:, m * 128:(m + 1) * 128], V[r][:, m, :], None, rk) for m in (16, 17)]
                    slot = bi % 2
                    self.attn_keys(pT, qT[r][:, t0:t0 + n], n, kl, self.mbias[:, 16 + h:17 + h], scale, slot, [("qT", r)])

                    def epi(t0=t0, n=n, r=r, slot=slot, h=h):
                        self.ts("dve", rr[:, 0:n], self.ps[:, 5 + slot, 0:n], esk[:, h:h + 1], None, ALU.add, None, [("ps", 5 + slot), ("esk", h)], ["rr"])
                        self.recip(rr[:, 0:n], rr[:, 0:n], ["rr"], ["rr"])
                        self.tt("dve", ybuf[r][:, t0:t0 + n], self.ps[:, 3 + slot, 0:n], rr[:, 0:n], ALU.mult, [("ps", 3 + slot), "rr"], [("ybuf", r)])
                    self.pipe_push(epi)
                tmax = NT if need_ctx else T

                def store(h=h, r=r, tmax=tmax):
                    self.dma("pool", self.yT[8 + h][:, 0:tmax], ybuf[r][:, 0:tmax], [("ybuf", r)], [("yT", 8 + h)])
                self.pipe_push(store)
            self.pipe_flush()
            P.stage_end()

    def stage_gdn(self, l):
        P = self.P
        i2 = l // 2
        NDT = self.ndt
        orders = [[16, 17] + list(range(16)), [17, 16] + list(range(15, -1, -1))]
        with ExitStack() as es:
            sb = lambda n, s, d: self.sb(es, n, s, d)
            tri = sb("tri", [128, 4, 128], F32)
            trin = sb("trin", [128, 4, 128], NDT) if NDT != F32 else tri
            identn = self.ident_f if NDT == F32 else self.ident_b
            ab = sb("ab", [128, NTILE, 32], F32)
            gvec = sb("gvec", [128, 32], F32)
            negA = sb("negA", [128, 16], F32)
            gsb = sb("gsb", [128, NTILE, 16], F32)
            bsb = sb("bsb", [128, NTILE, 16], F32)
            nbs = sb("nbs", [128, NTILE, 16], F32)
            tot = sb("tot", [128, NTILE, 16], F32)
            glast = sb("glast", [128, NTILE, 16], F32)
            gc = sb("gc", [128, NTILE, 16], F32)
            eg = sb("eg", [128, NTILE, 16], F32)
            kd = sb("kd", [128, NTILE, 16], F32)
            bg = sb("bg", [128, NTILE, 16], F32)
            S = [[sb("S", [128, 4, 128], F32) for _ in range(2)] for _ in range(2)]
            Sb = [[sb("Sb", [128, 4, 128], BF16) for _ in range(2)] for _ in range(2)]
            self.dma("sp", tri[:], self.c_tri.rearrange("a p f -> p a f"), [], ["tri"])
            if NDT != F32:
                self.cp("dve", trin[:], tri[:], ["tri"], ["trin"])
            self.dma("sp", ab[:], self.gab.rearrange("(t p) c -> p t c", p=128), [], ["ab"])
            self.dma("sp", gvec[:], self.gdn_vec[i2], [], ["gvec"])
            self.act(negA[:], gvec[:, 0:16], AF.Exp, ["gvec"], ["negA"])
            self.ts("dve", negA[:], negA[:], -1.0, None, ALU.mult, None, ["negA"], ["negA"])
            self.tt("dve", gsb[:], ab[:, :, 0:16], gvec[:, 16:32].unsqueeze(1).to_broadcast([128, NTILE, 16]), ALU.add, ["ab", "gvec"], ["gsb"])
            self.act(gsb[:], gsb[:], AF.Exp, ["gsb"], ["gsb"])
            self.act(gsb[:], gsb[:], AF.Ln, ["gsb"], ["gsb"], bias=1.0, scale=1.0)
            self.tt("dve", gsb[:], gsb[:], negA[:].unsqueeze(1).to_broadcast([128, NTILE, 16]), ALU.mult, ["gsb", "negA"], ["gsb"])
            self.act(bsb[:], ab[:, :, 16:32], AF.Sigmoid, ["ab"], ["bsb"])
            self.ts("dve", nbs[:], bsb[:], -1.0, None, ALU.mult, None, ["bsb"], ["nbs"])
            gflat = gsb[:].rearrange("p t c -> p (t c)")
            self.mm(self.ps[:, 0, 0:NTILE * 16], self.ones_f[:], gflat, True, True, ["gsb", "ones_f"], [("ps", 0)])
            self.cp("dve", tot[:].rearrange("p t c -> p (t c)"), self.ps[:, 0, 0:NTILE * 16], [("ps", 0)], ["tot"])
            self.act(glast[:], tot[:], AF.Exp, ["tot"], ["glast"])
            for d in range(2):
                self.mm(self.ps[:, 1 + d, 0:NTILE * 8], tri[:, d, :], gsb[:, :, d * 8:(d + 1) * 8], True, True, ["gsb", "tri"], [("ps", 1 + d)])
                self.cp("dve", gc[:, :, d * 8:(d + 1) * 8], self.ps[:, 1 + d, 0:NTILE * 8].rearrange("p (t c) -> p t c", c=8), [("ps", 1 + d)], ["gc"])
            self.act(eg[:], gc[:], AF.Exp, ["gc"], ["eg"])
            self.tt("dve", kd[:], tot[:], gc[:], ALU.subtract, ["tot", "gc"], ["kd"])
            self.act(kd[:], kd[:], AF.Exp, ["kd"], ["kd"])
            self.tt("dve", bg[:], bsb[:], eg[:], ALU.mult, ["bsb", "eg"], ["bg"])
            for d in range(2):
                for hg in range(2):
                    P.op("pool", lambda e, d=d, hg=hg: e.memset(S[d][hg][:], 0.0), [], [("S", d, hg)])
                    P.op("pool", lambda e, d=d, hg=hg: e.memset(Sb[d][hg][:], 0.0), [], [("Sb", d, hg)])
            qTt = [[sb("qTt", [128, 8, 128], BF16) for _ in range(2)] for _ in range(2)]
            kTt = [[sb("kTt", [128, 8, 128], BF16) for _ in range(2)] for _ in range(2)]
            ktm = [[sb("ktm", [128, 8, 128], BF16) for _ in range(2)] for _ in range(1)]
            vtm = [[sb("vtm", [128, 8, 128], BF16) for _ in range(2)] for _ in range(1)]
            vb = [[sb("vb", [128, 8, 128], BF16) for _ in range(2)] for _ in range(2)]
            kbg = [[sb("kbg", [128, 8, 128], BF16) for _ in range(2)] for _ in range(2)]
            kdc = [[sb("kdc", [128, 8, 128], BF16) for _ in range(2)] for _ in range(2)]
            Ug = [[sb("Ug", [128, 8, 128], F32) for _ in range(2)] for _ in range(1)]
            def slotbufs(name, dt, nring):
                return [[[sb(name, [128, 4, 128], dt) for _ in range(nring)] for _ in range(2)] for _ in range(2)]
            Eb = slotbufs("Eb", F32, 1)
            ETb = slotbufs("ETb", F32, 1)
            Pb = slotbufs("Pb", NDT, 2)
            PTb = slotbufs("PTb", NDT, 2)
            RTb = slotbufs("RTb", NDT, 2)
            TTb = slotbufs("TTb", BF16, 1)
            aT = slotbufs("aT", BF16, 2)
            ub = slotbufs("ub", F32, 2)
            wT = slotbufs("wT", BF16, 2)
            vn = slotbufs("vn", BF16, 1)
            ot = slotbufs("ot", F32, 1)
            oo = slotbufs("oo", F32, 1)
            qv = self.qT_a.rearrange("h p t -> p h t")
            kv = self.kT_a.rearrange("h p t -> p h t")
            self.bank = 0

            def nb():
                b = self.bank % 8
                self.bank += 1
                return b

            def bc_h(ap2):
                return ap2.unsqueeze(1).to_broadcast([128, 4, 128])

            def bc_e(ap1):
                return ap1.unsqueeze(2).to_broadcast([128, ap1.shape[1], 128])

            def precompute(s):
                sr = s % 2
                for d in range(2):
                    c = orders[d][s]
                    cs = slice(d * 8, (d + 1) * 8)
                    self.tt("dve", Ug[0][d][:], tri[:, d, :].unsqueeze(1).to_broadcast([128, 8, 128]), bc_e(gsb[:, c, cs]), ALU.mult,
                            ["tri", "gsb"], [("Ug", d)])
                for d in range(2):
                    c = orders[d][s]
                    tk = (sr, d)
                    self.dma("sp", qTt[sr][d][:], qv[:, :, c * 128:(c + 1) * 128], [], [("qTt",) + tk])
                    self.dma("sp", kTt[sr][d][:], kv[:, :, c * 128:(c + 1) * 128], [], [("kTt",) + tk])
                    self.dma("sp", ktm[0][d][:].rearrange("p h e -> p (h e)"), self.k_tm[c * 128:(c + 1) * 128, :], [], [("ktm", d)])
                    self.dma("sp", vtm[0][d][:].rearrange("p h e -> p (h e)"), self.v_a[c * 128:(c + 1) * 128, :], [], [("vtm", d)])
                    cs = slice(d * 8, (d + 1) * 8)
                    self.tt("dve", vb[sr][d][:], vtm[0][d][:], bc_e(bsb[:, c, cs]), ALU.mult, [("vtm", d), "bsb"], [("vb",) + tk])
                    self.tt("pool", kbg[sr][d][:], ktm[0][d][:], bc_e(bg[:, c, cs]), ALU.mult, [("ktm", d), "bg"], [("kbg",) + tk])
                    self.tt("pool", kdc[sr][d][:], ktm[0][d][:], bc_e(kd[:, c, cs]), ALU.mult, [("ktm", d), "kd"], [("kdc",) + tk])
                slots = [(d, hg) for d in range(2) for hg in range(2)]
                lvl = getattr(self, "gdn_lvl", 9)
                if lvl <= 1:
                    return
                for (d, hg) in slots:
                    c = orders[d][s]
                    tk = (sr, d)
                    sk = (d, hg)
                    hs = range(hg * 4, hg * 4 + 4)
                    bD, bDT, bG, bR = nb(), nb(), nb(), nb()
                    for j, h in enumerate(hs):
                        self.mm(self.ps[:, bD, j * 128:(j + 1) * 128], Ug[0][d][:, h, :], tri[:, 2 + d, :], True, True, [("Ug", d), "tri"], [("ps", bD)])
                    for j, h in enumerate(hs):
                        self.mm(self.ps[:, bDT, j * 128:(j + 1) * 128], tri[:, 2 + d, :], Ug[0][d][:, h, :], True, True, [("Ug", d), "tri"], [("ps", bDT)])
                    for j, h in enumerate(hs):
                        self.mm(self.ps[:, bG, j * 128:(j + 1) * 128], kTt[sr][d][:, h, :], kTt[sr][d][:, h, :], True, True, [("kTt",) + tk], [("ps", bG)])
                    for j, h in enumerate(hs):
                        self.mm(self.ps[:, bR, j * 128:(j + 1) * 128], kTt[sr][d][:, h, :], qTt[sr][d][:, h, :], True, True,
                                [("kTt",) + tk, ("qTt",) + tk], [("ps", bR)])
                    E = Eb[d][hg][0]
                    ET = ETb[d][hg][0]
                    p4 = lambda b: self.ps[:, b, :].rearrange("p (h e) -> p h e", h=4)
                    self.act(E[:], p4(bD), AF.Exp, [("ps", bD)], [("E",) + sk])
                    self.tt("dve", E[:], E[:], bc_h(tri[:, 2 + d, :]), ALU.mult, [("E",) + sk, "tri"], [("E",) + sk])
                    self.tt("dve", E[:], p4(bG), E[:], ALU.mult, [("ps", bG), ("E",) + sk], [("E",) + sk])
                    P0 = Pb[d][hg][0]
                    hsl = slice(d * 8 + hg * 4, d * 8 + hg * 4 + 4)
                    self.tt("dve", P0[:], E[:], bc_e(nbs[:, c, hsl]), ALU.mult, [("E",) + sk, "nbs"], [("P", 0) + sk])
                    self.act(ET[:], p4(bDT), AF.Exp, [("ps", bDT)], [("ET",) + sk])
                    self.tt("dve", ET[:], ET[:], bc_h(tri[:, d, :]), ALU.mult, [("ET",) + sk, "tri"], [("ET",) + sk])
                    self.tt("dve", aT[d][hg][sr][:], p4(bR), ET[:], ALU.mult, [("ps", bR), ("ET",) + sk], [("aT", sr) + sk])
                    bT = nb()
                    for j in range(4):
                        self.mm(self.ps[:, bT, j * 128:(j + 1) * 128], P0[:, j, :], identn[:], True, True, [("P", 0) + sk], [("ps", bT)])
                    self.cp("act", PTb[d][hg][0][:], p4(bT), [("ps", bT)], [("PT", 0) + sk])
                    self.tt("dve", RTb[d][hg][0][:], p4(bT), bc_h(identn[:]), ALU.add, [("ps", bT)], [("RT", 0) + sk])
                if lvl <= 2:
                    return
                for k in range(1, 7):
                    cur, prv = k % 2, (k - 1) % 2
                    for (d, hg) in slots:
                        sk = (d, hg)
                        p4 = lambda b: self.ps[:, b, :].rearrange("p (h e) -> p h e", h=4)
                        Pp, PTp = Pb[d][hg][prv], PTb[d][hg][prv]
                        Pn, PTn = Pb[d][hg][cur], PTb[d][hg][cur]
                        bP = nb()
                        for j in range(4):
                            self.mm(self.ps[:, bP, j * 128:(j + 1) * 128], PTp[:, j, :], Pp[:, j, :], True, True,
                                    [("P", prv) + sk, ("PT", prv) + sk], [("ps", bP)])
                        if k < 6:
                            bPT = nb()
                            for j in range(4):
                                self.mm(self.ps[:, bPT, j * 128:(j + 1) * 128], Pp[:, j, :], PTp[:, j, :], True, True,
                                        [("P", prv) + sk, ("PT", prv) + sk], [("ps", bPT)])
                        self.cp("act", Pn[:], p4(bP), [("ps", bP)], [("P", cur) + sk])
                        if k < 6:
                            self.cp("act", PTn[:], p4(bPT), [("ps", bPT)], [("PT", cur) + sk])
                        bRT = nb()
                        for j in range(4):
                            self.mm(self.ps[:, bRT, j * 128:(j + 1) * 128], Pn[:, j, :], RTb[d][hg][prv][:, j, :], True, True,
                                    [("P", cur) + sk, ("RT", prv) + sk], [("ps", bRT)])
                        self.tt("dve", RTb[d][hg][cur][:], p4(bRT), RTb[d][hg][prv][:], ALU.add, [("ps", bRT), ("RT", prv) + sk], [("RT", cur) + sk])
                if lvl <= 3:
                    return
                for (d, hg) in slots:
                    sk = (d, hg)
                    tk = (sr, d)
                    p4 = lambda b: self.ps[:, b, :].rearrange("p (h e) -> p h e", h=4)
                    self.cp("act", TTb[d][hg][0][:], RTb[d][hg][0][:], [("RT", 0) + sk], [("TT",) + sk])
                    bU, bW = nb(), nb()
                    for j in range(4):
                        h = hg * 4 + j
                        self.mm(self.ps[:, bU, j * 128:(j + 1) * 128], TTb[d][hg][0][:, j, :], vb[sr][d][:, h, :], True, True,
                                [("TT",) + sk, ("vb",) + tk], [("ps", bU)])
                    for j in range(4):
                        h = hg * 4 + j
                        self.mm(self.ps[:, bW, j * 128:(j + 1) * 128], kbg[sr][d][:, h, :], TTb[d][hg][0][:, j, :], True, True,
                                [("TT",) + sk, ("kbg",) + tk], [("ps", bW)])
                    self.cp("act", ub[d][hg][sr][:], p4(bU), [("ps", bU)], [("ub", sr) + sk])
                    self.cp("dve", wT[d][hg][sr][:], p4(bW), [("ps", bW)], [("wT", sr) + sk])

            def recur(s):
                sr = s % 2
                for d in range(2):
                    c = orders[d][s]
                    tk = (sr, d)
                    for hg in range(2):
                        sk = (d, hg)
                        p4 = lambda b: self.ps[:, b, :].rearrange("p (h e) -> p h e", h=4)
                        hsl = slice(d * 8 + hg * 4, d * 8 + hg * 4 + 4)
                        bW, bO1, bO2, bS = nb(), nb(), nb(), nb()
                        for j in range(4):
                            self.mm(self.ps[:, bW, j * 128:(j + 1) * 128], wT[d][hg][sr][:, j, :], Sb[d][hg][:, j, :], True, True,
                                    [("wT", sr) + sk, ("Sb",) + sk], [("ps", bW)])
                        for j in range(4):
                            h = hg * 4 + j
                            self.mm(self.ps[:, bO1, j * 128:(j + 1) * 128], qTt[sr][d][:, h, :], Sb[d][hg][:, j, :], True, True,
                                    [("qTt",) + tk, ("Sb",) + sk], [("ps", bO1)])
                        V_ = vn[d][hg][0]
                        self.tt("dve", V_[:], ub[d][hg][sr][:], p4(bW), ALU.subtract, [("ub", sr) + sk, ("ps", bW)], [("vn",) + sk])
                        for j in range(4):
                            self.mm(self.ps[:, bO2, j * 128:(j + 1) * 128], aT[d][hg][sr][:, j, :], V_[:, j, :], True, True,
                                    [("aT", sr) + sk, ("vn",) + sk], [("ps", bO2)])
                        for j in range(4):
                            h = hg * 4 + j
                            self.mm(self.ps[:, bS, j * 128:(j + 1) * 128], kdc[sr][d][:, h, :], V_[:, j, :], True, True,
                                    [("kdc",) + tk, ("vn",) + sk], [("ps", bS)])
                        O_ = oo[d][hg][0]
                        self.tt("dve", ot[d][hg][0][:], p4(bO1), bc_e(eg[:, c, hsl]), ALU.mult, [("ps", bO1), "eg"], [("ot",) + sk])
                        self.tt("dve", O_[:], ot[d][hg][0][:], p4(bO2), ALU.add, [("ot",) + sk, ("ps", bO2)], [("oo",) + sk])
                        self.dma("pool", self.go[d, c * 128:(c + 1) * 128, hg * 512:(hg + 1) * 512], O_[:].rearrange("p h e -> p (h e)"),
                                 [("oo",) + sk], [("go", d, c, hg)])
                        self.tt("dve", S[d][hg][:], S[d][hg][:], bc_e(glast[:, c, hsl]), ALU.mult, [("S",) + sk, "glast"], [("S",) + sk])
                        self.tt("dve", S[d][hg][:], S[d][hg][:], p4(bS), ALU.add, [("S",) + sk, ("ps", bS)], [("S",) + sk])
                        self.cp("act", Sb[d][hg][:], S[d][hg][:], [("S",) + sk], [("Sb",) + sk])

            stop = getattr(self, "gdn_stop", None)
            if stop != "pre":
                precompute(0)
            nsteps = NTILE if stop is None else (0 if stop in ("pre", "pc0") else int(stop))
            for s in range(nsteps):
                if s + 1 < NTILE:
                    precompute(s + 1)
                recur(s)
            P.stage_end()
        with ExitStack() as es:
            sb = lambda n, s, d: self.sb(es, n, s, d)
            of = [sb("of", [128, 8, 128], F32) for _ in range(2)]
            obk = [sb("obk", [128, 8, 128], F32) for _ in range(2)]
            zb = [sb("zb", [128, 8, 128], BF16) for _ in range(2)]
            sz = sb("sz", [128, 8, 128], F32)
            sq = sb("sqg", [128, 8, 128], F32)
            ss = sb("ss", [128, 8], F32)
            st = sb("sst", [128, 8], F32)
            an = sb("an", [128, 8, 128], BF16)
            gn = sb("gn", [128, 128], F32)
            ybuf = sb("ybuf8", [128, 8, NT], BF16)
            self.dma("sp", gn[:], self.gng[i2], [], ["gn"])
            bank = 0
            for m in range(NTILE):
                r = m % 2
                rows = slice(m * 128, (m + 1) * 128)
                self.dma("sp", of[r][:].rearrange("p h e -> p (h e)"), self.go[0, rows, :], [], [("of", r)])
                self.dma("sp", obk[r][:].rearrange("p h e -> p (h e)"), self.go[1, rows, :], [], [("obk", r)])
                self.dma("sp", zb[r][:].rearrange("p h e -> p (h e)"), self.gz[rows, :], [], [("zb", r)])
                self.tt("dve", of[r][:], of[r][:], obk[r][:], ALU.add, [("of", r), ("obk", r)], [("of", r)])
                self.act(sq[:], of[r][:], AF.Square, [("of", r)], ["sqg"])
                P.op("dve", lambda e: e.tensor_reduce(out=ss[:], in_=sq[:], axis=AX.X, op=ALU.add), ["sqg"], ["ss"])
                self.rsqrt(ss[:], ss[:], 1.0 / 128, st[:], "sst", ["ss"], ["ss"])
                self.tt("dve", of[r][:], of[r][:], ss[:].unsqueeze(2).to_broadcast([128, 8, 128]), ALU.mult, [("of", r), "ss"], [("of", r)])
                self.tt("dve", of[r][:], of[r][:], gn[:].unsqueeze(1).to_broadcast([128, 8, 128]), ALU.mult, [("of", r), "gn"], [("of", r)])
                self.act(sz[:], zb[r][:], AF.Silu, [("zb", r)], ["sz"])
                self.tt("dve", an[:], of[r][:], sz[:], ALU.mult, [("of", r), "sz"], ["an"])
                for hg in range(2):
                    b = bank % 8
                    bank += 1
                    for j in range(4):
                        self.mm(self.ps[:, b, j * 128:(j + 1) * 128], an[:, hg * 4 + j, :], self.ident_b[:], True, True, ["an"], [("ps", b)])
                    self.cp("act", ybuf[:, hg * 4:(hg + 1) * 4, m * 128:(m + 1) * 128], self.ps[:, b, :].rearrange("p (h e) -> p h e", h=4),
                            [("ps", b)], [("ybuf", m, hg)])
            for h in range(8):
                self.dma("pool", self.yT[h], ybuf[:, h, :], [("ybuf", m, h // 4) for m in range(NTILE)], [("yT", h)])
            P.stage_end()


    def build(self, layer_list=None, do_final=True, upto=None):
        self.declare()
        layer_list = list(range(DEPTH)) if layer_list is None else layer_list
        with ExitStack() as es:
            self.setup(es)
            self.stage_load_inputs()
            for l in layer_list:
                need_ctx = l < DEPTH - 1
                tiles = TT if need_ctx else TT[:4]
                self.stage_mod(l)
                if upto == "mod":
                    break
                self.stage_inproj(l)
                if upto == "inproj":
                    break
                if l % 2 == 0:
                    self.stage_gdn(l)
                    if upto == "gdn":
                        break
                    self.stage_diff(l, need_ctx)
                else:
                    self.stage_na(l, need_ctx)
                    self.stage_win(l, need_ctx)
                if upto == "mixer":
                    break
                self.stage_outproj(l, tiles)
                if upto == "outproj":
                    break
                self.stage_ffn(l, tiles)
            if do_final and upto is None:
                self.stage_final()
            self.P.final_wait()
        self.P.close()


def _fm(v):
    return np.ascontiguousarray(np.asarray(v, np.float32).reshape(KC, 128).T)


def _rep(v):
    v = np.asarray(v, np.float32)
    return np.ascontiguousarray(np.broadcast_to(v[None], (128,) + v.shape))


def prep_shared(inp):
    f = lambda a: np.ascontiguousarray(np.asarray(a, np.float32))
    sh = {}
    for k in ("ada_w", "w_in_even", "w_in_odd", "w_out", "ffn_w_gate", "ffn_w_up", "ffn_w_down"):
        sh[k] = f(inp[k])
    sh["ada_b_fm"] = np.ascontiguousarray(f(inp["ada_b"]).reshape(DEPTH, 96, 128).transpose(0, 2, 1))
    sh["nmg_fm"] = np.stack([_fm(inp["norm_mix_g"][l]) for l in range(DEPTH)])
    sh["nfg_fm"] = np.stack([_fm(inp["norm_ffn_g"][l]) for l in range(DEPTH)])
    sh["fng_fm"] = _fm(inp["final_norm_g"])
    cw = f(inp["gdn_conv_w"])
    sh["conv_fm"] = np.ascontiguousarray(cw.reshape(2, 3, 24, 128).transpose(0, 3, 2, 1))
    gv = np.concatenate([f(inp["gdn_a_log"]).reshape(2, 16), f(inp["gdn_dt_bias"]).reshape(2, 16)], axis=1)
    sh["gdn_vec"] = np.stack([_rep(gv[i]) for i in range(2)])
    sh["gng_rep"] = np.stack([_rep(f(inp["gdn_norm_g"])[i]) for i in range(2)])
    lam = np.stack([f(inp["diff_lambda_q1"]), f(inp["diff_lambda_k1"]), f(inp["diff_lambda_q2"]), f(inp["diff_lambda_k2"])], axis=1)
    sh["lam_rep"] = np.stack([_rep(lam[i]) for i in range(2)])
    sh["subln_fm"] = np.ascontiguousarray(f(inp["diff_subln_g"]).reshape(2, 128, 1))
    rpb = f(inp["na_rpb"])
    kr = np.repeat(np.arange(2), 64)
    kc = np.tile(np.arange(64), 2)
    nb = np.full((2, 8, 128, N_NA_MASK, 128), -30000.0, np.float32)
    for mid, (delta, ok) in enumerate(_NA_MASKS):
        dr = np.clip(2 * delta + kr[:, None] - kr[None, :] + 7, 0, 14)
        dc = np.clip(kc[:, None] - kc[None, :] + 15, 0, 30)
        g = rpb[:, :, dr, dc]
        nb[:, :, :, mid, :] = np.where(ok[None, None] > 0, g, np.float32(-30000.0))
    sh["na_bias"] = nb
    sh["sink_rep"] = np.stack([_rep(f(inp["win_sink"])[i]) for i in range(2)])
    c = _consts()
    sh["c_ident"] = c["ident"]
    sh["c_tri"] = c["tri"]
    sh["c_rope_d"] = c["rope_d"]
    sh["c_rope_w"] = c["rope_w"]
    sh["c_perm"] = c["perm"]
    sh["c_win_mask"] = c["win_mask"]
    return sh


def prep_core(inp, b):
    x = np.ascontiguousarray(np.asarray(inp["x"][b], np.float32))
    ctx = np.ascontiguousarray(np.asarray(inp["ctx"][b], np.float32))
    cf = np.stack([_fm(inp["c"][b]), _fm(inp["c_ctx"])], axis=2)
    return {"x": x, "ctx": ctx, "c_fm": np.ascontiguousarray(cf)}


_CACHE = {}


def kernel(**inputs):
    n = 8
    nc = bass.Bass("TRN2", target_bir_lowering=False)
    bld = Builder(nc)
    bld.build()
    shared = prep_shared(inputs)
    in_maps = []
    for b in range(n):
        m = dict(shared)
        m.update(prep_core(inputs, b))
        in_maps.append(m)
    res = run_bass_kernel_spmd(nc, in_maps, core_ids=list(range(n)))
    return np.stack([np.asarray(r["out"], np.float32) for r in res.results], axis=0)
```

```python
import math
from contextlib import ExitStack

import numpy as np

import concourse.bass as bass
import concourse.mybir as mybir
from concourse.bass_utils import run_bass_kernel_spmd

F32 = mybir.dt.float32
BF16 = mybir.dt.bfloat16
AF = mybir.ActivationFunctionType
ALU = mybir.AluOpType
AX = mybir.AxisListType

D = 2048
KC = 16
T = 2048
C = 256
NT = T + C
NTILE = NT // 128
FF = 5632
FC = FF // 128
DEPTH = 4
EPS = 1e-6
TT = [(0, 512), (512, 512), (1024, 512), (1536, 512), (2048, 256)]
W_EVEN = 7200
W_ODD = 4608

ENGS = ("pe", "act", "dve", "pool", "sp")


class _Op:
    __slots__ = ("eng", "emit", "deps", "is_dma", "sem", "semval", "need_inc")

    def __init__(self, eng, emit, is_dma):
        self.eng = eng
        self.emit = emit
        self.deps = []
        self.is_dma = is_dma
        self.sem = None
        self.semval = 0
        self.need_inc = False


class Prog:
    NDMASEM = 8

    def __init__(self, nc, same_eng_sync=True):
        self.nc = nc
        self.same_eng_sync = same_eng_sync
        self.stack = ExitStack()
        self.esem = {e: self.stack.enter_context(nc.semaphore("s_" + e)) for e in ENGS}
        self.ecount = {e: 0 for e in ENGS}
        self.dsem, self.dcount, self.drr, self.dlast = {}, {}, {}, {}
        for q in ("sp", "pool", "act"):
            self.dsem[q] = [self.stack.enter_context(nc.semaphore("d_%s%d" % (q, i))) for i in range(self.NDMASEM)]
            self.dcount[q] = [0] * self.NDMASEM
            self.drr[q] = 0
            self.dlast[q] = [None] * self.NDMASEM
        self.ops = {e: [] for e in ENGS}
        self.lastw = {}
        self.readers = {}
        self.waited = {e: {} for e in ENGS}
        self.n_inst = 0
        self._cur_barrier = None

    def op(self, eng, emit, reads=(), writes=(), dma=False):
        o = _Op(eng, emit, dma)
        deps = []
        psr = [k for k in reads if isinstance(k, tuple) and k and k[0] == "ps"]
        if psr:
            writes = list(writes) + [k for k in psr if k not in writes]
        for k in reads:
            w = self.lastw.get(k)
            if w is not None:
                deps.append(w)
        for k in writes:
            w = self.lastw.get(k)
            if w is not None:
                deps.append(w)
            deps.extend(self.readers.get(k, ()))
        if dma:
            q = eng
            i = self.drr[q]
            self.drr[q] = (i + 1) % self.NDMASEM
            prev = self.dlast[q][i]
            if prev is not None:
                deps.append(prev)
            self.dcount[q][i] += 16
            o.sem = self.dsem[q][i]
            o.semval = self.dcount[q][i]
            o.need_inc = True
            self.dlast[q][i] = o
        seen = set()
        for d in deps:
            if d is o or id(d) in seen:
                continue
            seen.add(id(d))
            if (not d.is_dma) and d.eng == eng and (eng == "pe" or not self.same_eng_sync):
                continue
            o.deps.append(d)
            d.need_inc = True
        for k in writes:
            self.lastw[k] = o
            self.readers[k] = []
        for k in reads:
            self.readers.setdefault(k, []).append(o)
        self.ops[eng].append(o)
        return o

    def stage_end(self):
        lasts = []
        for e in ENGS:
            if self.ops[e]:
                lasts.append(self.ops[e][-1])
        for q in self.dlast:
            for d in self.dlast[q]:
                if d is not None:
                    lasts.append(d)
        for o in lasts:
            o.need_inc = True
        self.flush()
        cur = self._cur_barrier or {}
        self._cur_barrier = {e: list(lasts) + list(cur.get(e) or []) for e in ENGS}
        self.lastw.clear()
        self.readers.clear()

    def flush(self):
        nc = self.nc
        for e in ENGS:
            for o in self.ops[e]:
                if o.is_dma:
                    continue
                if o.need_inc and o.sem is None:
                    self.ecount[e] += 1
                    o.sem = self.esem[e]
                    o.semval = self.ecount[e]
        pend = self._cur_barrier
        engmap = {"pe": "tensor", "act": "scalar", "dve": "vector", "pool": "gpsimd", "sp": "sync"}
        if not any(self.ops[e] for e in ENGS):
            return
        with nc.Block() as block:
            for e in ENGS:
                ops = self.ops[e]
                if not ops:
                    continue

                def body(engine, ops=ops, e=e):
                    waited = self.waited[e]
                    first = True
                    for o in ops:
                        deps = o.deps
                        if first and pend and pend.get(e):
                            deps = list(deps) + [d for d in pend[e] if d is not o]
                            pend[e] = None
                        first = False
                        need = {}
                        for d in deps:
                            sid = id(d.sem)
                            if waited.get(sid, 0) >= d.semval:
                                continue
                            if sid not in need or need[sid][1] < d.semval:
                                need[sid] = (d.sem, d.semval)
                        for sid, (s, v) in need.items():
                            engine.wait_ge(s, v)
                            waited[sid] = v
                            self.n_inst += 1
                        ins = o.emit(engine)
                        self.n_inst += 1
                        if o.need_inc:
                            ins.then_inc(o.sem, 16 if o.is_dma else 1)

                getattr(block, engmap[e])(body)
        for e in ENGS:
            self.ops[e] = []

    def final_wait(self, eng="sp"):
        self.stage_end()
        self.op(eng, lambda en: en.nop())
        self.flush()

    def close(self):
        self.stack.close()


def _rope_tables(kind):
    pos = np.arange(T)
    rows, cols = pos // 64, pos % 64
    cos = np.zeros((128, T), np.float32)
    sin = np.zeros((128, T), np.float32)
    perm = np.zeros((128, 128), np.float32)
    if kind == "d":
        blocks = [(0, 32, rows), (32, 32, cols), (64, 32, rows), (96, 32, cols)]
    else:
        blocks = [(0, 64, rows), (64, 64, cols)]
    for base, n, p in blocks:
        half = n // 2
        inv = (10000.0 ** (-np.arange(0, n, 2, dtype=np.float32) / n)).astype(np.float32)
        ang = p.astype(np.float32)[None, :] * inv[:, None]
        c, s = np.cos(ang).astype(np.float32), np.sin(ang).astype(np.float32)
        cos[base:base + half] = c
        cos[base + half:base + n] = c
        sin[base:base + half] = -s
        sin[base + half:base + n] = s
        for i in range(half):
            perm[base + half + i, base + i] = 1.0
            perm[base + i, base + half + i] = 1.0
    return cos, sin, perm


def _na_geometry():
    masks, index, plan = [], {}, {}
    for a in range(16):
        plan[a] = []
        qrow = np.repeat(np.array([2 * a, 2 * a + 1]), 64)
        qcol = np.tile(np.arange(64), 2)
        start = np.clip(qrow - 4, 0, 24)
        winc = np.clip(qcol - 8, 0, 48)
        for t in range(16):
            krow = np.repeat(np.array([2 * t, 2 * t + 1]), 64)
            kcol = np.tile(np.arange(64), 2)
            ok = ((krow[:, None] >= start[None, :]) & (krow[:, None] < start[None, :] + 8)
                  & (kcol[:, None] >= winc[None, :]) & (kcol[:, None] < winc[None, :] + 16))
            if not ok.any():
                continue
            key = (t - a, ok.tobytes())
            if key not in index:
                index[key] = len(masks)
                masks.append((t - a, ok.astype(np.float32)))
            plan[a].append((t, index[key]))
    return masks, plan


_NA_MASKS, _NA_PLAN = _na_geometry()
N_NA_MASK = len(_NA_MASKS)


def _win_masks():
    m = np.zeros((6, 128, 512), np.float32)
    for r in range(6):
        j = (r - 1) * 128 + np.arange(128)[:, None]
        i = np.arange(512)[None, :]
        m[r] = (np.abs(i - j) <= 128).astype(np.float32)
    return m


def _consts():
    c = {}
    c["ident"] = np.eye(128, dtype=np.float32)
    m = np.arange(128)[:, None]
    i = np.arange(128)[None, :]
    c["tri"] = np.stack([(m <= i), (m >= i), (m > i), (m < i)]).astype(np.float32)
    cd, sd, pd = _rope_tables("d")
    cw, sw, pw = _rope_tables("w")
    c["rope_d"] = np.stack([cd, sd])
    c["rope_w"] = np.stack([cw, sw])
    c["perm"] = np.stack([pd, pw])
    c["na_mask"] = np.stack([mk for _, mk in _NA_MASKS])
    c["win_mask"] = _win_masks()
    return c


class Builder:
    def __init__(self, nc, debug=False, layers=DEPTH, ndt=F32):
        self.nc = nc
        self.debug = debug
        self.layers = layers
        self.ndt = ndt
        self.P = Prog(nc)
        self.inp = {}
        self.scr = {}
        self.uid = 0

    def din(self, name, shape, dtype=F32):
        t = self.nc.dram_tensor(name, list(shape), dtype, kind="ExternalInput").ap()
        self.inp[name] = t
        return t

    def dscr(self, name, shape, dtype):
        kind = "ExternalOutput" if self.debug else "Internal"
        t = self.nc.dram_tensor(name, list(shape), dtype, kind=kind).ap()
        self.scr[name] = t
        return t

    def sb(self, es, name, shape, dtype):
        self.uid += 1
        return es.enter_context(self.nc.sbuf_tensor("%s_%d" % (name, self.uid), list(shape), dtype))

    def mm(self, out, lhsT, rhs, start, stop, reads, writes):
        self.P.op("pe", lambda e: e.matmul(out, lhsT=lhsT, rhs=rhs, start=start, stop=stop), reads, writes)

    def act(self, out, in_, func, reads, writes, bias=None, scale=None, accum_out=None):
        kw = {}
        if bias is not None:
            kw["bias"] = bias
        if scale is not None:
            kw["scale"] = scale
        if accum_out is not None:
            kw["accum_out"] = accum_out
        self.P.op("act", lambda e: e.activation(out=out, in_=in_, func=func, **kw), reads, writes)

    def tt(self, eng, out, in0, in1, op, reads, writes):
        self.P.op(eng, lambda e: e.tensor_tensor(out=out, in0=in0, in1=in1, op=op), reads, writes)

    def ts(self, eng, out, in0, s1, s2, op0, op1, reads, writes):
        if op1 is None:
            self.P.op(eng, lambda e: e.tensor_scalar(out=out, in0=in0, scalar1=s1, scalar2=None, op0=op0), reads, writes)
        else:
            self.P.op(eng, lambda e: e.tensor_scalar(out=out, in0=in0, scalar1=s1, scalar2=s2, op0=op0, op1=op1), reads, writes)

    def stt(self, out, in0, scalar, in1, op0, op1, reads, writes):
        self.P.op("dve", lambda e: e.scalar_tensor_tensor(out=out, in0=in0, scalar=scalar, in1=in1, op0=op0, op1=op1), reads, writes)

    def cp(self, eng, out, in_, reads, writes):
        if eng == "act":
            self.P.op("act", lambda e: e.copy(out=out, in_=in_), reads, writes)
        else:
            self.P.op(eng, lambda e: e.tensor_copy(out=out, in_=in_), reads, writes)

    def recip(self, out, in_, reads, writes):
        self.P.op("dve", lambda e: e.reciprocal(out=out, in_=in_), reads, writes)

    def dma(self, q, out, in_, reads, writes, **kw):
        self.P.op(q, lambda e: e.dma_start(out=out, in_=in_, **kw), reads, writes, dma=True)

    def rsqrt(self, out, in_, mult, tmp, tmpkey, reads, writes):
        self.act(tmp, in_, AF.Sqrt, reads, [tmpkey], bias=self.eps_col[:, 0:1], scale=mult)
        self.recip(out, tmp, [tmpkey], writes)

    def declare(self):
        di = self.din
        self.x = di("x", [T, D])
        self.ctx = di("ctx", [C, D])
        self.c_fm = di("c_fm", [128, KC, 2])
        self.ada_w = di("ada_w", [DEPTH, D, 6 * D])
        self.ada_b = di("ada_b_fm", [DEPTH, 128, 96])
        self.nmg = di("nmg_fm", [DEPTH, 128, KC])
        self.nfg = di("nfg_fm", [DEPTH, 128, KC])
        self.fng = di("fng_fm", [128, KC])
        self.w_in_even = di("w_in_even", [2, D, W_EVEN])
        self.w_in_odd = di("w_in_odd", [2, D, W_ODD])
        self.w_out = di("w_out", [DEPTH, D, D])
        self.w_gate = di("ffn_w_gate", [DEPTH, D, FF])
        self.w_up = di("ffn_w_up", [DEPTH, D, FF])
        self.w_down = di("ffn_w_down", [DEPTH, FF, D])
        self.conv_fm = di("conv_fm", [2, 128, 24, 3])
        self.gdn_vec = di("gdn_vec", [2, 128, 32])
        self.gng = di("gng_rep", [2, 128, 128])
        self.lam = di("lam_rep", [2, 128, 4, 64])
        self.subln = di("subln_fm", [2, 128, 1])
        self.na_bias = di("na_bias", [2, 8, 128, N_NA_MASK, 128])
        self.sink = di("sink_rep", [2, 128, 8])
        self.c_ident = di("c_ident", [128, 128])
        self.c_tri = di("c_tri", [4, 128, 128])
        self.c_rope_d = di("c_rope_d", [2, 128, T])
        self.c_rope_w = di("c_rope_w", [2, 128, T])
        self.c_perm = di("c_perm", [2, 128, 128])
        self.c_win = di("c_win_mask", [6, 128, 512])
        self.out = self.nc.dram_tensor("out", [T, D], F32, kind="ExternalOutput").ap()
        ds = self.dscr
        self.hT = ds("hT", [KC, 128, NT], F32)
        self.yT = ds("yT", [KC, 128, NT], BF16)
        self.qT_a = ds("qT_a", [8, 128, NT], BF16)
        self.kT_a = ds("kT_a", [8, 128, NT], BF16)
        self.v_a = ds("v_a", [NT, 1024], BF16)
        self.k_tm = ds("k_tm", [NT, 1024], BF16)
        self.qT_b = ds("qT_b", [8, 128, NT], BF16)
        self.kT_b = ds("kT_b", [8, 128, NT], BF16)
        self.v_b = ds("v_b", [NT, 1024], BF16)
        self.gz = ds("gz", [NT, 1024], BF16)
        self.gab = ds("gab", [NT, 32], F32)
        self.go = ds("go", [2, NT, 1024], F32)

    def setup(self, es):
        nc = self.nc
        sb = lambda n, s, d: self.sb(es, n, s, d)
        self.ps = es.enter_context(nc.psum_tensor("ps", [128, 8, 512], F32))
        self.ident_f = sb("ident_f", [128, 128], F32)
        self.ident_b = sb("ident_b", [128, 128], BF16)
        self.ones_b = sb("ones_b", [128, 128], BF16)
        self.ones_f = sb("ones_f", [128, 128], F32)
        self.eps_col = sb("eps_col", [128, 1], F32)
        self.cact = sb("cact", [128, KC, 2], BF16)
        self.modv_s = [sb("modv", [128, 96, 2], F32) for _ in range(2)]
        self.G1_s = [sb("G1", [128, KC, 2], F32) for _ in range(2)]
        self.G2_s = [sb("G2", [128, KC, 2], F32) for _ in range(2)]
        self.cur = 0
        self.nrm2 = sb("nrm2", [128, 64], F32)
        self.mbias = sb("mbias", [128, 64], F32)
        P = self.P
        with ExitStack() as e1:
            craw = self.sb(e1, "craw", [128, KC, 2], F32)
            self.dma("sp", self.ident_f[:], self.c_ident, [], ["ident_f"])
            self.dma("sp", craw[:], self.c_fm, [], ["craw"])
            self.cp("dve", self.ident_b[:], self.ident_f[:], ["ident_f"], ["ident_b"])
            P.op("pool", lambda e: e.memset(self.ones_b[:], 1.0), [], ["ones_b"])
            P.op("pool", lambda e: e.memset(self.ones_f[:], 1.0), [], ["ones_f"])
            P.op("pool", lambda e: e.memset(self.eps_col[:], EPS), [], ["eps"])
            self.act(self.cact[:], craw[:], AF.Silu, ["craw"], ["cact"])
            P.stage_end()

    @property
    def modv(self):
        return self.modv_s[self.cur]

    @property
    def G1(self):
        return self.G1_s[self.cur]

    @property
    def G2(self):
        return self.G2_s[self.cur]

    def mod_part(self, es, l, part):
        wv = self.ada_w[l].rearrange("(k p) n -> p k n", p=128)
        wbm = [self.sb(es, "wbm", [128, KC, 256], BF16) for _ in range(2)]
        g0, g1 = (0, 24) if part == 0 else (24, 48)

        def load(cg):
            self.dma("pool", wbm[cg % 2][:], wv[:, :, cg * 256:(cg + 1) * 256], [], [("wbm", cg % 2)])
        if part == 1:
            adab = self.sb(es, "adab2", [128, 96], F32)
            nmg = self.sb(es, "nmg2", [128, KC], F32)
            nfg = self.sb(es, "nfg2", [128, KC], F32)
            self.dma("sp", adab[:], self.ada_b[l], [], ["adab2"])
            self.dma("sp", nmg[:], self.nmg[l], [], ["nmg2"])
            self.dma("sp", nfg[:], self.nfg[l], [], ["nfg2"])
        load(g0)
        for cg in range(g0, g1):
            if cg + 1 < g1:
                load(cg + 1)
            for j in range(2):
                ch = cg * 2 + j
                for k in range(KC):
                    self.mm(self.ps[:, 6, ch * 2:ch * 2 + 2], wbm[cg % 2][:, k, j * 128:(j + 1) * 128], self.cact[:, k, :],
                            k == 0, k == KC - 1, [("wbm", cg % 2), "cact"], [("ps", 6)])
            yield
        if part == 1:
            nx = 1 - self.cur
            mv, g1t, g2t = self.modv_s[nx], self.G1_s[nx], self.G2_s[nx]
            self.tt("dve", mv[:], self.ps[:, 6, 0:192].rearrange("p (a b) -> p a b", b=2),
                    adab[:].unsqueeze(2).to_broadcast([128, 96, 2]), ALU.add, [("ps", 6), "adab2"], ["modv_n"])
            self.stt(g1t[:], mv[:, 16:32, :], 1.0, nmg[:].unsqueeze(2).to_broadcast([128, KC, 2]), ALU.add, ALU.mult,
                     ["modv_n", "nmg2"], ["G1n"])
            self.stt(g2t[:], mv[:, 64:80, :], 1.0, nfg[:].unsqueeze(2).to_broadcast([128, KC, 2]), ALU.add, ALU.mult,
                     ["modv_n", "nfg2"], ["G2n"])
        yield

    def stage_load_inputs(self):
        P = self.P
        with ExitStack() as es:
            xin = [self.sb(es, "xin", [128, D], F32) for _ in range(2)]
            hst = [self.sb(es, "hst", [128, KC, 128], F32) for _ in range(2)]
            hv = self.hT.rearrange("k p t -> p k t")
            for m in range(NTILE):
                src = self.x[m * 128:(m + 1) * 128, :] if m < 16 else self.ctx[(m - 16) * 128:(m - 15) * 128, :]
                r = m % 2
                self.dma("sp", xin[r][:], src, [], [("xin", r)])
                for g in range(4):
                    b = (m * 4 + g) % 8
                    for j in range(4):
                        k = g * 4 + j
                        self.mm(self.ps[:, b, j * 128:(j + 1) * 128], xin[r][:, k * 128:(k + 1) * 128], self.ident_f[:],
                                True, True, [("xin", r)], [("ps", b)])
                    dst = hst[r][:, g * 4:(g + 1) * 4, :]
                    srcp = self.ps[:, b, :].rearrange("p (a b) -> p a b", a=4)
                    self.cp("act" if g % 2 else "dve", dst, srcp, [("ps", b)], [("hst", r, g)])
                self.dma("pool", hv[:, :, m * 128:(m + 1) * 128], hst[r][:], [("hst", r, g) for g in range(4)], [("hT", m)])
            P.stage_end()

    def stage_mod(self, l):
        P = self.P
        with ExitStack() as es:
            wb = [self.sb(es, "wb", [128, KC, 512], BF16) for _ in range(2)]
            adab = self.sb(es, "adab", [128, 96], F32)
            nmg = self.sb(es, "nmg", [128, KC], F32)
            nfg = self.sb(es, "nfg", [128, KC], F32)
            self.dma("sp", adab[:], self.ada_b[l], [], ["adab"])
            self.dma("sp", nmg[:], self.nmg[l], [], ["nmg"])
            self.dma("sp", nfg[:], self.nfg[l], [], ["nfg"])
            wv = self.ada_w[l].rearrange("(k p) n -> p k n", p=128)

            def load(cg):
                self.dma("pool", wb[cg % 2][:], wv[:, :, cg * 512:(cg + 1) * 512], [], [("wb", cg % 2)])
            load(0)
            for cg in range(24):
                if cg + 1 < 24:
                    load(cg + 1)
                for j in range(4):
                    ch = cg * 4 + j
                    for k in range(KC):
                        self.mm(self.ps[:, 0, ch * 2:ch * 2 + 2], wb[cg % 2][:, k, j * 128:(j + 1) * 128], self.cact[:, k, :],
                                k == 0, k == KC - 1, [("wb", cg % 2), "cact"], [("ps", 0)])
            self.tt("dve", self.modv[:], self.ps[:, 0, 0:192].rearrange("p (a b) -> p a b", b=2),
                    adab[:].unsqueeze(2).to_broadcast([128, 96, 2]), ALU.add, [("ps", 0), "adab"], ["modv"])
            self.stt(self.G1[:], self.modv[:, 16:32, :], 1.0, nmg[:].unsqueeze(2).to_broadcast([128, KC, 2]), ALU.add, ALU.mult,
                     ["modv", "nmg"], ["G1"])
            self.stt(self.G2[:], self.modv[:, 64:80, :], 1.0, nfg[:].unsqueeze(2).to_broadcast([128, KC, 2]), ALU.add, ALU.mult,
                     ["modv", "nfg"], ["G2"])
            if self.debug:
                dm = self.dscr("dbg_mod%d" % l, [128, 192], F32)
                self.dma("sp", dm, self.modv[:].rearrange("p a b -> p (a b)"), ["modv"], ["dbgm"])
            P.stage_end()

    def norm_tiles(self, es, tiles, which, uT, base, tag):
        G = self.G1 if which == 0 else self.G2
        sh = 0 if which == 0 else 48
        hv = self.hT.rearrange("k p t -> p k t")
        hb = [self.sb(es, "hb", [128, KC, 256], F32) for _ in range(2)]
        sq = [self.sb(es, "sq", [128, KC, 256], BF16) for _ in range(2)]
        rs = [self.sb(es, "rs", [128, 256], F32) for _ in range(2)]
        rt = self.sb(es, "rt", [128, 256], F32)
        sub = []
        for (t0, n) in tiles:
            for o in range(0, n, 256):
                sub.append((t0 + o, min(256, n - o)))
        pend = None
        for i, (t0, n) in enumerate(sub):
            r = i % 2
            s = 1 if t0 >= T else 0
            b = 7
            self.dma("sp", hb[r][:, :, 0:n], hv[:, :, t0:t0 + n], [("hT", "all")], [("hb", r)])
            self.tt("pool", sq[r][:, :, 0:n], hb[r][:, :, 0:n], hb[r][:, :, 0:n], ALU.mult, [("hb", r)], [("sq", r)])
            for k in range(KC):
                self.mm(self.ps[:, b, 0:n], self.ones_b[:], sq[r][:, k, 0:n], k == 0, k == KC - 1, [("sq", r), "ones_b"], [("ps", b)])
            self.rsqrt(rs[r][:, 0:n], self.ps[:, b, 0:n], 1.0 / D, rt[:, 0:n], "rt", [("ps", b), "eps"], [("rs", r)])
            self.tt("dve", hb[r][:, :, 0:n], hb[r][:, :, 0:n], rs[r][:, 0:n].unsqueeze(1).to_broadcast([128, KC, n]), ALU.mult,
                    [("hb", r), ("rs", r)], [("hb", r)])

            def second(r=r, s=s, t0=t0, n=n):
                for k in range(KC):
                    self.act(uT[:, k, t0 - base:t0 - base + n], hb[r][:, k, 0:n], AF.Identity, [("hb", r), "G1", "G2", "modv"],
                             [(tag, k, t0)], scale=G[:, k, s:s + 1], bias=self.modv[:, sh + k, s:s + 1])
            if pend is not None:
                pend()
            pend = second
        if pend is not None:
            pend()

    def resid_update(self, hrow, c, t0, n, ps_ap, gate_chunk_base, rkeys, wkey):
        s = 1 if t0 >= T else 0
        self.stt(hrow[:, t0:t0 + n], ps_ap, self.modv[:, gate_chunk_base + c, s:s + 1], hrow[:, t0:t0 + n], ALU.mult, ALU.add,
                 rkeys + ["modv"], [wkey])

    def stats_row(self, orow, okeys, slot0, split, sqrow, mx, sn="sqrow", presq=False):
        if not presq:
            self.act(sqrow[:], orow[:], AF.Square, okeys, [sn])
        parts = [(0, 64), (64, 128)] if split else [(0, 128)]
        for pi, (p0, p1) in enumerate(parts):
            for ti, (t0, n) in enumerate(TT):
                b = 4 + (self.auxr % 3)
                self.auxr += 1
                self.mm(self.ps[:, b, 0:n], self.ones_b[p0:p1, :], sqrow[p0:p1, t0:t0 + n], True, True, [sn, "ones_b"], [("ps", b)])
                self.P.op("dve", lambda e, b=b, n=n, ti=ti: e.tensor_reduce(out=mx[:, ti:ti + 1], in_=self.ps[:, b, 0:n], axis=AX.X, op=ALU.max),
                          [("ps", b)], [("mx", ti)])
            sl = slot0 + pi
            self.P.op("dve", lambda e, sl=sl: e.tensor_reduce(out=self.nrm2[:, sl:sl + 1], in_=mx[:, 0:5], axis=AX.X, op=ALU.max),
                      [("mx", ti) for ti in range(5)], [("nrm2", sl)])

    def stage_inproj(self, l):
        P = self.P
        even = (l % 2 == 0)
        i2 = l // 2
        w = (self.w_in_even if even else self.w_in_odd)[i2]
        wv = w.rearrange("(k p) n -> p k n", p=128)
        if even:
            jobs = [("fm", "gq", 0, 1024), ("fm", "gk", 1024, 1024), ("fm", "gv", 2048, 1024), ("tm", "z", 3072, 1024),
                    ("tm", "ab", 4096, 32), ("fm", "dq", 4128, 1024), ("fm", "dk", 5152, 1024), ("tm", "dv", 6176, 1024)]
        else:
            jobs = [("fm", "nq", 0, 1024), ("fm", "nk", 1024, 1024), ("tm", "nv", 2048, 1024), ("fm", "wq", 3072, 1024),
                    ("fm", "wk", 4096, 256), ("tm", "wv", 4352, 256)]
        groups = []
        for mode, kind, c0, nc_ in jobs:
            for g0 in range(0, nc_, 512):
                groups.append((mode, kind, c0, g0, min(512, nc_ - g0)))
        with ExitStack() as eo:
            uT = self.sb(eo, "uT", [128, KC, NT], BF16)
            with ExitStack() as es:
                self.norm_tiles(es, TT, 0, uT, 0, "uT")
                P.stage_end()
                if self.debug:
                    du = self.dscr("dbg_uT%d" % l, [128, KC, NT], BF16)
                    self.dma("sp", du, uT[:], [], ["dbgu"])
                    P.stage_end()
            with ExitStack() as es:
                sb = lambda n, s, d: self.sb(es, n, s, d)
                wb = [sb("wb", [128, KC, 512], BF16) for _ in range(2)]
                xrows = [sb("xrow", [128, NT], F32) for _ in range(2)]
                yas = [sb("ya", [128, NT], F32) for _ in range(2)]
                orow0s = [sb("orow0", [128, NT], BF16) for _ in range(2)]
                orow = [sb("orow", [128, NT], BF16) for _ in range(2)]
                sqrows = [sb("sqrow", [128, NT], BF16) for _ in range(2)]
                rsr = sb("rsr", [128, 512], F32)
                rst = sb("rst", [128, 512], F32)
                t1 = sb("t1", [128, 512], F32)
                t2 = sb("t2", [128, 512], F32)
                mx = sb("mx", [128, 8], F32)
                tms = [sb("tms", [128, NTILE, 128], BF16) for _ in range(1)]
                gst = [sb("gst", [128, 512], BF16) for _ in range(3)]
                abst = sb("abst", [128, NTILE, 32], F32)
                rope = sb("rope", [128, 2, T], F32)
                permb = sb("permb", [128, 128], BF16)
                permf = sb("permf", [128, 128], F32)
                cw = sb("cw", [128, 24, 3], F32)
                self.auxr = 0
                self.dma("sp", rope[:], (self.c_rope_d if even else self.c_rope_w).rearrange("a p t -> p a t"), [], ["rope"])
                self.dma("sp", permf[:], self.c_perm[0 if even else 1], [], ["permf"])
                self.cp("dve", permb[:], permf[:], ["permf"], ["permb"])
                if even:
                    self.dma("sp", cw[:], self.conv_fm[i2], [], ["cw"])

                def load(gi):
                    mode, kind, c0, g0, gw = groups[gi]
                    self.dma("pool", wb[gi % 2][:, :, 0:gw], wv[:, :, c0 + g0:c0 + g0 + gw], [], [("wb", gi % 2)])

                load(0)
                mainr = 0
                orr = 0
                self.tmr = 0
                gsr = 0
                self.pipe = []
                self.pipe_depth = 1
                for gi, (mode, kind, c0, g0, gw) in enumerate(groups):
                    if gi + 1 < len(groups):
                        load(gi + 1)
                    wbg = wb[gi % 2]
                    wkey = ("wb", gi % 2)
                    if mode == "tm":
                        if kind == "ab":
                            for m in range(NTILE):
                                b = mainr % 4
                                mainr += 1
                                for k in range(KC):
                                    self.mm(self.ps[:, b, 0:gw], uT[:, k, m * 128:(m + 1) * 128], wbg[:, k, 0:gw], k == 0, k == KC - 1,
                                            [wkey], [("ps", b)])
                                self.cp("act", abst[:, m, :], self.ps[:, b, 0:gw], [("ps", b)], [("abst", m)])
                            self.dma("pool", self.gab.rearrange("(t p) c -> p t c", p=128), abst[:], [("abst", m) for m in range(NTILE)], ["gab"])
                            continue
                        dst = {"z": self.gz, "dv": self.v_b, "nv": self.v_a, "wv": self.v_b}[kind]
                        for m in range(NTILE):
                            b = mainr % 4
                            mainr += 1
                            for k in range(KC):
                                self.mm(self.ps[:, b, 0:gw], uT[:, k, m * 128:(m + 1) * 128], wbg[:, k, 0:gw], k == 0, k == KC - 1,
                                        [wkey], [("ps", b)])
                            r = gsr % 3
                            gsr += 1
                            self.cp("act" if m % 2 else "dve", gst[r][:, 0:gw], self.ps[:, b, 0:gw], [("ps", b)], [("gst", r)])
                            self.dma("pool", dst[m * 128:(m + 1) * 128, g0:g0 + gw], gst[r][:, 0:gw], [("gst", r)], [(kind, m, g0)])
                        continue
                    for j in range(gw // 128):
                        ch = (g0 // 128) + j
                        o_r = orr % 2
                        orr += 1
                        ob = orow[o_r]
                        okey = ("orow", o_r)
                        xrow, ya, sqrow, ob0 = xrows[o_r], yas[o_r], sqrows[o_r], orow0s[o_r]
                        xn, yn, sn, o0n = ("xrow", o_r), ("ya", o_r), ("sqrow", o_r), ("orow0", o_r)
                        plain = kind in ("nq", "nk")
                        roped = kind in ("dq", "dk", "wq", "wk")
                        gdn = kind in ("gq", "gk", "gv")
                        for ti, (t0, n) in enumerate(TT):
                            b = mainr % 4
                            mainr += 1
                            for k in range(KC):
                                self.mm(self.ps[:, b, 0:n], wbg[:, k, j * 128:(j + 1) * 128], uT[:, k, t0:t0 + n], k == 0, k == KC - 1,
                                        [wkey], [("ps", b)])
                            if plain or (roped and t0 >= T):
                                self.cp("act", ob[:, t0:t0 + n], self.ps[:, b, 0:n], [("ps", b)], [(okey, ti)])
                            elif roped:
                                self.cp("act", ob0[:, t0:t0 + n], self.ps[:, b, 0:n], [("ps", b)], [(o0n, ti)])
                            else:
                                self.cp("act", xrow[:, t0:t0 + n], self.ps[:, b, 0:n], [("ps", b)], [(xn, ti)])

                        okeys_now = [(okey, ti) for ti in range(5)]
                        sqst = sqrows[o_r]
                        if gdn:
                            cch = {"gq": 0, "gk": 8, "gv": 16}[kind] + ch
                            xk = [(xn, ti) for ti in range(5)]
                            self.ts("dve", ya[:], xrow[:], cw[:, cch, 1:2], None, ALU.mult, None, xk + ["cw"], [yn])
                            for (s0, s1) in ((0, T), (T, NT)):
                                self.stt(ya[:, s0 + 1:s1], xrow[:, s0:s1 - 1], cw[:, cch, 0:1], ya[:, s0 + 1:s1], ALU.mult, ALU.add,
                                         xk + ["cw", yn], [yn])
                                self.stt(ya[:, s0:s1 - 1], xrow[:, s0 + 1:s1], cw[:, cch, 2:3], ya[:, s0:s1 - 1], ALU.mult, ALU.add,
                                         xk + ["cw", yn], [yn])
                            if kind == "gv":
                                self.act(ob[:], ya[:], AF.Silu, [yn], okeys_now)
                            else:
                                self.act(ya[:], ya[:], AF.Silu, [yn], [yn])
                                self.act(sqrow[:], ya[:], AF.Square, [yn], [sn])
                        elif not roped:
                            self.act(sqrow[:], ob[:], AF.Square, okeys_now, [sn])

                        def epi(kind=kind, ch=ch, ob=ob, okey=okey, xrow=xrow, ya=ya, sqrow=sqrow, ob0=ob0, xn=xn, yn=yn, sn=sn, o0n=o0n,
                                roped=roped, gdn=gdn):
                            okeys = [(okey, ti) for ti in range(5)]
                            if roped:
                                for ti, (t0, n) in enumerate(TT[:4]):
                                    ba = 4 + (self.auxr % 3)
                                    self.auxr += 1
                                    self.mm(self.ps[:, ba, 0:n], permb[:], ob0[:, t0:t0 + n], True, True, [(o0n, ti), "permb"], [("ps", ba)])
                                    self.tt("dve", t1[:, 0:n], ob0[:, t0:t0 + n], rope[:, 0, t0:t0 + n], ALU.mult, [(o0n, ti), "rope"], ["t1"])
                                    self.tt("dve", t2[:, 0:n], self.ps[:, ba, 0:n], rope[:, 1, t0:t0 + n], ALU.mult, [("ps", ba), "rope"], ["t2"])
                                    self.tt("dve", ob[:, t0:t0 + n], t1[:, 0:n], t2[:, 0:n], ALU.add, ["t1", "t2"], [(okey, ti)])
                                self.act(sqrow[:], ob[:], AF.Square, okeys, [sn])
                            if kind in ("gq", "gk"):
                                for ti, (t0, n) in enumerate(TT):
                                    ba = 4 + (self.auxr % 3)
                                    self.auxr += 1
                                    self.mm(self.ps[:, ba, 0:n], self.ones_b[:], sqrow[:, t0:t0 + n], True, True, [sn, "ones_b"], [("ps", ba)])
                                    self.rsqrt(rsr[:, 0:n], self.ps[:, ba, 0:n], 1.0, rst[:, 0:n], "rst", [("ps", ba), "eps"], ["rsr"])
                                    sc = (128.0 ** -0.5) if kind == "gq" else 1.0
                                    self.stt(ob[:, t0:t0 + n], ya[:, t0:t0 + n], sc, rsr[:, 0:n], ALU.mult, ALU.mult, [yn, "rsr"], [(okey, ti)])
                            if kind in ("dq", "dk"):
                                self.stats_row(ob, okeys, (0 if kind == "dq" else 16) + 2 * ch, True, sqrow, mx, sn, presq=True)
                            elif kind in ("nq", "nk"):
                                self.stats_row(ob, okeys, (0 if kind == "nq" else 8) + ch, False, sqrow, mx, sn, presq=True)
                            elif kind in ("wq", "wk"):
                                self.stats_row(ob, okeys, (16 if kind == "wq" else 24) + ch, False, sqrow, mx, sn, presq=True)
                            fm_dst = {"gq": self.qT_a, "gk": self.kT_a, "dq": self.qT_b, "dk": self.kT_b, "nq": self.qT_a, "nk": self.kT_a,
                                      "wq": self.qT_b, "wk": self.kT_b}.get(kind)
                            if fm_dst is not None:
                                self.dma("pool", fm_dst[ch], ob[:], okeys, [(kind, "fm", ch)])
                            if kind in ("gk", "gv"):
                                tr = 0
                                for m4 in range(0, NTILE, 4):
                                    ba = 4 + (self.auxr % 3)
                                    self.auxr += 1
                                    nm = min(4, NTILE - m4)
                                    for mm_ in range(nm):
                                        m = m4 + mm_
                                        self.mm(self.ps[:, ba, mm_ * 128:(mm_ + 1) * 128], ob[:, m * 128:(m + 1) * 128], self.ident_b[:], True, True,
                                                okeys + ["ident_b"], [("ps", ba)])
                                    self.cp("act" if (m4 // 4) % 2 else "dve", tms[tr][:, m4:m4 + nm, :],
                                            self.ps[:, ba, 0:nm * 128].rearrange("p (a b) -> p a b", b=128), [("ps", ba)], [("tms", tr, m4)])
                                tdst = self.k_tm if kind == "gk" else self.v_a
                                self.dma("pool", tdst[:, ch * 128:(ch + 1) * 128].rearrange("(t p) d -> p t d", p=128), tms[tr][:],
                                         [("tms", tr, m4) for m4 in range(0, NTILE, 4)], [(kind, "tm", ch)])
                        self.pipe_push(epi)
                self.pipe_flush()
                P.stage_end()

    def stage_outproj(self, l, tiles):
        P = self.P
        wv = self.w_out[l].rearrange("(k p) n -> p k n", p=128)
        hv = self.hT
        with ExitStack() as es:
            sb = lambda n, s, d: self.sb(es, n, s, d)
            yT = sb("yTs", [128, KC, NT], BF16)
            wb = [sb("wb", [128, KC, 512], BF16) for _ in range(2)]
            hrow = [sb("hrow", [128, NT], F32) for _ in range(3)]
            for k in range(KC):
                self.dma("sp", yT[:, k, :], self.yT[k], [], [("yT", k)])

            def load(g):
                self.dma("pool", wb[g % 2][:], wv[:, :, g * 512:(g + 1) * 512], [], [("wb", g % 2)])
            load(0)
            tmax = max(t0 + n for t0, n in tiles)
            mainr = 0
            for g in range(4):
                if g + 1 < 4:
                    load(g + 1)
                for j in range(4):
                    c = g * 4 + j
                    hr = c % 3
                    self.dma("sp", hrow[hr][:, 0:tmax], hv[c][:, 0:tmax], [], [("hrow", hr)])
                    for (t0, n) in tiles:
                        b = mainr % 8
                        mainr += 1
                        for k in range(KC):
                            self.mm(self.ps[:, b, 0:n], wb[g % 2][:, k, j * 128:(j + 1) * 128], yT[:, k, t0:t0 + n], k == 0, k == KC - 1,
                                    [("wb", g % 2), ("yT", k)], [("ps", b)])
                        self.resid_update(hrow[hr], c, t0, n, self.ps[:, b, 0:n], 32, [("ps", b), ("hrow", hr)], ("hrow", hr))
                    self.dma("pool", hv[c][:, 0:tmax], hrow[hr][:, 0:tmax], [("hrow", hr)], [("hTc", c)])
            P.stage_end()

    def stage_ffn(self, l, tiles, bg_mod=None):
        P = self.P
        wg = self.w_gate[l].rearrange("(k p) n -> p k n", p=128)
        wu = self.w_up[l].rearrange("(k p) n -> p k n", p=128)
        wd = self.w_down[l].rearrange("(f p) n -> p f n", p=128)
        hv = self.hT
        halves = [tiles[:2], tiles[2:]]
        for hi, half in enumerate(halves):
            if not half:
                continue
            base = half[0][0]
            ntok = sum(n for _, n in half)
            with ExitStack() as eo:
                actT = self.sb(eo, "actT", [128, FC, ntok], BF16)
                with ExitStack() as es:
                    sb = lambda n, s, d: self.sb(es, n, s, d)
                    u2 = sb("u2", [128, KC, ntok], BF16)
                    with ExitStack() as en:
                        self.norm_tiles(en, half, 1, u2, base, "u2")
                        P.stage_end()
                    wgb = [sb("wgb", [128, KC, 256], BF16) for _ in range(2)]
                    wub = [sb("wub", [128, KC, 256], BF16) for _ in range(2)]
                    sg = sb("sg", [128, 512], F32)
                    gen = self.mod_part(es, bg_mod, hi) if bg_mod is not None else None

                    def load(g):
                        self.dma("pool", wgb[g % 2][:], wg[:, :, g * 256:(g + 1) * 256], [], [("wgb", g % 2)])
                        self.dma("pool", wub[g % 2][:], wu[:, :, g * 256:(g + 1) * 256], [], [("wub", g % 2)])
                    load(0)
                    r = 0
                    for g in range(FC // 2):
                        if g + 1 < FC // 2:
                            load(g + 1)
                        for j in range(2):
                            f = g * 2 + j
                            for (t0, n) in half:
                                bg = (r % 3) * 2
                                bu = bg + 1
                                r += 1
                                for k in range(KC):
                                    self.mm(self.ps[:, bg, 0:n], wgb[g % 2][:, k, j * 128:(j + 1) * 128], u2[:, k, t0 - base:t0 - base + n],
                                            k == 0, k == KC - 1, [("wgb", g % 2)], [("ps", bg)])
                                for k in range(KC):
                                    self.mm(self.ps[:, bu, 0:n], wub[g % 2][:, k, j * 128:(j + 1) * 128], u2[:, k, t0 - base:t0 - base + n],
                                            k == 0, k == KC - 1, [("wub", g % 2)], [("ps", bu)])
                                self.act(sg[:, 0:n], self.ps[:, bg, 0:n], AF.Silu, [("ps", bg)], ["sg"])
                                self.tt("dve", actT[:, f, t0 - base:t0 - base + n], sg[:, 0:n], self.ps[:, bu, 0:n], ALU.mult,
                                        ["sg", ("ps", bu)], [("actT", f, t0)])
                        if gen is not None:
                            next(gen, None)
                    if gen is not None:
                        for _ in gen:
                            pass
                    P.stage_end()
                with ExitStack() as es:
                    sb = lambda n, s, d: self.sb(es, n, s, d)
                    wdb = [sb("wdb", [128, FC, 256], BF16) for _ in range(2)]
                    hrow = [sb("hrow", [128, ntok], F32) for _ in range(3)]

                    def loadd(g):
                        self.dma("pool", wdb[g % 2][:], wd[:, :, g * 256:(g + 1) * 256], [], [("wdb", g % 2)])
                    loadd(0)
                    r = 0
                    for g in range(8):
                        if g + 1 < 8:
                            loadd(g + 1)
                        for j in range(2):
                            c = g * 2 + j
                            hr = c % 3
                            self.dma("sp", hrow[hr][:], hv[c][:, base:base + ntok], [], [("hrow", hr)])
                            for (t0, n) in half:
                                b = r % 6
                                r += 1
                                for f in range(FC):
                                    self.mm(self.ps[:, b, 0:n], wdb[g % 2][:, f, j * 128:(j + 1) * 128], actT[:, f, t0 - base:t0 - base + n],
                                            f == 0, f == FC - 1, [("wdb", g % 2)], [("ps", b)])
                                s = 1 if t0 >= T else 0
                                self.stt(hrow[hr][:, t0 - base:t0 - base + n], self.ps[:, b, 0:n], self.modv[:, 80 + c, s:s + 1],
                                         hrow[hr][:, t0 - base:t0 - base + n], ALU.mult, ALU.add, [("ps", b), ("hrow", hr)], [("hrow", hr)])
                            self.dma("pool", hv[c][:, base:base + ntok], hrow[hr][:], [("hrow", hr)], [("hTc", c)])
                    P.stage_end()

    def stage_final(self):
        P = self.P
        hv = self.hT.rearrange("k p t -> p k t")
        with ExitStack() as es:
            sb = lambda n, s, d: self.sb(es, n, s, d)
            hb = [sb("hb", [128, KC, 512], F32) for _ in range(2)]
            sq = sb("sq", [128, KC, 512], BF16)
            rs = sb("rs", [128, 512], F32)
            rt = sb("rt", [128, 512], F32)
            fg = sb("fg", [128, KC], F32)
            ot = [sb("ot", [128, D], F32) for _ in range(2)]
            self.dma("sp", fg[:], self.fng, [], ["fg"])
            orr = 0
            self.fbank = 0
            for i, (t0, n) in enumerate(TT[:4]):
                r = i % 2
                self.dma("sp", hb[r][:], hv[:, :, t0:t0 + n], [], [("hb", r)])
                self.act(sq[:], hb[r][:], AF.Square, [("hb", r)], ["sq"])
                for k in range(KC):
                    self.mm(self.ps[:, 7, 0:n], self.ones_b[:], sq[:, k, :], k == 0, k == KC - 1, ["sq", "ones_b"], [("ps", 7)])
                self.rsqrt(rs[:], self.ps[:, 7, 0:n], 1.0 / D, rt[:], "rt", [("ps", 7), "eps"], ["rs"])
                self.tt("dve", hb[r][:], hb[r][:], rs[:].unsqueeze(1).to_broadcast([128, KC, n]), ALU.mult, [("hb", r), "rs"], [("hb", r)])
                self.tt("dve", hb[r][:], hb[r][:], fg[:].unsqueeze(2).to_broadcast([128, KC, n]), ALU.mult, [("hb", r), "fg"], [("hb", r)])
                for m in range(n // 128):
                    o = orr % 2
                    orr += 1
                    for g in range(4):
                        b = self.fbank % 7
                        self.fbank += 1
                        for j in range(4):
                            k = g * 4 + j
                            self.mm(self.ps[:, b, j * 128:(j + 1) * 128], hb[r][:, k, m * 128:(m + 1) * 128], self.ident_f[:], True, True,
                                    [("hb", r), "ident_f"], [("ps", b)])
                        self.cp("act" if g % 2 else "dve", ot[o][:, g * 512:(g + 1) * 512], self.ps[:, b, :], [("ps", b)], [("ot", o, g)])
                    tok = t0 + m * 128
                    self.dma("pool", self.out[tok:tok + 128, :], ot[o][:], [("ot", o, g) for g in range(4)], [("out", tok)])
            P.stage_end()

    def pipe_push(self, fn):
        self.pipe.append(fn)
        while len(self.pipe) > self.pipe_depth:
            self.pipe.pop(0)()

    def pipe_flush(self):
        while self.pipe:
            self.pipe.pop(0)()

    def attn_keys(self, pT, q_ap, n, ktiles, bias_ap, scale, slot, qkeys):
        last = len(ktiles) - 1
        for idx, (k_ap, v_ap, mask_ap, rk) in enumerate(ktiles):
            bs = self.sr % 3
            self.sr += 1
            self.mm(self.ps[:, bs, 0:n], k_ap, q_ap, True, True, qkeys + rk, [("ps", bs)])
            pr = self.pr % 4
            self.pr += 1
            self.act(pT[pr][:, 0:n], self.ps[:, bs, 0:n], AF.Exp, [("ps", bs), "mbias"], [("pT", pr)], bias=bias_ap, scale=scale)
            if mask_ap is not None:
                self.tt("dve", pT[pr][:, 0:n], pT[pr][:, 0:n], mask_ap, ALU.mult, [("pT", pr), "mask"], [("pT", pr)])

            def second(idx=idx, pr=pr, v_ap=v_ap, rk=rk):
                self.mm(self.ps[:, 3 + slot, 0:n], v_ap, pT[pr][:, 0:n], idx == 0, idx == last, [("pT", pr)] + rk, [("ps", 3 + slot)])
                self.mm(self.ps[:, 5 + slot, 0:n], self.ones_b[:], pT[pr][:, 0:n], idx == 0, idx == last, [("pT", pr)], [("ps", 5 + slot)])
            self.pipe_push(second)

    def score_bounds(self, es, pairs, scale):
        tmp = self.sb(es, "mtmp", [128, 64], F32)
        for slot, qs, ks in pairs:
            self.tt("dve", tmp[:, slot:slot + 1], self.nrm2[:, qs:qs + 1], self.nrm2[:, ks:ks + 1], ALU.mult, [], [("mtmp", slot)])
            self.act(tmp[:, slot:slot + 1], tmp[:, slot:slot + 1], AF.Sqrt, [("mtmp", slot)], [("mtmp", slot)])
            self.ts("dve", self.mbias[:, slot:slot + 1], tmp[:, slot:slot + 1], -scale, None, ALU.mult, None, [("mtmp", slot)], ["mbias"])

    def stage_diff(self, l, need_ctx):
        P = self.P
        i2 = l // 2
        lam_init = 0.8 - 0.6 * math.exp(-0.3 * l)
        scale = 64.0 ** -0.5
        with ExitStack() as es:
            sb = lambda n, s, d: self.sb(es, n, s, d)
            qT = [sb("qT", [128, NT], BF16) for _ in range(2)]
            qz = [[sb("qz", [128, NT], BF16) for _ in range(2)] for _ in range(2)]
            kT = [sb("kT", [128, NT], BF16) for _ in range(2)]
            V = [sb("V", [128, NTILE, 128], BF16) for _ in range(2)]
            pT = [sb("pT", [128, 512], BF16) for _ in range(4)]
            ybuf = [sb("ybuf", [128, NT], BF16) for _ in range(2)]
            r0 = sb("r0", [128, 512], F32)
            o0 = sb("o0", [128, 512], F32)
            r1 = sb("r1", [128, 512], F32)
            o1 = sb("o1", [128, 512], F32)
            od = sb("od", [128, 512], F32)
            sq = sb("sqd", [128, 512], BF16)
            rs = sb("rsd", [128, 512], F32)
            rt = sb("rtd", [128, 512], F32)
            lamv = sb("lamv", [128, 4, 64], F32)
            lp = sb("lp", [128, 2, 64], F32)
            le = sb("le", [128, 2], F32)
            lamc = sb("lamc", [128, 1], F32)
            sgc = sb("sgc", [128, 1], F32)
            self.sr = 0
            self.pr = 0
            self.pipe = []
            self.pipe_depth = 2
            self.score_bounds(es, [(h * 2 + c, h * 2 + c, 16 + h * 2 + c) for h in range(8) for c in range(2)], scale)
            self.dma("sp", lamv[:], self.lam[i2], [], ["lamv"])
            self.dma("sp", sgc[:], self.subln[i2], [], ["sgc"])
            self.tt("dve", lp[:, 0, :], lamv[:, 0, :], lamv[:, 1, :], ALU.mult, ["lamv"], ["lp"])
            self.tt("dve", lp[:, 1, :], lamv[:, 2, :], lamv[:, 3, :], ALU.mult, ["lamv", "lp"], ["lp"])
            P.op("dve", lambda e: e.tensor_reduce(out=le[:], in_=lp[:], axis=AX.X, op=ALU.add), ["lp"], ["le"])
            self.act(le[:], le[:], AF.Exp, ["le"], ["le"])
            self.tt("dve", lamc[:], le[:, 0:1], le[:, 1:2], ALU.subtract, ["le"], ["lamc"])
            self.ts("dve", lamc[:], lamc[:], lam_init, None, ALU.add, None, ["lamc"], ["lamc"])
            self.ts("dve", sgc[:], sgc[:], 1.0 - lam_init, None, ALU.mult, None, ["sgc"], ["sgc"])
            for c in range(2):
                for r in range(2):
                    P.op("pool", lambda e, c=c, r=r: e.memset(qz[r][c][:], 0.0), [], [("qz", r, c)])
            qtiles = TT if need_ctx else TT[:4]
            for h in range(8):
                r = h % 2
                self.dma("sp", qT[r][:], self.qT_b[h], [], [("qT", r)])
                self.dma("sp", kT[r][:], self.kT_b[h], [], [("kT", r)])
                self.dma("sp", V[r][:], self.v_b[:, h * 128:(h + 1) * 128].rearrange("(t p) d -> p t d", p=128), [], [("V", r)])
                for c in range(2):
                    self.cp("pool", qz[r][c][c * 64:(c + 1) * 64, :], qT[r][c * 64:(c + 1) * 64, :], [("qT", r)], [("qz", r, c)])
                for (t0, n) in qtiles:
                    kts = list(range(NTILE)) if t0 < T else [16, 17]
                    for c in range(2):
                        kl = [(kT[r][:, m * 128:(m + 1) * 128], V[r][:, m, :], None, [("kT", r), ("V", r)]) for m in kts]
                        self.attn_keys(pT, qz[r][c][:, t0:t0 + n], n, kl, self.mbias[:, h * 2 + c:h * 2 + c + 1], scale, c, [("qz", r, c)])

                    def epi(t0=t0, n=n, r=r):
                        self.recip(r0[:, 0:n], self.ps[:, 5, 0:n], [("ps", 5)], ["r0"])
                        self.tt("dve", o0[:, 0:n], self.ps[:, 3, 0:n], r0[:, 0:n], ALU.mult, [("ps", 3), "r0"], ["o0"])
                        self.recip(r1[:, 0:n], self.ps[:, 6, 0:n], [("ps", 6)], ["r1"])
                        self.ts("dve", r1[:, 0:n], r1[:, 0:n], lamc[:, 0:1], None, ALU.mult, None, ["r1", "lamc"], ["r1"])
                        self.tt("dve", o1[:, 0:n], self.ps[:, 4, 0:n], r1[:, 0:n], ALU.mult, [("ps", 4), "r1"], ["o1"])
                        self.tt("dve", od[:, 0:n], o0[:, 0:n], o1[:, 0:n], ALU.subtract, ["o0", "o1"], ["od"])
                        self.act(sq[:, 0:n], od[:, 0:n], AF.Square, ["od"], ["sqd"])
                        self.mm(self.ps[:, 7, 0:n], self.ones_b[:], sq[:, 0:n], True, True, ["sqd"], [("ps", 7)])
                        self.rsqrt(rs[:, 0:n], self.ps[:, 7, 0:n], 1.0 / 128, rt[:, 0:n], "rtd", [("ps", 7)], ["rsd"])
                        self.stt(ybuf[r][:, t0:t0 + n], od[:, 0:n], sgc[:, 0:1], rs[:, 0:n], ALU.mult, ALU.mult, ["od", "rsd", "sgc"], [("ybuf", r)])
                    self.pipe_push(epi)
                tmax = NT if need_ctx else T

                def store(h=h, r=r, tmax=tmax):
                    self.dma("pool", self.yT[8 + h][:, 0:tmax], ybuf[r][:, 0:tmax], [("ybuf", r)], [("yT", 8 + h)])
                self.pipe_push(store)
            self.pipe_flush()
            P.stage_end()

    def stage_na(self, l, need_ctx):
        P = self.P
        i2 = l // 2
        scale = 128.0 ** -0.5
        with ExitStack() as es:
            sb = lambda n, s, d: self.sb(es, n, s, d)
            qT = [sb("qT", [128, NT], BF16) for _ in range(2)]
            kT = [sb("kT", [128, NT], BF16) for _ in range(2)]
            V = [sb("V", [128, NTILE, 128], BF16) for _ in range(2)]
            Bf = sb("Bf", [128, N_NA_MASK, 128], F32)
            E = [sb("E", [128, N_NA_MASK, 128], BF16) for _ in range(2)]
            pT = [sb("pT", [128, 512], BF16) for _ in range(4)]
            ybuf = [sb("ybuf", [128, NT], BF16) for _ in range(2)]
            rr = sb("rr", [128, 512], F32)
            self.sr = 0
            self.pr = 0
            self.pipe = []
            self.pipe_depth = 2
            self.score_bounds(es, [(h, h, 8 + h) for h in range(8)], scale)
            for h in range(8):
                r = h % 2
                self.dma("sp", qT[r][:], self.qT_a[h], [], [("qT", r)])
                self.dma("sp", kT[r][:], self.kT_a[h], [], [("kT", r)])
                self.dma("sp", V[r][:], self.v_a[:, h * 128:(h + 1) * 128].rearrange("(t p) d -> p t d", p=128), [], [("V", r)])
                self.dma("sp", Bf[:], self.na_bias[i2, h], [], ["Bf"])
                self.act(E[r][:], Bf[:], AF.Exp, ["Bf"], [("E", r)])
                blocks = [(a * 128, 128, a) for a in range(16)]
                if need_ctx:
                    blocks.append((T, C, None))
                for bi, (t0, n, a) in enumerate(blocks):
                    rk = [("kT", r), ("V", r)]
                    if a is None:
                        kl = [(kT[r][:, m * 128:(m + 1) * 128], V[r][:, m, :], None, rk) for m in (16, 17)]
                    else:
                        kl = [(kT[r][:, t * 128:(t + 1) * 128], V[r][:, t, :], E[r][:, mid, :], rk + [("E", r)]) for (t, mid) in _NA_PLAN[a]]
                        kl += [(kT[r][:, m * 128:(m + 1) * 128], V[r][:, m, :], None, rk) for m in (16, 17)]
                    slot = bi % 2
                    self.attn_keys(pT, qT[r][:, t0:t0 + n], n, kl, self.mbias[:, h:h + 1], scale, slot, [("qT", r)])

                    def epi(t0=t0, n=n, r=r, slot=slot):
                        self.recip(rr[:, 0:n], self.ps[:, 5 + slot, 0:n], [("ps", 5 + slot)], ["rr"])
                        self.tt("dve", ybuf[r][:, t0:t0 + n], self.ps[:, 3 + slot, 0:n], rr[:, 0:n], ALU.mult, [("ps", 3 + slot), "rr"], [("ybuf", r)])
                    self.pipe_push(epi)
                tmax = NT if need_ctx else T

                def store(h=h, r=r, tmax=tmax):
                    self.dma("pool", self.yT[h][:, 0:tmax], ybuf[r][:, 0:tmax], [("ybuf", r)], [("yT", h)])
                self.pipe_push(store)
            self.pipe_flush()
            P.stage_end()

    def stage_win(self, l, need_ctx):
        P = self.P
        i2 = l // 2
        scale = 128.0 ** -0.5
        with ExitStack() as es:
            sb = lambda n, s, d: self.sb(es, n, s, d)
            qT = [sb("qT", [128, NT], BF16) for _ in range(2)]
            kT = [sb("kT", [128, NT], BF16) for _ in range(2)]
            V = [sb("V", [128, NTILE, 128], BF16) for _ in range(2)]
            wmf = sb("wmf", [128, 6, 512], F32)
            wm = sb("wm", [128, 6, 512], BF16)
            pT = [sb("pT", [128, 512], BF16) for _ in range(4)]
            ybuf = [sb("ybuf", [128, NT], BF16) for _ in range(2)]
            rr = sb("rr", [128, 512], F32)
            sk = sb("sk", [128, 8], F32)
            esk = sb("esk", [128, 8], F32)
            self.sr = 0
            self.pr = 0
            self.pipe = []
            self.pipe_depth = 2
            self.score_bounds(es, [(16 + h, 16 + h, 24 + h // 4) for h in range(8)], scale)
            self.dma("sp", wmf[:], self.c_win.rearrange("r p q -> p r q"), [], ["wmf"])
            self.cp("dve", wm[:], wmf[:], ["wmf"], ["mask"])
            self.dma("sp", sk[:], self.sink[i2], [], ["sk"])
            for h in range(8):
                r = h % 2
                kv = h // 4
                self.dma("sp", qT[r][:], self.qT_b[h], [], [("qT", r)])
                self.dma("sp", kT[r][:], self.kT_b[kv], [], [("kT", r)])
                self.dma("sp", V[r][:], self.v_b[:, kv * 128:(kv + 1) * 128].rearrange("(t p) d -> p t d", p=128), [], [("V", r)])
                self.act(esk[:, h:h + 1], sk[:, h:h + 1], AF.Exp, ["sk", "mbias"], [("esk", h)], bias=self.mbias[:, 16 + h:17 + h])
                blocks = [(b4 * 512, 512, b4) for b4 in range(4)]
                if need_ctx:
                    blocks.append((T, C, None))
                for bi, (t0, n, b4) in enumerate(blocks):
                    rk = [("kT", r), ("V", r)]
                    kl = []
                    if b4 is not None:
                        for t in range(4 * b4 - 1, 4 * b4 + 5):
                            if 0 <= t < 16:
                                kl.append((kT[r][:, t * 128:(t + 1) * 128], V[r][:, t, :], wm[:, t - 4 * b4 + 1, :], rk))
                    kl += [(kT[r][:, m * 128:(m + 1) * 128], V[r][:, m, :], None, rk) for m in (16, 17)]
                    slot = bi % 2
                    self.attn_keys(pT, qT[r][:, t0:t0 + n], n, kl, self.mbias[:, 16 + h:17 + h], scale, slot, [("qT", r)])

                    def epi(t0=t0, n=n, r=r, slot=slot, h=h):
                        self.ts("dve", rr[:, 0:n], self.ps[:, 5 + slot, 0:n], esk[:, h:h + 1], None, ALU.add, None, [("ps", 5 + slot), ("esk", h)], ["rr"])
                        self.recip(rr[:, 0:n], rr[:, 0:n], ["rr"], ["rr"])
                        self.tt("dve", ybuf[r][:, t0:t0 + n], self.ps[:, 3 + slot, 0:n], rr[:, 0:n], ALU.mult, [("ps", 3 + slot), "rr"], [("ybuf", r)])
                    self.pipe_push(epi)
                tmax = NT if need_ctx else T

                def store(h=h, r=r, tmax=tmax):
                    self.dma("pool", self.yT[8 + h][:, 0:tmax], ybuf[r][:, 0:tmax], [("ybuf", r)], [("yT", 8 + h)])
                self.pipe_push(store)
            self.pipe_flush()
            P.stage_end()

    def stage_gdn(self, l):
        P = self.P
        i2 = l // 2
        NDT = self.ndt
        orders = [[16, 17] + list(range(16)), [17, 16] + list(range(15, -1, -1))]
        with ExitStack() as es:
            sb = lambda n, s, d: self.sb(es, n, s, d)
            tri = sb("tri", [128, 4, 128], F32)
            trin = sb("trin", [128, 4, 128], NDT) if NDT != F32 else tri
            identn = self.ident_f if NDT == F32 else self.ident_b
            ab = sb("ab", [128, NTILE, 32], F32)
            gvec = sb("gvec", [128, 32], F32)
            negA = sb("negA", [128, 16], F32)
            gsb = sb("gsb", [128, NTILE, 16], F32)
            bsb = sb("bsb", [128, NTILE, 16], F32)
            nbs = sb("nbs", [128, NTILE, 16], F32)
            tot = sb("tot", [128, NTILE, 16], F32)
            glast = sb("glast", [128, NTILE, 16], F32)
            gc = sb("gc", [128, NTILE, 16], F32)
            eg = sb("eg", [128, NTILE, 16], F32)
            kd = sb("kd", [128, NTILE, 16], F32)
            bg = sb("bg", [128, NTILE, 16], F32)
            S = [[sb("S", [128, 4, 128], F32) for _ in range(2)] for _ in range(2)]
            Sb = [[sb("Sb", [128, 4, 128], BF16) for _ in range(2)] for _ in range(2)]
            self.dma("sp", tri[:], self.c_tri.rearrange("a p f -> p a f"), [], ["tri"])
            if NDT != F32:
                self.cp("dve", trin[:], tri[:], ["tri"], ["trin"])
            self.dma("sp", ab[:], self.gab.rearrange("(t p) c -> p t c", p=128), [], ["ab"])
            self.dma("sp", gvec[:], self.gdn_vec[i2], [], ["gvec"])
            self.act(negA[:], gvec[:, 0:16], AF.Exp, ["gvec"], ["negA"])
            self.ts("dve", negA[:], negA[:], -1.0, None, ALU.mult, None, ["negA"], ["negA"])
            self.tt("dve", gsb[:], ab[:, :, 0:16], gvec[:, 16:32].unsqueeze(1).to_broadcast([128, NTILE, 16]), ALU.add, ["ab", "gvec"], ["gsb"])
            self.act(gsb[:], gsb[:], AF.Exp, ["gsb"], ["gsb"])
            self.act(gsb[:], gsb[:], AF.Ln, ["gsb"], ["gsb"], bias=1.0, scale=1.0)
            self.tt("dve", gsb[:], gsb[:], negA[:].unsqueeze(1).to_broadcast([128, NTILE, 16]), ALU.mult, ["gsb", "negA"], ["gsb"])
            self.act(bsb[:], ab[:, :, 16:32], AF.Sigmoid, ["ab"], ["bsb"])
            self.ts("dve", nbs[:], bsb[:], -1.0, None, ALU.mult, None, ["bsb"], ["nbs"])
            gflat = gsb[:].rearrange("p t c -> p (t c)")
            self.mm(self.ps[:, 0, 0:NTILE * 16], self.ones_f[:], gflat, True, True, ["gsb", "ones_f"], [("ps", 0)])
            self.cp("dve", tot[:].rearrange("p t c -> p (t c)"), self.ps[:, 0, 0:NTILE * 16], [("ps", 0)], ["tot"])
            self.act(glast[:], tot[:], AF.Exp, ["tot"], ["glast"])
            for d in range(2):
                self.mm(self.ps[:, 1 + d, 0:NTILE * 8], tri[:, d, :], gsb[:, :, d * 8:(d + 1) * 8], True, True, ["gsb", "tri"], [("ps", 1 + d)])
                self.cp("dve", gc[:, :, d * 8:(d + 1) * 8], self.ps[:, 1 + d, 0:NTILE * 8].rearrange("p (t c) -> p t c", c=8), [("ps", 1 + d)], ["gc"])
            self.act(eg[:], gc[:], AF.Exp, ["gc"], ["eg"])
            self.tt("dve", kd[:], tot[:], gc[:], ALU.subtract, ["tot", "gc"], ["kd"])
            self.act(kd[:], kd[:], AF.Exp, ["kd"], ["kd"])
            self.tt("dve", bg[:], bsb[:], eg[:], ALU.mult, ["bsb", "eg"], ["bg"])
            for d in range(2):
                for hg in range(2):
                    P.op("pool", lambda e, d=d, hg=hg: e.memset(S[d][hg][:], 0.0), [], [("S", d, hg)])
                    P.op("pool", lambda e, d=d, hg=hg: e.memset(Sb[d][hg][:], 0.0), [], [("Sb", d, hg)])
            qTt = [[sb("qTt", [128, 8, 128], BF16) for _ in range(2)] for _ in range(2)]
            kTt = [[sb("kTt", [128, 8, 128], BF16) for _ in range(2)] for _ in range(2)]
            ktm = [[sb("ktm", [128, 8, 128], BF16) for _ in range(2)] for _ in range(1)]
            vtm = [[sb("vtm", [128, 8, 128], BF16) for _ in range(2)] for _ in range(1)]
            vb = [[sb("vb", [128, 8, 128], BF16) for _ in range(2)] for _ in range(2)]
            kbg = [[sb("kbg", [128, 8, 128], BF16) for _ in range(2)] for _ in range(2)]
            kdc = [[sb("kdc", [128, 8, 128], BF16) for _ in range(2)] for _ in range(2)]
            Ug = [[sb("Ug", [128, 8, 128], F32) for _ in range(2)] for _ in range(1)]
            def slotbufs(name, dt, nring):
                return [[[sb(name, [128, 4, 128], dt) for _ in range(nring)] for _ in range(2)] for _ in range(2)]
            Eb = slotbufs("Eb", F32, 1)
            ETb = slotbufs("ETb", F32, 1)
            Pb = slotbufs("Pb", NDT, 2)
            PTb = slotbufs("PTb", NDT, 2)
            RTb = slotbufs("RTb", NDT, 2)
            TTb = slotbufs("TTb", BF16, 1)
            aT = slotbufs("aT", BF16, 2)
            ub = slotbufs("ub", F32, 2)
            wT = slotbufs("wT", BF16, 2)
            vn = slotbufs("vn", BF16, 1)
            ot = slotbufs("ot", F32, 1)
            oo = slotbufs("oo", F32, 1)
            qv = self.qT_a.rearrange("h p t -> p h t")
            kv = self.kT_a.rearrange("h p t -> p h t")
            self.bank = 0

            def nb():
                b = self.bank % 8
                self.bank += 1
                return b

            def bc_h(ap2):
                return ap2.unsqueeze(1).to_broadcast([128, 4, 128])

            def bc_e(ap1):
                return ap1.unsqueeze(2).to_broadcast([128, ap1.shape[1], 128])

            def precompute(s):
                sr = s % 2
                for d in range(2):
                    c = orders[d][s]
                    cs = slice(d * 8, (d + 1) * 8)
                    self.tt("dve", Ug[0][d][:], tri[:, d, :].unsqueeze(1).to_broadcast([128, 8, 128]), bc_e(gsb[:, c, cs]), ALU.mult,
                            ["tri", "gsb"], [("Ug", d)])
                for d in range(2):
                    c = orders[d][s]
                    tk = (sr, d)
                    self.dma("sp", qTt[sr][d][:], qv[:, :, c * 128:(c + 1) * 128], [], [("qTt",) + tk])
                    self.dma("sp", kTt[sr][d][:], kv[:, :, c * 128:(c + 1) * 128], [], [("kTt",) + tk])
                    self.dma("sp", ktm[0][d][:].rearrange("p h e -> p (h e)"), self.k_tm[c * 128:(c + 1) * 128, :], [], [("ktm", d)])
                    self.dma("sp", vtm[0][d][:].rearrange("p h e -> p (h e)"), self.v_a[c * 128:(c + 1) * 128, :], [], [("vtm", d)])
                    cs = slice(d * 8, (d + 1) * 8)
                    self.tt("dve", vb[sr][d][:], vtm[0][d][:], bc_e(bsb[:, c, cs]), ALU.mult, [("vtm", d), "bsb"], [("vb",) + tk])
                    self.tt("pool", kbg[sr][d][:], ktm[0][d][:], bc_e(bg[:, c, cs]), ALU.mult, [("ktm", d), "bg"], [("kbg",) + tk])
                    self.tt("pool", kdc[sr][d][:], ktm[0][d][:], bc_e(kd[:, c, cs]), ALU.mult, [("ktm", d), "kd"], [("kdc",) + tk])
                slots = [(d, hg) for d in range(2) for hg in range(2)]
                lvl = getattr(self, "gdn_lvl", 9)
                if lvl <= 1:
                    return
                for (d, hg) in slots:
                    c = orders[d][s]
                    tk = (sr, d)
                    sk = (d, hg)
                    hs = range(hg * 4, hg * 4 + 4)
                    bD, bDT, bG, bR = nb(), nb(), nb(), nb()
                    for j, h in enumerate(hs):
                        self.mm(self.ps[:, bD, j * 128:(j + 1) * 128], Ug[0][d][:, h, :], tri[:, 2 + d, :], True, True, [("Ug", d), "tri"], [("ps", bD)])
                    for j, h in enumerate(hs):
                        self.mm(self.ps[:, bDT, j * 128:(j + 1) * 128], tri[:, 2 + d, :], Ug[0][d][:, h, :], True, True, [("Ug", d), "tri"], [("ps", bDT)])
                    for j, h in enumerate(hs):
                        self.mm(self.ps[:, bG, j * 128:(j + 1) * 128], kTt[sr][d][:, h, :], kTt[sr][d][:, h, :], True, True, [("kTt",) + tk], [("ps", bG)])
                    for j, h in enumerate(hs):
                        self.mm(self.ps[:, bR, j * 128:(j + 1) * 128], kTt[sr][d][:, h, :], qTt[sr][d][:, h, :], True, True,
                                [("kTt",) + tk, ("qTt",) + tk], [("ps", bR)])
                    E = Eb[d][hg][0]
                    ET = ETb[d][hg][0]
                    p4 = lambda b: self.ps[:, b, :].rearrange("p (h e) -> p h e", h=4)
                    self.act(E[:], p4(bD), AF.Exp, [("ps", bD)], [("E",) + sk])
                    self.tt("dve", E[:], E[:], bc_h(tri[:, 2 + d, :]), ALU.mult, [("E",) + sk, "tri"], [("E",) + sk])
                    self.tt("dve", E[:], p4(bG), E[:], ALU.mult, [("ps", bG), ("E",) + sk], [("E",) + sk])
                    P0 = Pb[d][hg][0]
                    hsl = slice(d * 8 + hg * 4, d * 8 + hg * 4 + 4)
                    self.tt("dve", P0[:], E[:], bc_e(nbs[:, c, hsl]), ALU.mult, [("E",) + sk, "nbs"], [("P", 0) + sk])
                    self.act(ET[:], p4(bDT), AF.Exp, [("ps", bDT)], [("ET",) + sk])
                    self.tt("dve", ET[:], ET[:], bc_h(tri[:, d, :]), ALU.mult, [("ET",) + sk, "tri"], [("ET",) + sk])
                    self.tt("dve", aT[d][hg][sr][:], p4(bR), ET[:], ALU.mult, [("ps", bR), ("ET",) + sk], [("aT", sr) + sk])
                    bT = nb()
                    for j in range(4):
                        self.mm(self.ps[:, bT, j * 128:(j + 1) * 128], P0[:, j, :], identn[:], True, True, [("P", 0) + sk], [("ps", bT)])
                    self.cp("act", PTb[d][hg][0][:], p4(bT), [("ps", bT)], [("PT", 0) + sk])
                    self.tt("dve", RTb[d][hg][0][:], p4(bT), bc_h(identn[:]), ALU.add, [("ps", bT)], [("RT", 0) + sk])
                if lvl <= 2:
                    return
                for k in range(1, 7):
                    cur, prv = k % 2, (k - 1) % 2
                    for (d, hg) in slots:
                        sk = (d, hg)
                        p4 = lambda b: self.ps[:, b, :].rearrange("p (h e) -> p h e", h=4)
                        Pp, PTp = Pb[d][hg][prv], PTb[d][hg][prv]
                        Pn, PTn = Pb[d][hg][cur], PTb[d][hg][cur]
                        bP = nb()
                        for j in range(4):
                            self.mm(self.ps[:, bP, j * 128:(j + 1) * 128], PTp[:, j, :], Pp[:, j, :], True, True,
                                    [("P", prv) + sk, ("PT", prv) + sk], [("ps", bP)])
                        if k < 6:
                            bPT = nb()
                            for j in range(4):
                                self.mm(self.ps[:, bPT, j * 128:(j + 1) * 128], Pp[:, j, :], PTp[:, j, :], True, True,
                                        [("P", prv) + sk, ("PT", prv) + sk], [("ps", bPT)])
                        self.cp("act", Pn[:], p4(bP), [("ps", bP)], [("P", cur) + sk])
                        if k < 6:
                            self.cp("act", PTn[:], p4(bPT), [("ps", bPT)], [("PT", cur) + sk])
                        bRT = nb()
                        for j in range(4):
                            self.mm(self.ps[:, bRT, j * 128:(j + 1) * 128], Pn[:, j, :], RTb[d][hg][prv][:, j, :], True, True,
                                    [("P", cur) + sk, ("RT", prv) + sk], [("ps", bRT)])
                        self.tt("dve", RTb[d][hg][cur][:], p4(bRT), RTb[d][hg][prv][:], ALU.add, [("ps", bRT), ("RT", prv) + sk], [("RT", cur) + sk])
                if lvl <= 3:
                    return
                for (d, hg) in slots:
                    sk = (d, hg)
                    tk = (sr, d)
                    p4 = lambda b: self.ps[:, b, :].rearrange("p (h e) -> p h e", h=4)
                    self.cp("act", TTb[d][hg][0][:], RTb[d][hg][0][:], [("RT", 0) + sk], [("TT",) + sk])
                    bU, bW = nb(), nb()
                    for j in range(4):
                        h = hg * 4 + j
                        self.mm(self.ps[:, bU, j * 128:(j + 1) * 128], TTb[d][hg][0][:, j, :], vb[sr][d][:, h, :], True, True,
                                [("TT",) + sk, ("vb",) + tk], [("ps", bU)])
                    for j in range(4):
                        h = hg * 4 + j
                        self.mm(self.ps[:, bW, j * 128:(j + 1) * 128], kbg[sr][d][:, h, :], TTb[d][hg][0][:, j, :], True, True,
                                [("TT",) + sk, ("kbg",) + tk], [("ps", bW)])
                    self.cp("act", ub[d][hg][sr][:], p4(bU), [("ps", bU)], [("ub", sr) + sk])
                    self.cp("dve", wT[d][hg][sr][:], p4(bW), [("ps", bW)], [("wT", sr) + sk])

            def recur(s):
                sr = s % 2
                for d in range(2):
                    c = orders[d][s]
                    tk = (sr, d)
                    for hg in range(2):
                        sk = (d, hg)
                        p4 = lambda b: self.ps[:, b, :].rearrange("p (h e) -> p h e", h=4)
                        hsl = slice(d * 8 + hg * 4, d * 8 + hg * 4 + 4)
                        bW, bO1, bO2, bS = nb(), nb(), nb(), nb()
                        for j in range(4):
                            self.mm(self.ps[:, bW, j * 128:(j + 1) * 128], wT[d][hg][sr][:, j, :], Sb[d][hg][:, j, :], True, True,
                                    [("wT", sr) + sk, ("Sb",) + sk], [("ps", bW)])
                        for j in range(4):
                            h = hg * 4 + j
                            self.mm(self.ps[:, bO1, j * 128:(j + 1) * 128], qTt[sr][d][:, h, :], Sb[d][hg][:, j, :], True, True,
                                    [("qTt",) + tk, ("Sb",) + sk], [("ps", bO1)])
                        V_ = vn[d][hg][0]
                        self.tt("dve", V_[:], ub[d][hg][sr][:], p4(bW), ALU.subtract, [("ub", sr) + sk, ("ps", bW)], [("vn",) + sk])
                        for j in range(4):
                            self.mm(self.ps[:, bO2, j * 128:(j + 1) * 128], aT[d][hg][sr][:, j, :], V_[:, j, :], True, True,
                                    [("aT", sr) + sk, ("vn",) + sk], [("ps", bO2)])
                        for j in range(4):
                            h = hg * 4 + j
                            self.mm(self.ps[:, bS, j * 128:(j + 1) * 128], kdc[sr][d][:, h, :], V_[:, j, :], True, True,
                                    [("kdc",) + tk, ("vn",) + sk], [("ps", bS)])
                        O_ = oo[d][hg][0]
                        self.tt("dve", ot[d][hg][0][:], p4(bO1), bc_e(eg[:, c, hsl]), ALU.mult, [("ps", bO1), "eg"], [("ot",) + sk])
                        self.tt("dve", O_[:], ot[d][hg][0][:], p4(bO2), ALU.add, [("ot",) + sk, ("ps", bO2)], [("oo",) + sk])
                        self.dma("pool", self.go[d, c * 128:(c + 1) * 128, hg * 512:(hg + 1) * 512], O_[:].rearrange("p h e -> p (h e)"),
                                 [("oo",) + sk], [("go", d, c, hg)])
                        self.tt("dve", S[d][hg][:], S[d][hg][:], bc_e(glast[:, c, hsl]), ALU.mult, [("S",) + sk, "glast"], [("S",) + sk])
                        self.tt("dve", S[d][hg][:], S[d][hg][:], p4(bS), ALU.add, [("S",) + sk, ("ps", bS)], [("S",) + sk])
                        self.cp("act", Sb[d][hg][:], S[d][hg][:], [("S",) + sk], [("Sb",) + sk])

            stop = getattr(self, "gdn_stop", None)
            if stop != "pre":
                precompute(0)
            nsteps = NTILE if stop is None else (0 if stop in ("pre", "pc0") else int(stop))
            for s in range(nsteps):
                if s + 1 < NTILE:
                    precompute(s + 1)
                recur(s)
            P.stage_end()
        with ExitStack() as es:
            sb = lambda n, s, d: self.sb(es, n, s, d)
            of = [sb("of", [128, 8, 128], F32) for _ in range(2)]
            obk = [sb("obk", [128, 8, 128], F32) for _ in range(2)]
            zb = [sb("zb", [128, 8, 128], BF16) for _ in range(2)]
            sz = sb("sz", [128, 8, 128], F32)
            sq = sb("sqg", [128, 8, 128], F32)
            ss = sb("ss", [128, 8], F32)
            st = sb("sst", [128, 8], F32)
            an = sb("an", [128, 8, 128], BF16)
            gn = sb("gn", [128, 128], F32)
            ybuf = sb("ybuf8", [128, 8, NT], BF16)
            self.dma("sp", gn[:], self.gng[i2], [], ["gn"])
            bank = 0
            for m in range(NTILE):
                r = m % 2
                rows = slice(m * 128, (m + 1) * 128)
                self.dma("sp", of[r][:].rearrange("p h e -> p (h e)"), self.go[0, rows, :], [], [("of", r)])
                self.dma("sp", obk[r][:].rearrange("p h e -> p (h e)"), self.go[1, rows, :], [], [("obk", r)])
                self.dma("sp", zb[r][:].rearrange("p h e -> p (h e)"), self.gz[rows, :], [], [("zb", r)])
                self.tt("dve", of[r][:], of[r][:], obk[r][:], ALU.add, [("of", r), ("obk", r)], [("of", r)])
                self.act(sq[:], of[r][:], AF.Square, [("of", r)], ["sqg"])
                P.op("dve", lambda e: e.tensor_reduce(out=ss[:], in_=sq[:], axis=AX.X, op=ALU.add), ["sqg"], ["ss"])
                self.rsqrt(ss[:], ss[:], 1.0 / 128, st[:], "sst", ["ss"], ["ss"])
                self.tt("dve", of[r][:], of[r][:], ss[:].unsqueeze(2).to_broadcast([128, 8, 128]), ALU.mult, [("of", r), "ss"], [("of", r)])
                self.tt("dve", of[r][:], of[r][:], gn[:].unsqueeze(1).to_broadcast([128, 8, 128]), ALU.mult, [("of", r), "gn"], [("of", r)])
                self.act(sz[:], zb[r][:], AF.Silu, [("zb", r)], ["sz"])
                self.tt("dve", an[:], of[r][:], sz[:], ALU.mult, [("of", r), "sz"], ["an"])
                for hg in range(2):
                    b = bank % 8
                    bank += 1
                    for j in range(4):
                        self.mm(self.ps[:, b, j * 128:(j + 1) * 128], an[:, hg * 4 + j, :], self.ident_b[:], True, True, ["an"], [("ps", b)])
                    self.cp("act", ybuf[:, hg * 4:(hg + 1) * 4, m * 128:(m + 1) * 128], self.ps[:, b, :].rearrange("p (h e) -> p h e", h=4),
                            [("ps", b)], [("ybuf", m, hg)])
            for h in range(8):
                self.dma("pool", self.yT[h], ybuf[:, h, :], [("ybuf", m, h // 4) for m in range(NTILE)], [("yT", h)])
            P.stage_end()


    def build(self, layer_list=None, do_final=True, upto=None):
        self.declare()
        layer_list = list(range(DEPTH)) if layer_list is None else layer_list
        with ExitStack() as es:
            self.setup(es)
            self.stage_load_inputs()
            for li, l in enumerate(layer_list):
                need_ctx = l < DEPTH - 1
                tiles = TT if need_ctx else TT[:4]
                if li == 0:
                    self.stage_mod(l)
                if upto == "mod":
                    break
                self.stage_inproj(l)
                if upto == "inproj":
                    break
                if l % 2 == 0:
                    self.stage_gdn(l)
                    if upto == "gdn":
                        break
                    self.stage_diff(l, need_ctx)
                else:
                    self.stage_na(l, need_ctx)
                    self.stage_win(l, need_ctx)
                if upto == "mixer":
                    break
                self.stage_outproj(l, tiles)
                if upto == "outproj":
                    break
                nxt = layer_list[li + 1] if li + 1 < len(layer_list) else None
                self.stage_ffn(l, tiles, bg_mod=nxt)
                if nxt is not None:
                    self.cur = 1 - self.cur
            if do_final and upto is None:
                self.stage_final()
            self.P.final_wait()
        self.P.close()


def _fm(v):
    return np.ascontiguousarray(np.asarray(v, np.float32).reshape(KC, 128).T)


def _rep(v):
    v = np.asarray(v, np.float32)
    return np.ascontiguousarray(np.broadcast_to(v[None], (128,) + v.shape))


def prep_shared(inp):
    f = lambda a: np.ascontiguousarray(np.asarray(a, np.float32))
    sh = {}
    for k in ("ada_w", "w_in_even", "w_in_odd", "w_out", "ffn_w_gate", "ffn_w_up", "ffn_w_down"):
        sh[k] = f(inp[k])
    sh["ada_b_fm"] = np.ascontiguousarray(f(inp["ada_b"]).reshape(DEPTH, 96, 128).transpose(0, 2, 1))
    sh["nmg_fm"] = np.stack([_fm(inp["norm_mix_g"][l]) for l in range(DEPTH)])
    sh["nfg_fm"] = np.stack([_fm(inp["norm_ffn_g"][l]) for l in range(DEPTH)])
    sh["fng_fm"] = _fm(inp["final_norm_g"])
    cw = f(inp["gdn_conv_w"])
    sh["conv_fm"] = np.ascontiguousarray(cw.reshape(2, 3, 24, 128).transpose(0, 3, 2, 1))
    gv = np.concatenate([f(inp["gdn_a_log"]).reshape(2, 16), f(inp["gdn_dt_bias"]).reshape(2, 16)], axis=1)
    sh["gdn_vec"] = np.stack([_rep(gv[i]) for i in range(2)])
    sh["gng_rep"] = np.stack([_rep(f(inp["gdn_norm_g"])[i]) for i in range(2)])
    lam = np.stack([f(inp["diff_lambda_q1"]), f(inp["diff_lambda_k1"]), f(inp["diff_lambda_q2"]), f(inp["diff_lambda_k2"])], axis=1)
    sh["lam_rep"] = np.stack([_rep(lam[i]) for i in range(2)])
    sh["subln_fm"] = np.ascontiguousarray(f(inp["diff_subln_g"]).reshape(2, 128, 1))
    rpb = f(inp["na_rpb"])
    kr = np.repeat(np.arange(2), 64)
    kc = np.tile(np.arange(64), 2)
    nb = np.full((2, 8, 128, N_NA_MASK, 128), -30000.0, np.float32)
    for mid, (delta, ok) in enumerate(_NA_MASKS):
        dr = np.clip(2 * delta + kr[:, None] - kr[None, :] + 7, 0, 14)
        dc = np.clip(kc[:, None] - kc[None, :] + 15, 0, 30)
        g = rpb[:, :, dr, dc]
        nb[:, :, :, mid, :] = np.where(ok[None, None] > 0, g, np.float32(-30000.0))
    sh["na_bias"] = nb
    sh["sink_rep"] = np.stack([_rep(f(inp["win_sink"])[i]) for i in range(2)])
    c = _consts()
    sh["c_ident"] = c["ident"]
    sh["c_tri"] = c["tri"]
    sh["c_rope_d"] = c["rope_d"]
    sh["c_rope_w"] = c["rope_w"]
    sh["c_perm"] = c["perm"]
    sh["c_win_mask"] = c["win_mask"]
    return sh


def prep_core(inp, b):
    x = np.ascontiguousarray(np.asarray(inp["x"][b], np.float32))
    ctx = np.ascontiguousarray(np.asarray(inp["ctx"][b], np.float32))
    cf = np.stack([_fm(inp["c"][b]), _fm(inp["c_ctx"])], axis=2)
    return {"x": x, "ctx": ctx, "c_fm": np.ascontiguousarray(cf)}


_CACHE = {}


def kernel(**inputs):
    n = 8
    nc = bass.Bass("TRN2", target_bir_lowering=False)
    bld = Builder(nc)
    bld.build()
    shared = prep_shared(inputs)
    in_maps = []
    for b in range(n):
        m = dict(shared)
        m.update(prep_core(inputs, b))
        in_maps.append(m)
    res = run_bass_kernel_spmd(nc, in_maps, core_ids=list(range(n)))
    return np.stack([np.asarray(r["out"], np.float32) for r in res.results], axis=0)
```

```python
import math
from contextlib import ExitStack

import numpy as np

import concourse.bass as bass
import concourse.mybir as mybir
from concourse.bass_utils import run_bass_kernel_spmd

F32 = mybir.dt.float32
BF16 = mybir.dt.bfloat16
AF = mybir.ActivationFunctionType
ALU = mybir.AluOpType
AX = mybir.AxisListType

D = 2048
KC = 16
T = 2048
C = 256
NT = T + C
NTILE = NT // 128
FF = 5632
FC = FF // 128
DEPTH = 4
EPS = 1e-6
TT = [(0, 512), (512, 512), (1024, 512), (1536, 512), (2048, 256)]
W_EVEN = 7200
W_ODD = 4608

ENGS = ("pe", "act", "dve", "pool", "sp")


class _Op:
    __slots__ = ("eng", "emit", "deps", "is_dma", "sem", "semval", "need_inc")

    def __init__(self, eng, emit, is_dma):
        self.eng = eng
        self.emit = emit
        self.deps = []
        self.is_dma = is_dma
        self.sem = None
        self.semval = 0
        self.need_inc = False


class Prog:
    NDMASEM = 8

    def __init__(self, nc, same_eng_sync=True):
        self.nc = nc
        self.same_eng_sync = same_eng_sync
        self.stack = ExitStack()
        self.esem = {e: self.stack.enter_context(nc.semaphore("s_" + e)) for e in ENGS}
        self.ecount = {e: 0 for e in ENGS}
        self.dsem, self.dcount, self.drr, self.dlast = {}, {}, {}, {}
        for q in ("sp", "pool", "act"):
            self.dsem[q] = [self.stack.enter_context(nc.semaphore("d_%s%d" % (q, i))) for i in range(self.NDMASEM)]
            self.dcount[q] = [0] * self.NDMASEM
            self.drr[q] = 0
            self.dlast[q] = [None] * self.NDMASEM
        self.ops = {e: [] for e in ENGS}
        self.lastw = {}
        self.readers = {}
        self.waited = {e: {} for e in ENGS}
        self.n_inst = 0
        self._cur_barrier = None

    def op(self, eng, emit, reads=(), writes=(), dma=False):
        o = _Op(eng, emit, dma)
        deps = []
        psr = [k for k in reads if isinstance(k, tuple) and k and k[0] == "ps"]
        if psr:
            writes = list(writes) + [k for k in psr if k not in writes]
        for k in reads:
            w = self.lastw.get(k)
            if w is not None:
                deps.append(w)
        for k in writes:
            w = self.lastw.get(k)
            if w is not None:
                deps.append(w)
            deps.extend(self.readers.get(k, ()))
        if dma:
            q = eng
            i = self.drr[q]
            self.drr[q] = (i + 1) % self.NDMASEM
            prev = self.dlast[q][i]
            if prev is not None:
                deps.append(prev)
            self.dcount[q][i] += 16
            o.sem = self.dsem[q][i]
            o.semval = self.dcount[q][i]
            o.need_inc = True
            self.dlast[q][i] = o
        seen = set()
        for d in deps:
            if d is o or id(d) in seen:
                continue
            seen.add(id(d))
            if (not d.is_dma) and d.eng == eng and (eng == "pe" or not self.same_eng_sync):
                continue
            o.deps.append(d)
            d.need_inc = True
        for k in writes:
            self.lastw[k] = o
            self.readers[k] = []
        for k in reads:
            self.readers.setdefault(k, []).append(o)
        self.ops[eng].append(o)
        return o

    def stage_end(self):
        lasts = []
        for e in ENGS:
            if self.ops[e]:
                lasts.append(self.ops[e][-1])
        for q in self.dlast:
            for d in self.dlast[q]:
                if d is not None:
                    lasts.append(d)
        for o in lasts:
            o.need_inc = True
        self.flush()
        cur = self._cur_barrier or {}
        self._cur_barrier = {e: list(lasts) + list(cur.get(e) or []) for e in ENGS}
        self.lastw.clear()
        self.readers.clear()

    def flush(self):
        nc = self.nc
        for e in ENGS:
            for o in self.ops[e]:
                if o.is_dma:
                    continue
                if o.need_inc and o.sem is None:
                    self.ecount[e] += 1
                    o.sem = self.esem[e]
                    o.semval = self.ecount[e]
        pend = self._cur_barrier
        engmap = {"pe": "tensor", "act": "scalar", "dve": "vector", "pool": "gpsimd", "sp": "sync"}
        if not any(self.ops[e] for e in ENGS):
            return
        with nc.Block() as block:
            for e in ENGS:
                ops = self.ops[e]
                if not ops:
                    continue

                def body(engine, ops=ops, e=e):
                    waited = self.waited[e]
                    first = True
                    for o in ops:
                        deps = o.deps
                        if first and pend and pend.get(e):
                            deps = list(deps) + [d for d in pend[e] if d is not o]
                            pend[e] = None
                        first = False
                        need = {}
                        for d in deps:
                            sid = id(d.sem)
                            if waited.get(sid, 0) >= d.semval:
                                continue
                            if sid not in need or need[sid][1] < d.semval:
                                need[sid] = (d.sem, d.semval)
                        for sid, (s, v) in need.items():
                            engine.wait_ge(s, v)
                            waited[sid] = v
                            self.n_inst += 1
                        ins = o.emit(engine)
                        self.n_inst += 1
                        if o.need_inc:
                            ins.then_inc(o.sem, 16 if o.is_dma else 1)

                getattr(block, engmap[e])(body)
        for e in ENGS:
            self.ops[e] = []

    def final_wait(self, eng="sp"):
        self.stage_end()
        self.op(eng, lambda en: en.nop())
        self.flush()

    def close(self):
        self.stack.close()


def _rope_tables(kind):
    pos = np.arange(T)
    rows, cols = pos // 64, pos % 64
    cos = np.zeros((128, T), np.float32)
    sin = np.zeros((128, T), np.float32)
    perm = np.zeros((128, 128), np.float32)
    if kind == "d":
        blocks = [(0, 32, rows), (32, 32, cols), (64, 32, rows), (96, 32, cols)]
    else:
        blocks = [(0, 64, rows), (64, 64, cols)]
    for base, n, p in blocks:
        half = n // 2
        inv = (10000.0 ** (-np.arange(0, n, 2, dtype=np.float32) / n)).astype(np.float32)
        ang = p.astype(np.float32)[None, :] * inv[:, None]
        c, s = np.cos(ang).astype(np.float32), np.sin(ang).astype(np.float32)
        cos[base:base + half] = c
        cos[base + half:base + n] = c
        sin[base:base + half] = -s
        sin[base + half:base + n] = s
        for i in range(half):
            perm[base + half + i, base + i] = 1.0
            perm[base + i, base + half + i] = 1.0
    return cos, sin, perm


def _na_geometry():
    masks, index, plan = [], {}, {}
    for a in range(16):
        plan[a] = []
        qrow = np.repeat(np.array([2 * a, 2 * a + 1]), 64)
        qcol = np.tile(np.arange(64), 2)
        start = np.clip(qrow - 4, 0, 24)
        winc = np.clip(qcol - 8, 0, 48)
        for t in range(16):
            krow = np.repeat(np.array([2 * t, 2 * t + 1]), 64)
            kcol = np.tile(np.arange(64), 2)
            ok = ((krow[:, None] >= start[None, :]) & (krow[:, None] < start[None, :] + 8)
                  & (kcol[:, None] >= winc[None, :]) & (kcol[:, None] < winc[None, :] + 16))
            if not ok.any():
                continue
            key = (t - a, ok.tobytes())
            if key not in index:
                index[key] = len(masks)
                masks.append((t - a, ok.astype(np.float32)))
            plan[a].append((t, index[key]))
    return masks, plan


_NA_MASKS, _NA_PLAN = _na_geometry()
N_NA_MASK = len(_NA_MASKS)


def _win_masks():
    m = np.zeros((6, 128, 512), np.float32)
    for r in range(6):
        j = (r - 1) * 128 + np.arange(128)[:, None]
        i = np.arange(512)[None, :]
        m[r] = (np.abs(i - j) <= 128).astype(np.float32)
    return m


def _consts():
    c = {}
    c["ident"] = np.eye(128, dtype=np.float32)
    m = np.arange(128)[:, None]
    i = np.arange(128)[None, :]
    c["tri"] = np.stack([(m <= i), (m >= i), (m > i), (m < i)]).astype(np.float32)
    cd, sd, pd = _rope_tables("d")
    cw, sw, pw = _rope_tables("w")
    c["rope_d"] = np.stack([cd, sd])
    c["rope_w"] = np.stack([cw, sw])
    c["perm"] = np.stack([pd, pw])
    c["na_mask"] = np.stack([mk for _, mk in _NA_MASKS])
    c["win_mask"] = _win_masks()
    return c


class Builder:
    def __init__(self, nc, debug=False, layers=DEPTH, ndt=F32):
        self.nc = nc
        self.debug = debug
        self.layers = layers
        self.ndt = ndt
        self.P = Prog(nc)
        self.inp = {}
        self.scr = {}
        self.uid = 0

    def din(self, name, shape, dtype=F32):
        t = self.nc.dram_tensor(name, list(shape), dtype, kind="ExternalInput").ap()
        self.inp[name] = t
        return t

    def dscr(self, name, shape, dtype):
        kind = "ExternalOutput" if self.debug else "Internal"
        t = self.nc.dram_tensor(name, list(shape), dtype, kind=kind).ap()
        self.scr[name] = t
        return t

    def sb(self, es, name, shape, dtype):
        self.uid += 1
        return es.enter_context(self.nc.sbuf_tensor("%s_%d" % (name, self.uid), list(shape), dtype))

    def mm(self, out, lhsT, rhs, start, stop, reads, writes):
        self.P.op("pe", lambda e: e.matmul(out, lhsT=lhsT, rhs=rhs, start=start, stop=stop), reads, writes)

    def act(self, out, in_, func, reads, writes, bias=None, scale=None, accum_out=None):
        kw = {}
        if bias is not None:
            kw["bias"] = bias
        if scale is not None:
            kw["scale"] = scale
        if accum_out is not None:
            kw["accum_out"] = accum_out
        self.P.op("act", lambda e: e.activation(out=out, in_=in_, func=func, **kw), reads, writes)

    def tt(self, eng, out, in0, in1, op, reads, writes):
        self.P.op(eng, lambda e: e.tensor_tensor(out=out, in0=in0, in1=in1, op=op), reads, writes)

    def ts(self, eng, out, in0, s1, s2, op0, op1, reads, writes):
        if op1 is None:
            self.P.op(eng, lambda e: e.tensor_scalar(out=out, in0=in0, scalar1=s1, scalar2=None, op0=op0), reads, writes)
        else:
            self.P.op(eng, lambda e: e.tensor_scalar(out=out, in0=in0, scalar1=s1, scalar2=s2, op0=op0, op1=op1), reads, writes)

    def stt(self, out, in0, scalar, in1, op0, op1, reads, writes):
        self.P.op("dve", lambda e: e.scalar_tensor_tensor(out=out, in0=in0, scalar=scalar, in1=in1, op0=op0, op1=op1), reads, writes)

    def cp(self, eng, out, in_, reads, writes):
        if eng == "act":
            self.P.op("act", lambda e: e.copy(out=out, in_=in_), reads, writes)
        else:
            self.P.op(eng, lambda e: e.tensor_copy(out=out, in_=in_), reads, writes)

    def recip(self, out, in_, reads, writes):
        self.P.op("dve", lambda e: e.reciprocal(out=out, in_=in_), reads, writes)

    def dma(self, q, out, in_, reads, writes, **kw):
        self.P.op(q, lambda e: e.dma_start(out=out, in_=in_, **kw), reads, writes, dma=True)

    def rsqrt(self, out, in_, mult, tmp, tmpkey, reads, writes):
        self.act(tmp, in_, AF.Sqrt, reads, [tmpkey], bias=self.eps_col[:, 0:1], scale=mult)
        self.recip(out, tmp, [tmpkey], writes)

    def declare(self):
        di = self.din
        self.x = di("x", [T, D])
        self.ctx = di("ctx", [C, D])
        self.c_fm = di("c_fm", [128, KC, 2])
        self.ada_w = di("ada_w", [DEPTH, D, 6 * D])
        self.ada_b = di("ada_b_fm", [DEPTH, 128, 96])
        self.nmg = di("nmg_fm", [DEPTH, 128, KC])
        self.nfg = di("nfg_fm", [DEPTH, 128, KC])
        self.fng = di("fng_fm", [128, KC])
        self.w_in_even = di("w_in_even", [2, D, W_EVEN])
        self.w_in_odd = di("w_in_odd", [2, D, W_ODD])
        self.w_out = di("w_out", [DEPTH, D, D])
        self.w_gate = di("ffn_w_gate", [DEPTH, D, FF])
        self.w_up = di("ffn_w_up", [DEPTH, D, FF])
        self.w_down = di("ffn_w_down", [DEPTH, FF, D])
        self.conv_fm = di("conv_fm", [2, 128, 24, 3])
        self.gdn_vec = di("gdn_vec", [2, 128, 32])
        self.gng = di("gng_rep", [2, 128, 128])
        self.lam = di("lam_rep", [2, 128, 4, 64])
        self.subln = di("subln_fm", [2, 128, 1])
        self.na_bias = di("na_bias", [2, 8, 128, N_NA_MASK, 128])
        self.sink = di("sink_rep", [2, 128, 8])
        self.c_ident = di("c_ident", [128, 128])
        self.c_tri = di("c_tri", [4, 128, 128])
        self.c_rope_d = di("c_rope_d", [2, 128, T])
        self.c_rope_w = di("c_rope_w", [2, 128, T])
        self.c_perm = di("c_perm", [2, 128, 128])
        self.c_win = di("c_win_mask", [6, 128, 512])
        self.out = self.nc.dram_tensor("out", [T, D], F32, kind="ExternalOutput").ap()
        ds = self.dscr
        self.hT = ds("hT", [KC, 128, NT], F32)
        self.yT = ds("yT", [KC, 128, NT], BF16)
        self.qT_a = ds("qT_a", [8, 128, NT], BF16)
        self.kT_a = ds("kT_a", [8, 128, NT], BF16)
        self.v_a = ds("v_a", [NT, 1024], BF16)
        self.k_tm = ds("k_tm", [NT, 1024], BF16)
        self.qT_b = ds("qT_b", [8, 128, NT], BF16)
        self.kT_b = ds("kT_b", [8, 128, NT], BF16)
        self.v_b = ds("v_b", [NT, 1024], BF16)
        self.gz = ds("gz", [NT, 1024], BF16)
        self.gab = ds("gab", [NT, 32], F32)
        self.go = ds("go", [2, NT, 1024], F32)

    def setup(self, es):
        nc = self.nc
        sb = lambda n, s, d: self.sb(es, n, s, d)
        self.ps = es.enter_context(nc.psum_tensor("ps", [128, 8, 512], F32))
        self.ident_f = sb("ident_f", [128, 128], F32)
        self.ident_b = sb("ident_b", [128, 128], BF16)
        self.ones_b = sb("ones_b", [128, 128], BF16)
        self.ones_f = sb("ones_f", [128, 128], F32)
        self.eps_col = sb("eps_col", [128, 1], F32)
        self.cact = sb("cact", [128, KC, 2], BF16)
        self.modv_s = [sb("modv", [128, 96, 2], F32) for _ in range(2)]
        self.G1_s = [sb("G1", [128, KC, 2], F32) for _ in range(2)]
        self.G2_s = [sb("G2", [128, KC, 2], F32) for _ in range(2)]
        self.cur = 0
        self.nrm2 = sb("nrm2", [128, 64], F32)
        self.mbias = sb("mbias", [128, 64], F32)
        P = self.P
        with ExitStack() as e1:
            craw = self.sb(e1, "craw", [128, KC, 2], F32)
            self.dma("sp", self.ident_f[:], self.c_ident, [], ["ident_f"])
            self.dma("sp", craw[:], self.c_fm, [], ["craw"])
            self.cp("dve", self.ident_b[:], self.ident_f[:], ["ident_f"], ["ident_b"])
            P.op("pool", lambda e: e.memset(self.ones_b[:], 1.0), [], ["ones_b"])
            P.op("pool", lambda e: e.memset(self.ones_f[:], 1.0), [], ["ones_f"])
            P.op("pool", lambda e: e.memset(self.eps_col[:], EPS), [], ["eps"])
            self.act(self.cact[:], craw[:], AF.Silu, ["craw"], ["cact"])
            P.stage_end()

    @property
    def modv(self):
        return self.modv_s[self.cur]

    @property
    def G1(self):
        return self.G1_s[self.cur]

    @property
    def G2(self):
        return self.G2_s[self.cur]

    def mod_part(self, es, l, part):
        wv = self.ada_w[l].rearrange("(k p) n -> p k n", p=128)
        wbm = [self.sb(es, "wbm", [128, KC, 256], BF16) for _ in range(2)]
        g0, g1 = (0, 24) if part == 0 else (24, 48)

        def load(cg):
            self.dma("pool", wbm[cg % 2][:], wv[:, :, cg * 256:(cg + 1) * 256], [], [("wbm", cg % 2)])
        if part == 1:
            adab = self.sb(es, "adab2", [128, 96], F32)
            nmg = self.sb(es, "nmg2", [128, KC], F32)
            nfg = self.sb(es, "nfg2", [128, KC], F32)
            self.dma("sp", adab[:], self.ada_b[l], [], ["adab2"])
            self.dma("sp", nmg[:], self.nmg[l], [], ["nmg2"])
            self.dma("sp", nfg[:], self.nfg[l], [], ["nfg2"])
        load(g0)
        for cg in range(g0, g1):
            if cg + 1 < g1:
                load(cg + 1)
            for j in range(2):
                ch = cg * 2 + j
                for k in range(KC):
                    self.mm(self.ps[:, 6, ch * 2:ch * 2 + 2], wbm[cg % 2][:, k, j * 128:(j + 1) * 128], self.cact[:, k, :],
                            k == 0, k == KC - 1, [("wbm", cg % 2), "cact"], [("ps", 6)])
            yield
        if part == 1:
            nx = 1 - self.cur
            mv, g1t, g2t = self.modv_s[nx], self.G1_s[nx], self.G2_s[nx]
            self.tt("dve", mv[:], self.ps[:, 6, 0:192].rearrange("p (a b) -> p a b", b=2),
                    adab[:].unsqueeze(2).to_broadcast([128, 96, 2]), ALU.add, [("ps", 6), "adab2"], ["modv_n"])
            self.stt(g1t[:], mv[:, 16:32, :], 1.0, nmg[:].unsqueeze(2).to_broadcast([128, KC, 2]), ALU.add, ALU.mult,
                     ["modv_n", "nmg2"], ["G1n"])
            self.stt(g2t[:], mv[:, 64:80, :], 1.0, nfg[:].unsqueeze(2).to_broadcast([128, KC, 2]), ALU.add, ALU.mult,
                     ["modv_n", "nfg2"], ["G2n"])
        yield

    def stage_load_inputs(self):
        P = self.P
        with ExitStack() as es:
            xin = [self.sb(es, "xin", [128, D], F32) for _ in range(2)]
            hst = [self.sb(es, "hst", [128, KC, 128], F32) for _ in range(2)]
            hv = self.hT.rearrange("k p t -> p k t")
            for m in range(NTILE):
                src = self.x[m * 128:(m + 1) * 128, :] if m < 16 else self.ctx[(m - 16) * 128:(m - 15) * 128, :]
                r = m % 2
                self.dma("sp", xin[r][:], src, [], [("xin", r)])
                for g in range(4):
                    b = (m * 4 + g) % 8
                    for j in range(4):
                        k = g * 4 + j
                        self.mm(self.ps[:, b, j * 128:(j + 1) * 128], xin[r][:, k * 128:(k + 1) * 128], self.ident_f[:],
                                True, True, [("xin", r)], [("ps", b)])
                    dst = hst[r][:, g * 4:(g + 1) * 4, :]
                    srcp = self.ps[:, b, :].rearrange("p (a b) -> p a b", a=4)
                    self.cp("act" if g % 2 else "dve", dst, srcp, [("ps", b)], [("hst", r, g)])
                self.dma("pool", hv[:, :, m * 128:(m + 1) * 128], hst[r][:], [("hst", r, g) for g in range(4)], [("hT", m)])
            P.stage_end()

    def stage_mod(self, l):
        P = self.P
        with ExitStack() as es:
            wb = [self.sb(es, "wb", [128, KC, 512], BF16) for _ in range(2)]
            adab = self.sb(es, "adab", [128, 96], F32)
            nmg = self.sb(es, "nmg", [128, KC], F32)
            nfg = self.sb(es, "nfg", [128, KC], F32)
            self.dma("sp", adab[:], self.ada_b[l], [], ["adab"])
            self.dma("sp", nmg[:], self.nmg[l], [], ["nmg"])
            self.dma("sp", nfg[:], self.nfg[l], [], ["nfg"])
            wv = self.ada_w[l].rearrange("(k p) n -> p k n", p=128)

            def load(cg):
                self.dma("pool", wb[cg % 2][:], wv[:, :, cg * 512:(cg + 1) * 512], [], [("wb", cg % 2)])
            load(0)
            for cg in range(24):
                if cg + 1 < 24:
                    load(cg + 1)
                for j in range(4):
                    ch = cg * 4 + j
                    for k in range(KC):
                        self.mm(self.ps[:, 0, ch * 2:ch * 2 + 2], wb[cg % 2][:, k, j * 128:(j + 1) * 128], self.cact[:, k, :],
                                k == 0, k == KC - 1, [("wb", cg % 2), "cact"], [("ps", 0)])
            self.tt("dve", self.modv[:], self.ps[:, 0, 0:192].rearrange("p (a b) -> p a b", b=2),
                    adab[:].unsqueeze(2).to_broadcast([128, 96, 2]), ALU.add, [("ps", 0), "adab"], ["modv"])
            self.stt(self.G1[:], self.modv[:, 16:32, :], 1.0, nmg[:].unsqueeze(2).to_broadcast([128, KC, 2]), ALU.add, ALU.mult,
                     ["modv", "nmg"], ["G1"])
            self.stt(self.G2[:], self.modv[:, 64:80, :], 1.0, nfg[:].unsqueeze(2).to_broadcast([128, KC, 2]), ALU.add, ALU.mult,
                     ["modv", "nfg"], ["G2"])
            if self.debug:
                dm = self.dscr("dbg_mod%d" % l, [128, 192], F32)
                self.dma("sp", dm, self.modv[:].rearrange("p a b -> p (a b)"), ["modv"], ["dbgm"])
            P.stage_end()

    def norm_tiles(self, es, tiles, which, uT, base, tag):
        G = self.G1 if which == 0 else self.G2
        sh = 0 if which == 0 else 48
        hv = self.hT.rearrange("k p t -> p k t")
        hb = [self.sb(es, "hb", [128, KC, 256], F32) for _ in range(2)]
        sq = [self.sb(es, "sq", [128, KC, 256], BF16) for _ in range(2)]
        rs = [self.sb(es, "rs", [128, 256], F32) for _ in range(2)]
        rt = self.sb(es, "rt", [128, 256], F32)
        sub = []
        for (t0, n) in tiles:
            for o in range(0, n, 256):
                sub.append((t0 + o, min(256, n - o)))
        pend = None
        for i, (t0, n) in enumerate(sub):
            r = i % 2
            s = 1 if t0 >= T else 0
            b = 7
            self.dma("sp", hb[r][:, :, 0:n], hv[:, :, t0:t0 + n], [("hT", "all")], [("hb", r)])
            self.tt("pool", sq[r][:, :, 0:n], hb[r][:, :, 0:n], hb[r][:, :, 0:n], ALU.mult, [("hb", r)], [("sq", r)])
            for k in range(KC):
                self.mm(self.ps[:, b, 0:n], self.ones_b[:], sq[r][:, k, 0:n], k == 0, k == KC - 1, [("sq", r), "ones_b"], [("ps", b)])
            self.rsqrt(rs[r][:, 0:n], self.ps[:, b, 0:n], 1.0 / D, rt[:, 0:n], "rt", [("ps", b), "eps"], [("rs", r)])
            self.tt("dve", hb[r][:, :, 0:n], hb[r][:, :, 0:n], rs[r][:, 0:n].unsqueeze(1).to_broadcast([128, KC, n]), ALU.mult,
                    [("hb", r), ("rs", r)], [("hb", r)])

            def second(r=r, s=s, t0=t0, n=n):
                for k in range(KC):
                    self.act(uT[:, k, t0 - base:t0 - base + n], hb[r][:, k, 0:n], AF.Identity, [("hb", r), "G1", "G2", "modv"],
                             [(tag, k, t0)], scale=G[:, k, s:s + 1], bias=self.modv[:, sh + k, s:s + 1])
            if pend is not None:
                pend()
            pend = second
        if pend is not None:
            pend()

    def resid_update(self, hrow, c, t0, n, ps_ap, gate_chunk_base, rkeys, wkey):
        s = 1 if t0 >= T else 0
        self.stt(hrow[:, t0:t0 + n], ps_ap, self.modv[:, gate_chunk_base + c, s:s + 1], hrow[:, t0:t0 + n], ALU.mult, ALU.add,
                 rkeys + ["modv"], [wkey])

    def stats_row(self, orow, okeys, slot0, split, sqrow, mx, sn="sqrow", presq=False):
        if not presq:
            self.act(sqrow[:], orow[:], AF.Square, okeys, [sn])
        parts = [(0, 64), (64, 128)] if split else [(0, 128)]
        for pi, (p0, p1) in enumerate(parts):
            for ti, (t0, n) in enumerate(TT):
                b = 4 + (self.auxr % 3)
                self.auxr += 1
                self.mm(self.ps[:, b, 0:n], self.ones_b[p0:p1, :], sqrow[p0:p1, t0:t0 + n], True, True, [sn, "ones_b"], [("ps", b)])
                self.P.op("dve", lambda e, b=b, n=n, ti=ti: e.tensor_reduce(out=mx[:, ti:ti + 1], in_=self.ps[:, b, 0:n], axis=AX.X, op=ALU.max),
                          [("ps", b)], [("mx", ti)])
            sl = slot0 + pi
            self.P.op("dve", lambda e, sl=sl: e.tensor_reduce(out=self.nrm2[:, sl:sl + 1], in_=mx[:, 0:5], axis=AX.X, op=ALU.max),
                      [("mx", ti) for ti in range(5)], [("nrm2", sl)])

    def stage_inproj(self, l):
        P = self.P
        even = (l % 2 == 0)
        i2 = l // 2
        w = (self.w_in_even if even else self.w_in_odd)[i2]
        wv = w.rearrange("(k p) n -> p k n", p=128)
        if even:
            jobs = [("fm", "gq", 0, 1024), ("fm", "gk", 1024, 1024), ("fm", "gv", 2048, 1024), ("tm", "z", 3072, 1024),
                    ("tm", "ab", 4096, 32), ("fm", "dq", 4128, 1024), ("fm", "dk", 5152, 1024), ("tm", "dv", 6176, 1024)]
        else:
            jobs = [("fm", "nq", 0, 1024), ("fm", "nk", 1024, 1024), ("tm", "nv", 2048, 1024), ("fm", "wq", 3072, 1024),
                    ("fm", "wk", 4096, 256), ("tm", "wv", 4352, 256)]
        groups = []
        for mode, kind, c0, nc_ in jobs:
            for g0 in range(0, nc_, 512):
                groups.append((mode, kind, c0, g0, min(512, nc_ - g0)))
        with ExitStack() as eo:
            uT = self.sb(eo, "uT", [128, KC, NT], BF16)
            with ExitStack() as es:
                self.norm_tiles(es, TT, 0, uT, 0, "uT")
                P.stage_end()
                if self.debug:
                    du = self.dscr("dbg_uT%d" % l, [128, KC, NT], BF16)
                    self.dma("sp", du, uT[:], [], ["dbgu"])
                    P.stage_end()
            with ExitStack() as es:
                sb = lambda n, s, d: self.sb(es, n, s, d)
                wb = [sb("wb", [128, KC, 512], BF16) for _ in range(2)]
                xrows = [sb("xrow", [128, NT], F32) for _ in range(2)]
                yas = [sb("ya", [128, NT], F32) for _ in range(2)]
                orow0s = [sb("orow0", [128, NT], BF16) for _ in range(2)]
                orow = [sb("orow", [128, NT], BF16) for _ in range(2)]
                sqrows = [sb("sqrow", [128, NT], BF16) for _ in range(2)]
                rsr = sb("rsr", [128, 512], F32)
                rst = sb("rst", [128, 512], F32)
                t1 = sb("t1", [128, 512], F32)
                t2 = sb("t2", [128, 512], F32)
                mx = sb("mx", [128, 8], F32)
                tms = [sb("tms", [128, NTILE, 128], BF16) for _ in range(1)]
                gst = [sb("gst", [128, 512], BF16) for _ in range(3)]
                abst = sb("abst", [128, NTILE, 32], F32)
                rope = sb("rope", [128, 2, T], F32)
                permb = sb("permb", [128, 128], BF16)
                permf = sb("permf", [128, 128], F32)
                cw = sb("cw", [128, 24, 3], F32)
                self.auxr = 0
                self.dma("sp", rope[:], (self.c_rope_d if even else self.c_rope_w).rearrange("a p t -> p a t"), [], ["rope"])
                self.dma("sp", permf[:], self.c_perm[0 if even else 1], [], ["permf"])
                self.cp("dve", permb[:], permf[:], ["permf"], ["permb"])
                if even:
                    self.dma("sp", cw[:], self.conv_fm[i2], [], ["cw"])

                def load(gi):
                    mode, kind, c0, g0, gw = groups[gi]
                    self.dma("pool", wb[gi % 2][:, :, 0:gw], wv[:, :, c0 + g0:c0 + g0 + gw], [], [("wb", gi % 2)])

                load(0)
                mainr = 0
                orr = 0
                self.tmr = 0
                gsr = 0
                self.pipe = []
                self.pipe_depth = 1
                for gi, (mode, kind, c0, g0, gw) in enumerate(groups):
                    if gi + 1 < len(groups):
                        load(gi + 1)
                    wbg = wb[gi % 2]
                    wkey = ("wb", gi % 2)
                    if mode == "tm":
                        if kind == "ab":
                            for m in range(NTILE):
                                b = mainr % 4
                                mainr += 1
                                for k in range(KC):
                                    self.mm(self.ps[:, b, 0:gw], uT[:, k, m * 128:(m + 1) * 128], wbg[:, k, 0:gw], k == 0, k == KC - 1,
                                            [wkey], [("ps", b)])
                                self.cp("act", abst[:, m, :], self.ps[:, b, 0:gw], [("ps", b)], [("abst", m)])
                            self.dma("pool", self.gab.rearrange("(t p) c -> p t c", p=128), abst[:], [("abst", m) for m in range(NTILE)], ["gab"])
                            continue
                        dst = {"z": self.gz, "dv": self.v_b, "nv": self.v_a, "wv": self.v_b}[kind]
                        for m in range(NTILE):
                            b = mainr % 4
                            mainr += 1
                            for k in range(KC):
                                self.mm(self.ps[:, b, 0:gw], uT[:, k, m * 128:(m + 1) * 128], wbg[:, k, 0:gw], k == 0, k == KC - 1,
                                        [wkey], [("ps", b)])
                            r = gsr % 3
                            gsr += 1
                            self.cp("act" if m % 2 else "dve", gst[r][:, 0:gw], self.ps[:, b, 0:gw], [("ps", b)], [("gst", r)])
                            self.dma("pool", dst[m * 128:(m + 1) * 128, g0:g0 + gw], gst[r][:, 0:gw], [("gst", r)], [(kind, m, g0)])
                        continue
                    for j in range(gw // 128):
                        ch = (g0 // 128) + j
                        o_r = orr % 2
                        orr += 1
                        ob = orow[o_r]
                        okey = ("orow", o_r)
                        xrow, ya, sqrow, ob0 = xrows[o_r], yas[o_r], sqrows[o_r], orow0s[o_r]
                        xn, yn, sn, o0n = ("xrow", o_r), ("ya", o_r), ("sqrow", o_r), ("orow0", o_r)
                        plain = kind in ("nq", "nk")
                        roped = kind in ("dq", "dk", "wq", "wk")
                        gdn = kind in ("gq", "gk", "gv")
                        for ti, (t0, n) in enumerate(TT):
                            b = mainr % 4
                            mainr += 1
                            for k in range(KC):
                                self.mm(self.ps[:, b, 0:n], wbg[:, k, j * 128:(j + 1) * 128], uT[:, k, t0:t0 + n], k == 0, k == KC - 1,
                                        [wkey], [("ps", b)])
                            if plain or (roped and t0 >= T):
                                self.cp("act", ob[:, t0:t0 + n], self.ps[:, b, 0:n], [("ps", b)], [(okey, ti)])
                            elif roped:
                                self.cp("act", ob0[:, t0:t0 + n], self.ps[:, b, 0:n], [("ps", b)], [(o0n, ti)])
                            else:
                                self.cp("act", xrow[:, t0:t0 + n], self.ps[:, b, 0:n], [("ps", b)], [(xn, ti)])

                        okeys_now = [(okey, ti) for ti in range(5)]
                        sqst = sqrows[o_r]
                        if gdn:
                            cch = {"gq": 0, "gk": 8, "gv": 16}[kind] + ch
                            xk = [(xn, ti) for ti in range(5)]
                            self.ts("dve", ya[:], xrow[:], cw[:, cch, 1:2], None, ALU.mult, None, xk + ["cw"], [yn])
                            for (s0, s1) in ((0, T), (T, NT)):
                                self.stt(ya[:, s0 + 1:s1], xrow[:, s0:s1 - 1], cw[:, cch, 0:1], ya[:, s0 + 1:s1], ALU.mult, ALU.add,
                                         xk + ["cw", yn], [yn])
                                self.stt(ya[:, s0:s1 - 1], xrow[:, s0 + 1:s1], cw[:, cch, 2:3], ya[:, s0:s1 - 1], ALU.mult, ALU.add,
                                         xk + ["cw", yn], [yn])
                        elif not roped:
                            self.act(sqrow[:], ob[:], AF.Square, okeys_now, [sn])

                        def epi(kind=kind, ch=ch, ob=ob, okey=okey, xrow=xrow, ya=ya, sqrow=sqrow, ob0=ob0, xn=xn, yn=yn, sn=sn, o0n=o0n,
                                roped=roped, gdn=gdn):
                            okeys = [(okey, ti) for ti in range(5)]
                            if roped:
                                for ti, (t0, n) in enumerate(TT[:4]):
                                    ba = 4 + (self.auxr % 3)
                                    self.auxr += 1
                                    self.mm(self.ps[:, ba, 0:n], permb[:], ob0[:, t0:t0 + n], True, True, [(o0n, ti), "permb"], [("ps", ba)])
                                    self.tt("dve", t1[:, 0:n], ob0[:, t0:t0 + n], rope[:, 0, t0:t0 + n], ALU.mult, [(o0n, ti), "rope"], ["t1"])
                                    self.tt("dve", t2[:, 0:n], self.ps[:, ba, 0:n], rope[:, 1, t0:t0 + n], ALU.mult, [("ps", ba), "rope"], ["t2"])
                                    self.tt("dve", ob[:, t0:t0 + n], t1[:, 0:n], t2[:, 0:n], ALU.add, ["t1", "t2"], [(okey, ti)])
                                self.act(sqrow[:], ob[:], AF.Square, okeys, [sn])
                            if kind == "gv":
                                self.act(ob[:], ya[:], AF.Silu, [yn], okeys)
                            elif gdn:
                                self.act(ya[:], ya[:], AF.Silu, [yn], [yn])
                                self.act(sqrow[:], ya[:], AF.Square, [yn], [sn])
                            if kind in ("gq", "gk"):
                                for ti, (t0, n) in enumerate(TT):
                                    ba = 4 + (self.auxr % 3)
                                    self.auxr += 1
                                    self.mm(self.ps[:, ba, 0:n], self.ones_b[:], sqrow[:, t0:t0 + n], True, True, [sn, "ones_b"], [("ps", ba)])
                                    self.rsqrt(rsr[:, 0:n], self.ps[:, ba, 0:n], 1.0, rst[:, 0:n], "rst", [("ps", ba), "eps"], ["rsr"])
                                    sc = (128.0 ** -0.5) if kind == "gq" else 1.0
                                    self.stt(ob[:, t0:t0 + n], ya[:, t0:t0 + n], sc, rsr[:, 0:n], ALU.mult, ALU.mult, [yn, "rsr"], [(okey, ti)])
                            if kind in ("dq", "dk"):
                                self.stats_row(ob, okeys, (0 if kind == "dq" else 16) + 2 * ch, True, sqrow, mx, sn, presq=True)
                            elif kind in ("nq", "nk"):
                                self.stats_row(ob, okeys, (0 if kind == "nq" else 8) + ch, False, sqrow, mx, sn, presq=True)
                            elif kind in ("wq", "wk"):
                                self.stats_row(ob, okeys, (16 if kind == "wq" else 24) + ch, False, sqrow, mx, sn, presq=True)
                            fm_dst = {"gq": self.qT_a, "gk": self.kT_a, "dq": self.qT_b, "dk": self.kT_b, "nq": self.qT_a, "nk": self.kT_a,
                                      "wq": self.qT_b, "wk": self.kT_b}.get(kind)
                            if fm_dst is not None:
                                self.dma("pool", fm_dst[ch], ob[:], okeys, [(kind, "fm", ch)])
                            if kind in ("gk", "gv"):
                                tr = 0
                                for m4 in range(0, NTILE, 4):
                                    ba = 4 + (self.auxr % 3)
                                    self.auxr += 1
                                    nm = min(4, NTILE - m4)
                                    for mm_ in range(nm):
                                        m = m4 + mm_
                                        self.mm(self.ps[:, ba, mm_ * 128:(mm_ + 1) * 128], ob[:, m * 128:(m + 1) * 128], self.ident_b[:], True, True,
                                                okeys + ["ident_b"], [("ps", ba)])
                                    self.cp("act" if (m4 // 4) % 2 else "dve", tms[tr][:, m4:m4 + nm, :],
                                            self.ps[:, ba, 0:nm * 128].rearrange("p (a b) -> p a b", b=128), [("ps", ba)], [("tms", tr, m4)])
                                tdst = self.k_tm if kind == "gk" else self.v_a
                                self.dma("pool", tdst[:, ch * 128:(ch + 1) * 128].rearrange("(t p) d -> p t d", p=128), tms[tr][:],
                                         [("tms", tr, m4) for m4 in range(0, NTILE, 4)], [(kind, "tm", ch)])
                        self.pipe_push(epi)
                self.pipe_flush()
                P.stage_end()

    def stage_outproj(self, l, tiles):
        P = self.P
        wv = self.w_out[l].rearrange("(k p) n -> p k n", p=128)
        hv = self.hT
        with ExitStack() as es:
            sb = lambda n, s, d: self.sb(es, n, s, d)
            yT = sb("yTs", [128, KC, NT], BF16)
            wb = [sb("wb", [128, KC, 512], BF16) for _ in range(2)]
            hrow = [sb("hrow", [128, NT], F32) for _ in range(3)]
            for k in range(KC):
                self.dma("sp", yT[:, k, :], self.yT[k], [], [("yT", k)])

            def load(g):
                self.dma("pool", wb[g % 2][:], wv[:, :, g * 512:(g + 1) * 512], [], [("wb", g % 2)])
            load(0)
            tmax = max(t0 + n for t0, n in tiles)
            mainr = 0
            for g in range(4):
                if g + 1 < 4:
                    load(g + 1)
                for j in range(4):
                    c = g * 4 + j
                    hr = c % 3
                    self.dma("sp", hrow[hr][:, 0:tmax], hv[c][:, 0:tmax], [], [("hrow", hr)])
                    for (t0, n) in tiles:
                        b = mainr % 8
                        mainr += 1
                        for k in range(KC):
                            self.mm(self.ps[:, b, 0:n], wb[g % 2][:, k, j * 128:(j + 1) * 128], yT[:, k, t0:t0 + n], k == 0, k == KC - 1,
                                    [("wb", g % 2), ("yT", k)], [("ps", b)])
                        self.resid_update(hrow[hr], c, t0, n, self.ps[:, b, 0:n], 32, [("ps", b), ("hrow", hr)], ("hrow", hr))
                    self.dma("pool", hv[c][:, 0:tmax], hrow[hr][:, 0:tmax], [("hrow", hr)], [("hTc", c)])
            P.stage_end()

    def stage_ffn(self, l, tiles, bg_mod=None):
        P = self.P
        wg = self.w_gate[l].rearrange("(k p) n -> p k n", p=128)
        wu = self.w_up[l].rearrange("(k p) n -> p k n", p=128)
        wd = self.w_down[l].rearrange("(f p) n -> p f n", p=128)
        hv = self.hT
        halves = [tiles[:2], tiles[2:]]
        for hi, half in enumerate(halves):
            if not half:
                continue
            base = half[0][0]
            ntok = sum(n for _, n in half)
            with ExitStack() as eo:
                actT = self.sb(eo, "actT", [128, FC, ntok], BF16)
                with ExitStack() as es:
                    sb = lambda n, s, d: self.sb(es, n, s, d)
                    u2 = sb("u2", [128, KC, ntok], BF16)
                    with ExitStack() as en:
                        self.norm_tiles(en, half, 1, u2, base, "u2")
                        P.stage_end()
                    wgb = [sb("wgb", [128, KC, 256], BF16) for _ in range(2)]
                    wub = [sb("wub", [128, KC, 256], BF16) for _ in range(2)]
                    sg = sb("sg", [128, 512], F32)
                    gen = self.mod_part(es, bg_mod, hi) if bg_mod is not None else None

                    def load(g):
                        self.dma("pool", wgb[g % 2][:], wg[:, :, g * 256:(g + 1) * 256], [], [("wgb", g % 2)])
                        self.dma("pool", wub[g % 2][:], wu[:, :, g * 256:(g + 1) * 256], [], [("wub", g % 2)])
                    load(0)
                    r = 0
                    for g in range(FC // 2):
                        if g + 1 < FC // 2:
                            load(g + 1)
                        for j in range(2):
                            f = g * 2 + j
                            for (t0, n) in half:
                                bg = (r % 3) * 2
                                bu = bg + 1
                                r += 1
                                for k in range(KC):
                                    self.mm(self.ps[:, bg, 0:n], wgb[g % 2][:, k, j * 128:(j + 1) * 128], u2[:, k, t0 - base:t0 - base + n],
                                            k == 0, k == KC - 1, [("wgb", g % 2)], [("ps", bg)])
                                for k in range(KC):
                                    self.mm(self.ps[:, bu, 0:n], wub[g % 2][:, k, j * 128:(j + 1) * 128], u2[:, k, t0 - base:t0 - base + n],
                                            k == 0, k == KC - 1, [("wub", g % 2)], [("ps", bu)])
                                self.act(sg[:, 0:n], self.ps[:, bg, 0:n], AF.Silu, [("ps", bg)], ["sg"])
                                self.tt("dve", actT[:, f, t0 - base:t0 - base + n], sg[:, 0:n], self.ps[:, bu, 0:n], ALU.mult,
                                        ["sg", ("ps", bu)], [("actT", f, t0)])
                        if gen is not None:
                            next(gen, None)
                    if gen is not None:
                        for _ in gen:
                            pass
                    P.stage_end()
                with ExitStack() as es:
                    sb = lambda n, s, d: self.sb(es, n, s, d)
                    wdb = [sb("wdb", [128, FC, 256], BF16) for _ in range(2)]
                    hrow = [sb("hrow", [128, ntok], F32) for _ in range(3)]

                    def loadd(g):
                        self.dma("pool", wdb[g % 2][:], wd[:, :, g * 256:(g + 1) * 256], [], [("wdb", g % 2)])
                    loadd(0)
                    r = 0
                    for g in range(8):
                        if g + 1 < 8:
                            loadd(g + 1)
                        for j in range(2):
                            c = g * 2 + j
                            hr = c % 3
                            self.dma("sp", hrow[hr][:], hv[c][:, base:base + ntok], [], [("hrow", hr)])
                            for (t0, n) in half:
                                b = r % 6
                                r += 1
                                for f in range(FC):
                                    self.mm(self.ps[:, b, 0:n], wdb[g % 2][:, f, j * 128:(j + 1) * 128], actT[:, f, t0 - base:t0 - base + n],
                                            f == 0, f == FC - 1, [("wdb", g % 2)], [("ps", b)])
                                s = 1 if t0 >= T else 0
                                self.stt(hrow[hr][:, t0 - base:t0 - base + n], self.ps[:, b, 0:n], self.modv[:, 80 + c, s:s + 1],
                                         hrow[hr][:, t0 - base:t0 - base + n], ALU.mult, ALU.add, [("ps", b), ("hrow", hr)], [("hrow", hr)])
                            self.dma("pool", hv[c][:, base:base + ntok], hrow[hr][:], [("hrow", hr)], [("hTc", c)])
                    P.stage_end()

    def stage_final(self):
        P = self.P
        hv = self.hT.rearrange("k p t -> p k t")
        with ExitStack() as es:
            sb = lambda n, s, d: self.sb(es, n, s, d)
            hb = [sb("hb", [128, KC, 512], F32) for _ in range(2)]
            sq = sb("sq", [128, KC, 512], BF16)
            rs = sb("rs", [128, 512], F32)
            rt = sb("rt", [128, 512], F32)
            fg = sb("fg", [128, KC], F32)
            ot = [sb("ot", [128, D], F32) for _ in range(2)]
            self.dma("sp", fg[:], self.fng, [], ["fg"])
            orr = 0
            self.fbank = 0
            for i, (t0, n) in enumerate(TT[:4]):
                r = i % 2
                self.dma("sp", hb[r][:], hv[:, :, t0:t0 + n], [], [("hb", r)])
                self.act(sq[:], hb[r][:], AF.Square, [("hb", r)], ["sq"])
                for k in range(KC):
                    self.mm(self.ps[:, 7, 0:n], self.ones_b[:], sq[:, k, :], k == 0, k == KC - 1, ["sq", "ones_b"], [("ps", 7)])
                self.rsqrt(rs[:], self.ps[:, 7, 0:n], 1.0 / D, rt[:], "rt", [("ps", 7), "eps"], ["rs"])
                self.tt("dve", hb[r][:], hb[r][:], rs[:].unsqueeze(1).to_broadcast([128, KC, n]), ALU.mult, [("hb", r), "rs"], [("hb", r)])
                self.tt("dve", hb[r][:], hb[r][:], fg[:].unsqueeze(2).to_broadcast([128, KC, n]), ALU.mult, [("hb", r), "fg"], [("hb", r)])
                for m in range(n // 128):
                    o = orr % 2
                    orr += 1
                    for g in range(4):
                        b = self.fbank % 7
                        self.fbank += 1
                        for j in range(4):
                            k = g * 4 + j
                            self.mm(self.ps[:, b, j * 128:(j + 1) * 128], hb[r][:, k, m * 128:(m + 1) * 128], self.ident_f[:], True, True,
                                    [("hb", r), "ident_f"], [("ps", b)])
                        self.cp("act" if g % 2 else "dve", ot[o][:, g * 512:(g + 1) * 512], self.ps[:, b, :], [("ps", b)], [("ot", o, g)])
                    tok = t0 + m * 128
                    self.dma("pool", self.out[tok:tok + 128, :], ot[o][:], [("ot", o, g) for g in range(4)], [("out", tok)])
            P.stage_end()

    def pipe_push(self, fn):
        self.pipe.append(fn)
        while len(self.pipe) > self.pipe_depth:
            self.pipe.pop(0)()

    def pipe_flush(self):
        while self.pipe:
            self.pipe.pop(0)()

    def attn_keys(self, pT, q_ap, n, ktiles, bias_ap, scale, slot, qkeys):
        last = len(ktiles) - 1
        for idx, (k_ap, v_ap, mask_ap, rk) in enumerate(ktiles):
            bs = self.sr % 3
            self.sr += 1
            self.mm(self.ps[:, bs, 0:n], k_ap, q_ap, True, True, qkeys + rk, [("ps", bs)])
            pr = self.pr % 4
            self.pr += 1
            self.act(pT[pr][:, 0:n], self.ps[:, bs, 0:n], AF.Exp, [("ps", bs), "mbias"], [("pT", pr)], bias=bias_ap, scale=scale)
            if mask_ap is not None:
                self.tt("dve", pT[pr][:, 0:n], pT[pr][:, 0:n], mask_ap, ALU.mult, [("pT", pr), "mask"], [("pT", pr)])

            def second(idx=idx, pr=pr, v_ap=v_ap, rk=rk):
                self.mm(self.ps[:, 3 + slot, 0:n], v_ap, pT[pr][:, 0:n], idx == 0, idx == last, [("pT", pr)] + rk, [("ps", 3 + slot)])
                self.mm(self.ps[:, 5 + slot, 0:n], self.ones_b[:], pT[pr][:, 0:n], idx == 0, idx == last, [("pT", pr)], [("ps", 5 + slot)])
            self.pipe_push(second)

    def score_bounds(self, es, pairs, scale):
        tmp = self.sb(es, "mtmp", [128, 64], F32)
        for slot, qs, ks in pairs:
            self.tt("dve", tmp[:, slot:slot + 1], self.nrm2[:, qs:qs + 1], self.nrm2[:, ks:ks + 1], ALU.mult, [], [("mtmp", slot)])
            self.act(tmp[:, slot:slot + 1], tmp[:, slot:slot + 1], AF.Sqrt, [("mtmp", slot)], [("mtmp", slot)])
            self.ts("dve", self.mbias[:, slot:slot + 1], tmp[:, slot:slot + 1], -scale, None, ALU.mult, None, [("mtmp", slot)], ["mbias"])

    def stage_diff(self, l, need_ctx):
        P = self.P
        i2 = l // 2
        lam_init = 0.8 - 0.6 * math.exp(-0.3 * l)
        scale = 64.0 ** -0.5
        with ExitStack() as es:
            sb = lambda n, s, d: self.sb(es, n, s, d)
            qT = [sb("qT", [128, NT], BF16) for _ in range(2)]
            qz = [[sb("qz", [128, NT], BF16) for _ in range(2)] for _ in range(2)]
            kT = [sb("kT", [128, NT], BF16) for _ in range(2)]
            V = [sb("V", [128, NTILE, 128], BF16) for _ in range(2)]
            pT = [sb("pT", [128, 512], BF16) for _ in range(4)]
            ybuf = [sb("ybuf", [128, NT], BF16) for _ in range(2)]
            r0 = sb("r0", [128, 512], F32)
            o0 = sb("o0", [128, 512], F32)
            r1 = sb("r1", [128, 512], F32)
            o1 = sb("o1", [128, 512], F32)
            od = sb("od", [128, 512], F32)
            sq = sb("sqd", [128, 512], BF16)
            rs = sb("rsd", [128, 512], F32)
            rt = sb("rtd", [128, 512], F32)
            lamv = sb("lamv", [128, 4, 64], F32)
            lp = sb("lp", [128, 2, 64], F32)
            le = sb("le", [128, 2], F32)
            lamc = sb("lamc", [128, 1], F32)
            sgc = sb("sgc", [128, 1], F32)
            self.sr = 0
            self.pr = 0
            self.pipe = []
            self.pipe_depth = 2
            self.score_bounds(es, [(h * 2 + c, h * 2 + c, 16 + h * 2 + c) for h in range(8) for c in range(2)], scale)
            self.dma("sp", lamv[:], self.lam[i2], [], ["lamv"])
            self.dma("sp", sgc[:], self.subln[i2], [], ["sgc"])
            self.tt("dve", lp[:, 0, :], lamv[:, 0, :], lamv[:, 1, :], ALU.mult, ["lamv"], ["lp"])
            self.tt("dve", lp[:, 1, :], lamv[:, 2, :], lamv[:, 3, :], ALU.mult, ["lamv", "lp"], ["lp"])
            P.op("dve", lambda e: e.tensor_reduce(out=le[:], in_=lp[:], axis=AX.X, op=ALU.add), ["lp"], ["le"])
            self.act(le[:], le[:], AF.Exp, ["le"], ["le"])
            self.tt("dve", lamc[:], le[:, 0:1], le[:, 1:2], ALU.subtract, ["le"], ["lamc"])
            self.ts("dve", lamc[:], lamc[:], lam_init, None, ALU.add, None, ["lamc"], ["lamc"])
            self.ts("dve", sgc[:], sgc[:], 1.0 - lam_init, None, ALU.mult, None, ["sgc"], ["sgc"])
            for c in range(2):
                for r in range(2):
                    P.op("pool", lambda e, c=c, r=r: e.memset(qz[r][c][:], 0.0), [], [("qz", r, c)])
            qtiles = TT if need_ctx else TT[:4]
            for h in range(8):
                r = h % 2
                self.dma("sp", qT[r][:], self.qT_b[h], [], [("qT", r)])
                self.dma("sp", kT[r][:], self.kT_b[h], [], [("kT", r)])
                self.dma("sp", V[r][:], self.v_b[:, h * 128:(h + 1) * 128].rearrange("(t p) d -> p t d", p=128), [], [("V", r)])
                for c in range(2):
                    self.cp("pool", qz[r][c][c * 64:(c + 1) * 64, :], qT[r][c * 64:(c + 1) * 64, :], [("qT", r)], [("qz", r, c)])
                for (t0, n) in qtiles:
                    kts = list(range(NTILE)) if t0 < T else [16, 17]
                    for c in range(2):
                        kl = [(kT[r][:, m * 128:(m + 1) * 128], V[r][:, m, :], None, [("kT", r), ("V", r)]) for m in kts]
                        self.attn_keys(pT, qz[r][c][:, t0:t0 + n], n, kl, self.mbias[:, h * 2 + c:h * 2 + c + 1], scale, c, [("qz", r, c)])

                    def epi(t0=t0, n=n, r=r):
                        self.recip(r0[:, 0:n], self.ps[:, 5, 0:n], [("ps", 5)], ["r0"])
                        self.tt("dve", o0[:, 0:n], self.ps[:, 3, 0:n], r0[:, 0:n], ALU.mult, [("ps", 3), "r0"], ["o0"])
                        self.recip(r1[:, 0:n], self.ps[:, 6, 0:n], [("ps", 6)], ["r1"])
                        self.ts("dve", r1[:, 0:n], r1[:, 0:n], lamc[:, 0:1], None, ALU.mult, None, ["r1", "lamc"], ["r1"])
                        self.tt("dve", o1[:, 0:n], self.ps[:, 4, 0:n], r1[:, 0:n], ALU.mult, [("ps", 4), "r1"], ["o1"])
                        self.tt("dve", od[:, 0:n], o0[:, 0:n], o1[:, 0:n], ALU.subtract, ["o0", "o1"], ["od"])
                        self.act(sq[:, 0:n], od[:, 0:n], AF.Square, ["od"], ["sqd"])
                        self.mm(self.ps[:, 7, 0:n], self.ones_b[:], sq[:, 0:n], True, True, ["sqd"], [("ps", 7)])
                        self.rsqrt(rs[:, 0:n], self.ps[:, 7, 0:n], 1.0 / 128, rt[:, 0:n], "rtd", [("ps", 7)], ["rsd"])
                        self.stt(ybuf[r][:, t0:t0 + n], od[:, 0:n], sgc[:, 0:1], rs[:, 0:n], ALU.mult, ALU.mult, ["od", "rsd", "sgc"], [("ybuf", r)])
                    self.pipe_push(epi)
                tmax = NT if need_ctx else T

                def store(h=h, r=r, tmax=tmax):
                    self.dma("pool", self.yT[8 + h][:, 0:tmax], ybuf[r][:, 0:tmax], [("ybuf", r)], [("yT", 8 + h)])
                self.pipe_push(store)
            self.pipe_flush()
            P.stage_end()

    def stage_na(self, l, need_ctx):
        P = self.P
        i2 = l // 2
        scale = 128.0 ** -0.5
        with ExitStack() as es:
            sb = lambda n, s, d: self.sb(es, n, s, d)
            qT = [sb("qT", [128, NT], BF16) for _ in range(2)]
            kT = [sb("kT", [128, NT], BF16) for _ in range(2)]
            V = [sb("V", [128, NTILE, 128], BF16) for _ in range(2)]
            Bf = sb("Bf", [128, N_NA_MASK, 128], F32)
            E = [sb("E", [128, N_NA_MASK, 128], BF16) for _ in range(2)]
            pT = [sb("pT", [128, 512], BF16) for _ in range(4)]
            ybuf = [sb("ybuf", [128, NT], BF16) for _ in range(2)]
            rr = sb("rr", [128, 512], F32)
            self.sr = 0
            self.pr = 0
            self.pipe = []
            self.pipe_depth = 2
            self.score_bounds(es, [(h, h, 8 + h) for h in range(8)], scale)
            for h in range(8):
                r = h % 2
                self.dma("sp", qT[r][:], self.qT_a[h], [], [("qT", r)])
                self.dma("sp", kT[r][:], self.kT_a[h], [], [("kT", r)])
                self.dma("sp", V[r][:], self.v_a[:, h * 128:(h + 1) * 128].rearrange("(t p) d -> p t d", p=128), [], [("V", r)])
                self.dma("sp", Bf[:], self.na_bias[i2, h], [], ["Bf"])
                self.act(E[r][:], Bf[:], AF.Exp, ["Bf"], [("E", r)])
                blocks = [(a * 128, 128, a) for a in range(16)]
                if need_ctx:
                    blocks.append((T, C, None))
                for bi, (t0, n, a) in enumerate(blocks):
                    rk = [("kT", r), ("V", r)]
                    if a is None:
                        kl = [(kT[r][:, m * 128:(m + 1) * 128], V[r][:, m, :], None, rk) for m in (16, 17)]
                    else:
                        kl = [(kT[r][:, t * 128:(t + 1) * 128], V[r][:, t, :], E[r][:, mid, :], rk + [("E", r)]) for (t, mid) in _NA_PLAN[a]]
                        kl += [(kT[r][:, m * 128:(m + 1) * 128], V[r][:, m, :], None, rk) for m in (16, 17)]
                    slot = bi % 2
                    self.attn_keys(pT, qT[r][:, t0:t0 + n], n, kl, self.mbias[:, h:h + 1], scale, slot, [("qT", r)])

                    def epi(t0=t0, n=n, r=r, slot=slot):
                        self.recip(rr[:, 0:n], self.ps[:, 5 + slot, 0:n], [("ps", 5 + slot)], ["rr"])
                        self.tt("dve", ybuf[r][:, t0:t0 + n], self.ps[:, 3 + slot, 0:n], rr[:, 0:n], ALU.mult, [("ps", 3 + slot), "rr"], [("ybuf", r)])
                    self.pipe_push(epi)
                tmax = NT if need_ctx else T

                def store(h=h, r=r, tmax=tmax):
                    self.dma("pool", self.yT[h][:, 0:tmax], ybuf[r][:, 0:tmax], [("ybuf", r)], [("yT", h)])
                self.pipe_push(store)
            self.pipe_flush()
            P.stage_end()

    def stage_win(self, l, need_ctx):
        P = self.P
        i2 = l // 2
        scale = 128.0 ** -0.5
        with ExitStack() as es:
            sb = lambda n, s, d: self.sb(es, n, s, d)
            qT = [sb("qT", [128, NT], BF16) for _ in range(2)]
            kT = [sb("kT", [128, NT], BF16) for _ in range(2)]
            V = [sb("V", [128, NTILE, 128], BF16) for _ in range(2)]
            wmf = sb("wmf", [128, 6, 512], F32)
            wm = sb("wm", [128, 6, 512], BF16)
            pT = [sb("pT", [128, 512], BF16) for _ in range(4)]
            ybuf = [sb("ybuf", [128, NT], BF16) for _ in range(2)]
            rr = sb("rr", [128, 512], F32)
            sk = sb("sk", [128, 8], F32)
            esk = sb("esk", [128, 8], F32)
            self.sr = 0
            self.pr = 0
            self.pipe = []
            self.pipe_depth = 2
            self.score_bounds(es, [(16 + h, 16 + h, 24 + h // 4) for h in range(8)], scale)
            self.dma("sp", wmf[:], self.c_win.rearrange("r p q -> p r q"), [], ["wmf"])
            self.cp("dve", wm[:], wmf[:], ["wmf"], ["mask"])
            self.dma("sp", sk[:], self.sink[i2], [], ["sk"])
            for h in range(8):
                r = h % 2
                kv = h // 4
                self.dma("sp", qT[r][:], self.qT_b[h], [], [("qT", r)])
                self.dma("sp", kT[r][:], self.kT_b[kv], [], [("kT", r)])
                self.dma("sp", V[r][:], self.v_b[:, kv * 128:(kv + 1) * 128].rearrange("(t p) d -> p t d", p=128), [], [("V", r)])
                self.act(esk[:, h:h + 1], sk[:, h:h + 1], AF.Exp, ["sk", "mbias"], [("esk", h)], bias=self.mbias[:, 16 + h:17 + h])
                blocks = [(b4 * 512, 512, b4) for b4 in range(4)]
                if need_ctx:
                    blocks.append((T, C, None))
                for bi, (t0, n, b4) in enumerate(blocks):
                    rk = [("kT", r), ("V", r)]
                    kl = []
                    if b4 is not None:
                        for t in range(4 * b4 - 1, 4 * b4 + 5):
                            if 0 <= t < 16:
                                kl.append((kT[r][:, t * 128:(t + 1) * 128], V[r][:, t, :], wm[:, t - 4 * b4 + 1, :], rk))
                    kl += [(kT[r][:, m * 128:(m + 1) * 128], V[r][:, m, :], None, rk) for m in (16, 17)]
                    slot = bi % 2
                    self.attn_keys(pT, qT[r][:, t0:t0 + n], n, kl, self.mbias[:, 16 + h:17 + h], scale, slot, [("qT", r)])

                    def epi(t0=t0, n=n, r=r, slot=slot, h=h):
                        self.ts("dve", rr[:, 0:n], self.ps[:, 5 + slot, 0:n], esk[:, h:h + 1], None, ALU.add, None, [("ps", 5 + slot), ("esk", h)], ["rr"])
                        self.recip(rr[:, 0:n], rr[:, 0:n], ["rr"], ["rr"])
                        self.tt("dve", ybuf[r][:, t0:t0 + n], self.ps[:, 3 + slot, 0:n], rr[:, 0:n], ALU.mult, [("ps", 3 + slot), "rr"], [("ybuf", r)])
                    self.pipe_push(epi)
                tmax = NT if need_ctx else T

                def store(h=h, r=r, tmax=tmax):
                    self.dma("pool", self.yT[8 + h][:, 0:tmax], ybuf[r][:, 0:tmax], [("ybuf", r)], [("yT", 8 + h)])
                self.pipe_push(store)
            self.pipe_flush()
            P.stage_end()

    def stage_gdn(self, l):
        P = self.P
        i2 = l // 2
        NDT = self.ndt
        orders = [[16, 17] + list(range(16)), [17, 16] + list(range(15, -1, -1))]
        with ExitStack() as es:
            sb = lambda n, s, d: self.sb(es, n, s, d)
            tri = sb("tri", [128, 4, 128], F32)
            trin = sb("trin", [128, 4, 128], NDT) if NDT != F32 else tri
            identn = self.ident_f if NDT == F32 else self.ident_b
            ab = sb("ab", [128, NTILE, 32], F32)
            gvec = sb("gvec", [128, 32], F32)
            negA = sb("negA", [128, 16], F32)
            gsb = sb("gsb", [128, NTILE, 16], F32)
            bsb = sb("bsb", [128, NTILE, 16], F32)
            nbs = sb("nbs", [128, NTILE, 16], F32)
            tot = sb("tot", [128, NTILE, 16], F32)
            glast = sb("glast", [128, NTILE, 16], F32)
            gc = sb("gc", [128, NTILE, 16], F32)
            eg = sb("eg", [128, NTILE, 16], F32)
            kd = sb("kd", [128, NTILE, 16], F32)
            bg = sb("bg", [128, NTILE, 16], F32)
            S = [[sb("S", [128, 4, 128], F32) for _ in range(2)] for _ in range(2)]
            Sb = [[sb("Sb", [128, 4, 128], BF16) for _ in range(2)] for _ in range(2)]
            self.dma("sp", tri[:], self.c_tri.rearrange("a p f -> p a f"), [], ["tri"])
            if NDT != F32:
                self.cp("dve", trin[:], tri[:], ["tri"], ["trin"])
            self.dma("sp", ab[:], self.gab.rearrange("(t p) c -> p t c", p=128), [], ["ab"])
            self.dma("sp", gvec[:], self.gdn_vec[i2], [], ["gvec"])
            self.act(negA[:], gvec[:, 0:16], AF.Exp, ["gvec"], ["negA"])
            self.ts("dve", negA[:], negA[:], -1.0, None, ALU.mult, None, ["negA"], ["negA"])
            self.tt("dve", gsb[:], ab[:, :, 0:16], gvec[:, 16:32].unsqueeze(1).to_broadcast([128, NTILE, 16]), ALU.add, ["ab", "gvec"], ["gsb"])
            self.act(gsb[:], gsb[:], AF.Exp, ["gsb"], ["gsb"])
            self.act(gsb[:], gsb[:], AF.Ln, ["gsb"], ["gsb"], bias=1.0, scale=1.0)
            self.tt("dve", gsb[:], gsb[:], negA[:].unsqueeze(1).to_broadcast([128, NTILE, 16]), ALU.mult, ["gsb", "negA"], ["gsb"])
            self.act(bsb[:], ab[:, :, 16:32], AF.Sigmoid, ["ab"], ["bsb"])
            self.ts("dve", nbs[:], bsb[:], -1.0, None, ALU.mult, None, ["bsb"], ["nbs"])
            gflat = gsb[:].rearrange("p t c -> p (t c)")
            self.mm(self.ps[:, 0, 0:NTILE * 16], self.ones_f[:], gflat, True, True, ["gsb", "ones_f"], [("ps", 0)])
            self.cp("dve", tot[:].rearrange("p t c -> p (t c)"), self.ps[:, 0, 0:NTILE * 16], [("ps", 0)], ["tot"])
            self.act(glast[:], tot[:], AF.Exp, ["tot"], ["glast"])
            for d in range(2):
                self.mm(self.ps[:, 1 + d, 0:NTILE * 8], tri[:, d, :], gsb[:, :, d * 8:(d + 1) * 8], True, True, ["gsb", "tri"], [("ps", 1 + d)])
                self.cp("dve", gc[:, :, d * 8:(d + 1) * 8], self.ps[:, 1 + d, 0:NTILE * 8].rearrange("p (t c) -> p t c", c=8), [("ps", 1 + d)], ["gc"])
            self.act(eg[:], gc[:], AF.Exp, ["gc"], ["eg"])
            self.tt("dve", kd[:], tot[:], gc[:], ALU.subtract, ["tot", "gc"], ["kd"])
            self.act(kd[:], kd[:], AF.Exp, ["kd"], ["kd"])
            self.tt("dve", bg[:], bsb[:], eg[:], ALU.mult, ["bsb", "eg"], ["bg"])
            for d in range(2):
                for hg in range(2):
                    P.op("pool", lambda e, d=d, hg=hg: e.memset(S[d][hg][:], 0.0), [], [("S", d, hg)])
                    P.op("pool", lambda e, d=d, hg=hg: e.memset(Sb[d][hg][:], 0.0), [], [("Sb", d, hg)])
            qTt = [[sb("qTt", [128, 8, 128], BF16) for _ in range(2)] for _ in range(2)]
            kTt = [[sb("kTt", [128, 8, 128], BF16) for _ in range(2)] for _ in range(2)]
            ktm = [[sb("ktm", [128, 8, 128], BF16) for _ in range(2)] for _ in range(1)]
            vtm = [[sb("vtm", [128, 8, 128], BF16) for _ in range(2)] for _ in range(1)]
            vb = [[sb("vb", [128, 8, 128], BF16) for _ in range(2)] for _ in range(2)]
            kbg = [[sb("kbg", [128, 8, 128], BF16) for _ in range(2)] for _ in range(2)]
            kdc = [[sb("kdc", [128, 8, 128], BF16) for _ in range(2)] for _ in range(2)]
            Ug = [[sb("Ug", [128, 8, 128], F32) for _ in range(2)] for _ in range(1)]
            def slotbufs(name, dt, nring):
                return [[[sb(name, [128, 4, 128], dt) for _ in range(nring)] for _ in range(2)] for _ in range(2)]
            Eb = slotbufs("Eb", F32, 1)
            ETb = slotbufs("ETb", F32, 1)
            Pb = slotbufs("Pb", NDT, 2)
            PTb = slotbufs("PTb", NDT, 2)
            RTb = slotbufs("RTb", NDT, 2)
            TTb = slotbufs("TTb", BF16, 1)
            aT = slotbufs("aT", BF16, 2)
            ub = slotbufs("ub", F32, 2)
            wT = slotbufs("wT", BF16, 2)
            vn = slotbufs("vn", BF16, 1)
            ot = slotbufs("ot", F32, 1)
            oo = slotbufs("oo", F32, 1)
            qv = self.qT_a.rearrange("h p t -> p h t")
            kv = self.kT_a.rearrange("h p t -> p h t")
            self.bank = 0

            def nb():
                b = self.bank % 8
                self.bank += 1
                return b

            def bc_h(ap2):
                return ap2.unsqueeze(1).to_broadcast([128, 4, 128])

            def bc_e(ap1):
                return ap1.unsqueeze(2).to_broadcast([128, ap1.shape[1], 128])

            def prep_ug(s):
                for d in range(2):
                    c = orders[d][s]
                    cs = slice(d * 8, (d + 1) * 8)
                    self.tt("dve", Ug[0][d][:], tri[:, d, :].unsqueeze(1).to_broadcast([128, 8, 128]), bc_e(gsb[:, c, cs]), ALU.mult,
                            ["tri", "gsb"], [("Ug", d)])

            def precompute(s):
                sr = s % 2
                for d in range(2):
                    c = orders[d][s]
                    tk = (sr, d)
                    self.dma("sp", qTt[sr][d][:], qv[:, :, c * 128:(c + 1) * 128], [], [("qTt",) + tk])
                    self.dma("sp", kTt[sr][d][:], kv[:, :, c * 128:(c + 1) * 128], [], [("kTt",) + tk])
                    self.dma("sp", ktm[0][d][:].rearrange("p h e -> p (h e)"), self.k_tm[c * 128:(c + 1) * 128, :], [], [("ktm", d)])
                    self.dma("sp", vtm[0][d][:].rearrange("p h e -> p (h e)"), self.v_a[c * 128:(c + 1) * 128, :], [], [("vtm", d)])
                    cs = slice(d * 8, (d + 1) * 8)
                    self.tt("dve", vb[sr][d][:], vtm[0][d][:], bc_e(bsb[:, c, cs]), ALU.mult, [("vtm", d), "bsb"], [("vb",) + tk])
                    self.tt("pool", kbg[sr][d][:], ktm[0][d][:], bc_e(bg[:, c, cs]), ALU.mult, [("ktm", d), "bg"], [("kbg",) + tk])
                    self.tt("pool", kdc[sr][d][:], ktm[0][d][:], bc_e(kd[:, c, cs]), ALU.mult, [("ktm", d), "kd"], [("kdc",) + tk])
                slots = [(d, hg) for d in range(2) for hg in range(2)]
                lvl = getattr(self, "gdn_lvl", 9)
                if lvl <= 1:
                    return
                for (d, hg) in slots:
                    c = orders[d][s]
                    tk = (sr, d)
                    sk = (d, hg)
                    hs = range(hg * 4, hg * 4 + 4)
                    bD, bDT, bG, bR = nb(), nb(), nb(), nb()
                    for j, h in enumerate(hs):
                        self.mm(self.ps[:, bD, j * 128:(j + 1) * 128], Ug[0][d][:, h, :], tri[:, 2 + d, :], True, True, [("Ug", d), "tri"], [("ps", bD)])
                    for j, h in enumerate(hs):
                        self.mm(self.ps[:, bDT, j * 128:(j + 1) * 128], tri[:, 2 + d, :], Ug[0][d][:, h, :], True, True, [("Ug", d), "tri"], [("ps", bDT)])
                    for j, h in enumerate(hs):
                        self.mm(self.ps[:, bG, j * 128:(j + 1) * 128], kTt[sr][d][:, h, :], kTt[sr][d][:, h, :], True, True, [("kTt",) + tk], [("ps", bG)])
                    for j, h in enumerate(hs):
                        self.mm(self.ps[:, bR, j * 128:(j + 1) * 128], kTt[sr][d][:, h, :], qTt[sr][d][:, h, :], True, True,
                                [("kTt",) + tk, ("qTt",) + tk], [("ps", bR)])
                    E = Eb[d][hg][0]
                    ET = ETb[d][hg][0]
                    p4 = lambda b: self.ps[:, b, :].rearrange("p (h e) -> p h e", h=4)
                    self.act(E[:], p4(bD), AF.Exp, [("ps", bD)], [("E",) + sk])
                    self.tt("dve", E[:], E[:], bc_h(tri[:, 2 + d, :]), ALU.mult, [("E",) + sk, "tri"], [("E",) + sk])
                    self.tt("dve", E[:], p4(bG), E[:], ALU.mult, [("ps", bG), ("E",) + sk], [("E",) + sk])
                    P0 = Pb[d][hg][0]
                    hsl = slice(d * 8 + hg * 4, d * 8 + hg * 4 + 4)
                    self.tt("dve", P0[:], E[:], bc_e(nbs[:, c, hsl]), ALU.mult, [("E",) + sk, "nbs"], [("P", 0) + sk])
                    self.act(ET[:], p4(bDT), AF.Exp, [("ps", bDT)], [("ET",) + sk])
                    self.tt("dve", ET[:], ET[:], bc_h(tri[:, d, :]), ALU.mult, [("ET",) + sk, "tri"], [("ET",) + sk])
                    self.tt("dve", aT[d][hg][sr][:], p4(bR), ET[:], ALU.mult, [("ps", bR), ("ET",) + sk], [("aT", sr) + sk])
                    bT = nb()
                    for j in range(4):
                        self.mm(self.ps[:, bT, j * 128:(j + 1) * 128], P0[:, j, :], identn[:], True, True, [("P", 0) + sk], [("ps", bT)])
                    self.cp("act", PTb[d][hg][0][:], p4(bT), [("ps", bT)], [("PT", 0) + sk])
                    self.tt("dve", RTb[d][hg][0][:], p4(bT), bc_h(identn[:]), ALU.add, [("ps", bT)], [("RT", 0) + sk])
                if lvl <= 2:
                    return
                for k in range(1, 7):
                    cur, prv = k % 2, (k - 1) % 2
                    for (d, hg) in slots:
                        sk = (d, hg)
                        p4 = lambda b: self.ps[:, b, :].rearrange("p (h e) -> p h e", h=4)
                        Pp, PTp = Pb[d][hg][prv], PTb[d][hg][prv]
                        Pn, PTn = Pb[d][hg][cur], PTb[d][hg][cur]
                        bP = nb()
                        for j in range(4):
                            self.mm(self.ps[:, bP, j * 128:(j + 1) * 128], PTp[:, j, :], Pp[:, j, :], True, True,
                                    [("P", prv) + sk, ("PT", prv) + sk], [("ps", bP)])
                        if k < 6:
                            bPT = nb()
                            for j in range(4):
                                self.mm(self.ps[:, bPT, j * 128:(j + 1) * 128], Pp[:, j, :], PTp[:, j, :], True, True,
                                        [("P", prv) + sk, ("PT", prv) + sk], [("ps", bPT)])
                        self.cp("act", Pn[:], p4(bP), [("ps", bP)], [("P", cur) + sk])
                        if k < 6:
                            self.cp("act", PTn[:], p4(bPT), [("ps", bPT)], [("PT", cur) + sk])
                        bRT = nb()
                        for j in range(4):
                            self.mm(self.ps[:, bRT, j * 128:(j + 1) * 128], Pn[:, j, :], RTb[d][hg][prv][:, j, :], True, True,
                                    [("P", cur) + sk, ("RT", prv) + sk], [("ps", bRT)])
                        self.tt("dve", RTb[d][hg][cur][:], p4(bRT), RTb[d][hg][prv][:], ALU.add, [("ps", bRT), ("RT", prv) + sk], [("RT", cur) + sk])
                if lvl <= 3:
                    return
                for (d, hg) in slots:
                    sk = (d, hg)
                    tk = (sr, d)
                    p4 = lambda b: self.ps[:, b, :].rearrange("p (h e) -> p h e", h=4)
                    self.cp("act", TTb[d][hg][0][:], RTb[d][hg][0][:], [("RT", 0) + sk], [("TT",) + sk])
                    bU, bW = nb(), nb()
                    for j in range(4):
                        h = hg * 4 + j
                        self.mm(self.ps[:, bU, j * 128:(j + 1) * 128], TTb[d][hg][0][:, j, :], vb[sr][d][:, h, :], True, True,
                                [("TT",) + sk, ("vb",) + tk], [("ps", bU)])
                    for j in range(4):
                        h = hg * 4 + j
                        self.mm(self.ps[:, bW, j * 128:(j + 1) * 128], kbg[sr][d][:, h, :], TTb[d][hg][0][:, j, :], True, True,
                                [("TT",) + sk, ("kbg",) + tk], [("ps", bW)])
                    self.cp("act", ub[d][hg][sr][:], p4(bU), [("ps", bU)], [("ub", sr) + sk])
                    self.cp("dve", wT[d][hg][sr][:], p4(bW), [("ps", bW)], [("wT", sr) + sk])

            def recur(s):
                sr = s % 2
                for d in range(2):
                    c = orders[d][s]
                    tk = (sr, d)
                    for hg in range(2):
                        sk = (d, hg)
                        p4 = lambda b: self.ps[:, b, :].rearrange("p (h e) -> p h e", h=4)
                        hsl = slice(d * 8 + hg * 4, d * 8 + hg * 4 + 4)
                        bW, bO1, bO2, bS = nb(), nb(), nb(), nb()
                        for j in range(4):
                            self.mm(self.ps[:, bW, j * 128:(j + 1) * 128], wT[d][hg][sr][:, j, :], Sb[d][hg][:, j, :], True, True,
                                    [("wT", sr) + sk, ("Sb",) + sk], [("ps", bW)])
                        for j in range(4):
                            h = hg * 4 + j
                            self.mm(self.ps[:, bO1, j * 128:(j + 1) * 128], qTt[sr][d][:, h, :], Sb[d][hg][:, j, :], True, True,
                                    [("qTt",) + tk, ("Sb",) + sk], [("ps", bO1)])
                        V_ = vn[d][hg][0]
                        self.tt("dve", V_[:], ub[d][hg][sr][:], p4(bW), ALU.subtract, [("ub", sr) + sk, ("ps", bW)], [("vn",) + sk])
                        for j in range(4):
                            self.mm(self.ps[:, bO2, j * 128:(j + 1) * 128], aT[d][hg][sr][:, j, :], V_[:, j, :], True, True,
                                    [("aT", sr) + sk, ("vn",) + sk], [("ps", bO2)])
                        for j in range(4):
                            h = hg * 4 + j
                            self.mm(self.ps[:, bS, j * 128:(j + 1) * 128], kdc[sr][d][:, h, :], V_[:, j, :], True, True,
                                    [("kdc",) + tk, ("vn",) + sk], [("ps", bS)])
                        O_ = oo[d][hg][0]
                        self.tt("dve", ot[d][hg][0][:], p4(bO1), bc_e(eg[:, c, hsl]), ALU.mult, [("ps", bO1), "eg"], [("ot",) + sk])
                        self.tt("dve", O_[:], ot[d][hg][0][:], p4(bO2), ALU.add, [("ot",) + sk, ("ps", bO2)], [("oo",) + sk])
                        self.dma("pool", self.go[d, c * 128:(c + 1) * 128, hg * 512:(hg + 1) * 512], O_[:].rearrange("p h e -> p (h e)"),
                                 [("oo",) + sk], [("go", d, c, hg)])
                        self.tt("dve", S[d][hg][:], S[d][hg][:], bc_e(glast[:, c, hsl]), ALU.mult, [("S",) + sk, "glast"], [("S",) + sk])
                        self.tt("dve", S[d][hg][:], S[d][hg][:], p4(bS), ALU.add, [("S",) + sk, ("ps", bS)], [("S",) + sk])
                        self.cp("act", Sb[d][hg][:], S[d][hg][:], [("S",) + sk], [("Sb",) + sk])

            stop = getattr(self, "gdn_stop", None)
            if stop != "pre":
                prep_ug(0)
                precompute(0)
                prep_ug(1)
            nsteps = NTILE if stop is None else (0 if stop in ("pre", "pc0") else int(stop))
            for s in range(nsteps):
                if s + 1 < NTILE:
                    precompute(s + 1)
                if s + 2 < NTILE:
                    prep_ug(s + 2)
                recur(s)
            P.stage_end()
        with ExitStack() as es:
            sb = lambda n, s, d: self.sb(es, n, s, d)
            of = [sb("of", [128, 8, 128], F32) for _ in range(2)]
            obk = [sb("obk", [128, 8, 128], F32) for _ in range(2)]
            zb = [sb("zb", [128, 8, 128], BF16) for _ in range(2)]
            sz = sb("sz", [128, 8, 128], F32)
            sq = sb("sqg", [128, 8, 128], F32)
            ss = sb("ss", [128, 8], F32)
            st = sb("sst", [128, 8], F32)
            an = sb("an", [128, 8, 128], BF16)
            gn = sb("gn", [128, 128], F32)
            ybuf = sb("ybuf8", [128, 8, NT], BF16)
            self.dma("sp", gn[:], self.gng[i2], [], ["gn"])
            bank = 0
            for m in range(NTILE):
                r = m % 2
                rows = slice(m * 128, (m + 1) * 128)
                self.dma("sp", of[r][:].rearrange("p h e -> p (h e)"), self.go[0, rows, :], [], [("of", r)])
                self.dma("sp", obk[r][:].rearrange("p h e -> p (h e)"), self.go[1, rows, :], [], [("obk", r)])
                self.dma("sp", zb[r][:].rearrange("p h e -> p (h e)"), self.gz[rows, :], [], [("zb", r)])
                self.tt("dve", of[r][:], of[r][:], obk[r][:], ALU.add, [("of", r), ("obk", r)], [("of", r)])
                self.act(sq[:], of[r][:], AF.Square, [("of", r)], ["sqg"])
                P.op("dve", lambda e: e.tensor_reduce(out=ss[:], in_=sq[:], axis=AX.X, op=ALU.add), ["sqg"], ["ss"])
                self.rsqrt(ss[:], ss[:], 1.0 / 128, st[:], "sst", ["ss"], ["ss"])
                self.tt("dve", of[r][:], of[r][:], ss[:].unsqueeze(2).to_broadcast([128, 8, 128]), ALU.mult, [("of", r), "ss"], [("of", r)])
                self.tt("dve", of[r][:], of[r][:], gn[:].unsqueeze(1).to_broadcast([128, 8, 128]), ALU.mult, [("of", r), "gn"], [("of", r)])
                self.act(sz[:], zb[r][:], AF.Silu, [("zb", r)], ["sz"])
                self.tt("dve", an[:], of[r][:], sz[:], ALU.mult, [("of", r), "sz"], ["an"])
                for hg in range(2):
                    b = bank % 8
                    bank += 1
                    for j in range(4):
                        self.mm(self.ps[:, b, j * 128:(j + 1) * 128], an[:, hg * 4 + j, :], self.ident_b[:], True, True, ["an"], [("ps", b)])
                    self.cp("act", ybuf[:, hg * 4:(hg + 1) * 4, m * 128:(m + 1) * 128], self.ps[:, b, :].rearrange("p (h e) -> p h e", h=4),
                            [("ps", b)], [("ybuf", m, hg)])
            for h in range(8):
                self.dma("pool", self.yT[h], ybuf[:, h, :], [("ybuf", m, h // 4) for m in range(NTILE)], [("yT", h)])
            P.stage_end()


    def build(self, layer_list=None, do_final=True, upto=None):
        self.declare()
        layer_list = list(range(DEPTH)) if layer_list is None else layer_list
        with ExitStack() as es:
            self.setup(es)
            self.stage_load_inputs()
            for li, l in enumerate(layer_list):
                need_ctx = l < DEPTH - 1
                tiles = TT if need_ctx else TT[:4]
                if li == 0:
                    self.stage_mod(l)
                if upto == "mod":
                    break
                self.stage_inproj(l)
                if upto == "inproj":
                    break
                if l % 2 == 0:
                    self.stage_gdn(l)
                    if upto == "gdn":
                        break
                    self.stage_diff(l, need_ctx)
                else:
                    self.stage_na(l, need_ctx)
                    self.stage_win(l, need_ctx)
                if upto == "mixer":
                    break
                self.stage_outproj(l, tiles)
                if upto == "outproj":
                    break
                nxt = layer_list[li + 1] if li + 1 < len(layer_list) else None
                self.stage_ffn(l, tiles, bg_mod=nxt)
                if nxt is not None:
                    self.cur = 1 - self.cur
            if do_final and upto is None:
                self.stage_final()
            self.P.final_wait()
        self.P.close()


def _fm(v):
    return np.ascontiguousarray(np.asarray(v, np.float32).reshape(KC, 128).T)


def _rep(v):
    v = np.asarray(v, np.float32)
    return np.ascontiguousarray(np.broadcast_to(v[None], (128,) + v.shape))


def prep_shared(inp):
    f = lambda a: np.ascontiguousarray(np.asarray(a, np.float32))
    sh = {}
    for k in ("ada_w", "w_in_even", "w_in_odd", "w_out", "ffn_w_gate", "ffn_w_up", "ffn_w_down"):
        sh[k] = f(inp[k])
    sh["ada_b_fm"] = np.ascontiguousarray(f(inp["ada_b"]).reshape(DEPTH, 96, 128).transpose(0, 2, 1))
    sh["nmg_fm"] = np.stack([_fm(inp["norm_mix_g"][l]) for l in range(DEPTH)])
    sh["nfg_fm"] = np.stack([_fm(inp["norm_ffn_g"][l]) for l in range(DEPTH)])
    sh["fng_fm"] = _fm(inp["final_norm_g"])
    cw = f(inp["gdn_conv_w"])
    sh["conv_fm"] = np.ascontiguousarray(cw.reshape(2, 3, 24, 128).transpose(0, 3, 2, 1))
    gv = np.concatenate([f(inp["gdn_a_log"]).reshape(2, 16), f(inp["gdn_dt_bias"]).reshape(2, 16)], axis=1)
    sh["gdn_vec"] = np.stack([_rep(gv[i]) for i in range(2)])
    sh["gng_rep"] = np.stack([_rep(f(inp["gdn_norm_g"])[i]) for i in range(2)])
    lam = np.stack([f(inp["diff_lambda_q1"]), f(inp["diff_lambda_k1"]), f(inp["diff_lambda_q2"]), f(inp["diff_lambda_k2"])], axis=1)
    sh["lam_rep"] = np.stack([_rep(lam[i]) for i in range(2)])
    sh["subln_fm"] = np.ascontiguousarray(f(inp["diff_subln_g"]).reshape(2, 128, 1))
    rpb = f(inp["na_rpb"])
    kr = np.repeat(np.arange(2), 64)
    kc = np.tile(np.arange(64), 2)
    nb = np.full((2, 8, 128, N_NA_MASK, 128), -30000.0, np.float32)
    for mid, (delta, ok) in enumerate(_NA_MASKS):
        dr = np.clip(2 * delta + kr[:, None] - kr[None, :] + 7, 0, 14)
        dc = np.clip(kc[:, None] - kc[None, :] + 15, 0, 30)
        g = rpb[:, :, dr, dc]
        nb[:, :, :, mid, :] = np.where(ok[None, None] > 0, g, np.float32(-30000.0))
    sh["na_bias"] = nb
    sh["sink_rep"] = np.stack([_rep(f(inp["win_sink"])[i]) for i in range(2)])
    c = _consts()
    sh["c_ident"] = c["ident"]
    sh["c_tri"] = c["tri"]
    sh["c_rope_d"] = c["rope_d"]
    sh["c_rope_w"] = c["rope_w"]
    sh["c_perm"] = c["perm"]
    sh["c_win_mask"] = c["win_mask"]
    return sh


def prep_core(inp, b):
    x = np.ascontiguousarray(np.asarray(inp["x"][b], np.float32))
    ctx = np.ascontiguousarray(np.asarray(inp["ctx"][b], np.float32))
    cf = np.stack([_fm(inp["c"][b]), _fm(inp["c_ctx"])], axis=2)
    return {"x": x, "ctx": ctx, "c_fm": np.ascontiguousarray(cf)}


_CACHE = {}


def kernel(**inputs):
    n = 8
    nc = bass.Bass("TRN2", target_bir_lowering=False)
    bld = Builder(nc)
    bld.build()
    shared = prep_shared(inputs)
    in_maps = []
    for b in range(n):
        m = dict(shared)
        m.update(prep_core(inputs, b))
        in_maps.append(m)
    res = run_bass_kernel_spmd(nc, in_maps, core_ids=list(range(n)))
    return np.stack([np.asarray(r["out"], np.float32) for r in res.results], axis=0)
```

```python
import math
from contextlib import ExitStack

import numpy as np

import concourse.bass as bass
import concourse.mybir as mybir
from concourse.bass_utils import run_bass_kernel_spmd

F32 = mybir.dt.float32
BF16 = mybir.dt.bfloat16
AF = mybir.ActivationFunctionType
ALU = mybir.AluOpType
AX = mybir.AxisListType

D = 2048
KC = 16
T = 2048
C = 256
NT = T + C
NTILE = NT // 128
FF = 5632
FC = FF // 128
DEPTH = 4
EPS = 1e-6
TT = [(0, 512), (512, 512), (1024, 512), (1536, 512), (2048, 256)]
W_EVEN = 7200
W_ODD = 4608

ENGS = ("pe", "act", "dve", "pool", "sp")


class _Op:
    __slots__ = ("eng", "emit", "deps", "is_dma", "sem", "semval", "need_inc")

    def __init__(self, eng, emit, is_dma):
        self.eng = eng
        self.emit = emit
        self.deps = []
        self.is_dma = is_dma
        self.sem = None
        self.semval = 0
        self.need_inc = False


class Prog:
    NDMASEM = 8

    def __init__(self, nc, same_eng_sync=True):
        self.nc = nc
        self.same_eng_sync = same_eng_sync
        self.stack = ExitStack()
        self.esem = {e: self.stack.enter_context(nc.semaphore("s_" + e)) for e in ENGS}
        self.ecount = {e: 0 for e in ENGS}
        self.dsem, self.dcount, self.drr, self.dlast = {}, {}, {}, {}
        for q in ("sp", "pool", "act"):
            self.dsem[q] = [self.stack.enter_context(nc.semaphore("d_%s%d" % (q, i))) for i in range(self.NDMASEM)]
            self.dcount[q] = [0] * self.NDMASEM
            self.drr[q] = 0
            self.dlast[q] = [None] * self.NDMASEM
        self.ops = {e: [] for e in ENGS}
        self.lastw = {}
        self.readers = {}
        self.waited = {e: {} for e in ENGS}
        self.n_inst = 0
        self._cur_barrier = None

    def op(self, eng, emit, reads=(), writes=(), dma=False):
        o = _Op(eng, emit, dma)
        deps = []
        psr = [k for k in reads if isinstance(k, tuple) and k and k[0] == "ps"]
        if psr:
            writes = list(writes) + [k for k in psr if k not in writes]
        for k in reads:
            w = self.lastw.get(k)
            if w is not None:
                deps.append(w)
        for k in writes:
            w = self.lastw.get(k)
            if w is not None:
                deps.append(w)
            deps.extend(self.readers.get(k, ()))
        if dma:
            q = eng
            i = self.drr[q]
            self.drr[q] = (i + 1) % self.NDMASEM
            prev = self.dlast[q][i]
            if prev is not None:
                deps.append(prev)
            self.dcount[q][i] += 16
            o.sem = self.dsem[q][i]
            o.semval = self.dcount[q][i]
            o.need_inc = True
            self.dlast[q][i] = o
        seen = set()
        for d in deps:
            if d is o or id(d) in seen:
                continue
            seen.add(id(d))
            if (not d.is_dma) and d.eng == eng and (eng == "pe" or not self.same_eng_sync):
                continue
            o.deps.append(d)
            d.need_inc = True
        for k in writes:
            self.lastw[k] = o
            self.readers[k] = []
        for k in reads:
            self.readers.setdefault(k, []).append(o)
        self.ops[eng].append(o)
        return o

    def stage_end(self):
        lasts = []
        for e in ENGS:
            if self.ops[e]:
                lasts.append(self.ops[e][-1])
        for q in self.dlast:
            for d in self.dlast[q]:
                if d is not None:
                    lasts.append(d)
        for o in lasts:
            o.need_inc = True
        self.flush()
        cur = self._cur_barrier or {}
        self._cur_barrier = {e: list(lasts) + list(cur.get(e) or []) for e in ENGS}
        self.lastw.clear()
        self.readers.clear()

    def flush(self):
        nc = self.nc
        for e in ENGS:
            for o in self.ops[e]:
                if o.is_dma:
                    continue
                if o.need_inc and o.sem is None:
                    self.ecount[e] += 1
                    o.sem = self.esem[e]
                    o.semval = self.ecount[e]
        pend = self._cur_barrier
        engmap = {"pe": "tensor", "act": "scalar", "dve": "vector", "pool": "gpsimd", "sp": "sync"}
        if not any(self.ops[e] for e in ENGS):
            return
        with nc.Block() as block:
            for e in ENGS:
                ops = self.ops[e]
                if not ops:
                    continue

                def body(engine, ops=ops, e=e):
                    waited = self.waited[e]
                    first = True
                    for o in ops:
                        deps = o.deps
                        if first and pend and pend.get(e):
                            deps = list(deps) + [d for d in pend[e] if d is not o]
                            pend[e] = None
                        first = False
                        need = {}
                        for d in deps:
                            sid = id(d.sem)
                            if waited.get(sid, 0) >= d.semval:
                                continue
                            if sid not in need or need[sid][1] < d.semval:
                                need[sid] = (d.sem, d.semval)
                        for sid, (s, v) in need.items():
                            engine.wait_ge(s, v)
                            waited[sid] = v
                            self.n_inst += 1
                        ins = o.emit(engine)
                        self.n_inst += 1
                        if o.need_inc:
                            ins.then_inc(o.sem, 16 if o.is_dma else 1)

                getattr(block, engmap[e])(body)
        for e in ENGS:
            self.ops[e] = []

    def final_wait(self, eng="sp"):
        self.stage_end()
        self.op(eng, lambda en: en.nop())
        self.flush()

    def close(self):
        self.stack.close()


def _rope_tables(kind):
    pos = np.arange(T)
    rows, cols = pos // 64, pos % 64
    cos = np.zeros((128, T), np.float32)
    sin = np.zeros((128, T), np.float32)
    perm = np.zeros((128, 128), np.float32)
    if kind == "d":
        blocks = [(0, 32, rows), (32, 32, cols), (64, 32, rows), (96, 32, cols)]
    else:
        blocks = [(0, 64, rows), (64, 64, cols)]
    for base, n, p in blocks:
        half = n // 2
        inv = (10000.0 ** (-np.arange(0, n, 2, dtype=np.float32) / n)).astype(np.float32)
        ang = p.astype(np.float32)[None, :] * inv[:, None]
        c, s = np.cos(ang).astype(np.float32), np.sin(ang).astype(np.float32)
        cos[base:base + half] = c
        cos[base + half:base + n] = c
        sin[base:base + half] = -s
        sin[base + half:base + n] = s
        for i in range(half):
            perm[base + half + i, base + i] = 1.0
            perm[base + i, base + half + i] = 1.0
    return cos, sin, perm


def _na_geometry():
    masks, index, plan = [], {}, {}
    for a in range(16):
        plan[a] = []
        qrow = np.repeat(np.array([2 * a, 2 * a + 1]), 64)
        qcol = np.tile(np.arange(64), 2)
        start = np.clip(qrow - 4, 0, 24)
        winc = np.clip(qcol - 8, 0, 48)
        for t in range(16):
            krow = np.repeat(np.array([2 * t, 2 * t + 1]), 64)
            kcol = np.tile(np.arange(64), 2)
            ok = ((krow[:, None] >= start[None, :]) & (krow[:, None] < start[None, :] + 8)
                  & (kcol[:, None] >= winc[None, :]) & (kcol[:, None] < winc[None, :] + 16))
            if not ok.any():
                continue
            key = (t - a, ok.tobytes())
            if key not in index:
                index[key] = len(masks)
                masks.append((t - a, ok.astype(np.float32)))
            plan[a].append((t, index[key]))
    return masks, plan


_NA_MASKS, _NA_PLAN = _na_geometry()
N_NA_MASK = len(_NA_MASKS)


def _win_masks():
    m = np.zeros((6, 128, 512), np.float32)
    for r in range(6):
        j = (r - 1) * 128 + np.arange(128)[:, None]
        i = np.arange(512)[None, :]
        m[r] = (np.abs(i - j) <= 128).astype(np.float32)
    return m


def _consts():
    c = {}
    c["ident"] = np.eye(128, dtype=np.float32)
    m = np.arange(128)[:, None]
    i = np.arange(128)[None, :]
    c["tri"] = np.stack([(m <= i), (m >= i), (m > i), (m < i)]).astype(np.float32)
    cd, sd, pd = _rope_tables("d")
    cw, sw, pw = _rope_tables("w")
    c["rope_d"] = np.stack([cd, sd])
    c["rope_w"] = np.stack([cw, sw])
    c["perm"] = np.stack([pd, pw])
    c["na_mask"] = np.stack([mk for _, mk in _NA_MASKS])
    c["win_mask"] = _win_masks()
    return c


class Builder:
    def __init__(self, nc, debug=False, layers=DEPTH, ndt=F32):
        self.nc = nc
        self.debug = debug
        self.layers = layers
        self.ndt = ndt
        self.P = Prog(nc)
        self.inp = {}
        self.scr = {}
        self.uid = 0

    def din(self, name, shape, dtype=F32):
        t = self.nc.dram_tensor(name, list(shape), dtype, kind="ExternalInput").ap()
        self.inp[name] = t
        return t

    def dscr(self, name, shape, dtype):
        kind = "ExternalOutput" if self.debug else "Internal"
        t = self.nc.dram_tensor(name, list(shape), dtype, kind=kind).ap()
        self.scr[name] = t
        return t

    def sb(self, es, name, shape, dtype):
        self.uid += 1
        return es.enter_context(self.nc.sbuf_tensor("%s_%d" % (name, self.uid), list(shape), dtype))

    def mm(self, out, lhsT, rhs, start, stop, reads, writes):
        self.P.op("pe", lambda e: e.matmul(out, lhsT=lhsT, rhs=rhs, start=start, stop=stop), reads, writes)

    def act(self, out, in_, func, reads, writes, bias=None, scale=None, accum_out=None):
        kw = {}
        if bias is not None:
            kw["bias"] = bias
        if scale is not None:
            kw["scale"] = scale
        if accum_out is not None:
            kw["accum_out"] = accum_out
        self.P.op("act", lambda e: e.activation(out=out, in_=in_, func=func, **kw), reads, writes)

    def tt(self, eng, out, in0, in1, op, reads, writes):
        self.P.op(eng, lambda e: e.tensor_tensor(out=out, in0=in0, in1=in1, op=op), reads, writes)

    def ts(self, eng, out, in0, s1, s2, op0, op1, reads, writes):
        if op1 is None:
            self.P.op(eng, lambda e: e.tensor_scalar(out=out, in0=in0, scalar1=s1, scalar2=None, op0=op0), reads, writes)
        else:
            self.P.op(eng, lambda e: e.tensor_scalar(out=out, in0=in0, scalar1=s1, scalar2=s2, op0=op0, op1=op1), reads, writes)

    def stt(self, out, in0, scalar, in1, op0, op1, reads, writes):
        self.P.op("dve", lambda e: e.scalar_tensor_tensor(out=out, in0=in0, scalar=scalar, in1=in1, op0=op0, op1=op1), reads, writes)

    def cp(self, eng, out, in_, reads, writes):
        if eng == "act":
            self.P.op("act", lambda e: e.copy(out=out, in_=in_), reads, writes)
        else:
            self.P.op(eng, lambda e: e.tensor_copy(out=out, in_=in_), reads, writes)

    def recip(self, out, in_, reads, writes):
        self.P.op("dve", lambda e: e.reciprocal(out=out, in_=in_), reads, writes)

    def dma(self, q, out, in_, reads, writes, **kw):
        self.P.op(q, lambda e: e.dma_start(out=out, in_=in_, **kw), reads, writes, dma=True)

    def rsqrt(self, out, in_, mult, tmp, tmpkey, reads, writes):
        self.act(tmp, in_, AF.Sqrt, reads, [tmpkey], bias=self.eps_col[:, 0:1], scale=mult)
        self.recip(out, tmp, [tmpkey], writes)

    def declare(self):
        di = self.din
        self.x = di("x", [T, D])
        self.ctx = di("ctx", [C, D])
        self.c_fm = di("c_fm", [128, KC, 2])
        self.ada_w = di("ada_w", [DEPTH, D, 6 * D])
        self.ada_b = di("ada_b_fm", [DEPTH, 128, 96])
        self.nmg = di("nmg_fm", [DEPTH, 128, KC])
        self.nfg = di("nfg_fm", [DEPTH, 128, KC])
        self.fng = di("fng_fm", [128, KC])
        self.w_in_even = di("w_in_even", [2, D, W_EVEN])
        self.w_in_odd = di("w_in_odd", [2, D, W_ODD])
        self.w_out = di("w_out", [DEPTH, D, D])
        self.w_gate = di("ffn_w_gate", [DEPTH, D, FF])
        self.w_up = di("ffn_w_up", [DEPTH, D, FF])
        self.w_down = di("ffn_w_down", [DEPTH, FF, D])
        self.conv_fm = di("conv_fm", [2, 128, 24, 3])
        self.gdn_vec = di("gdn_vec", [2, 128, 32])
        self.gng = di("gng_rep", [2, 128, 128])
        self.lam = di("lam_rep", [2, 128, 4, 64])
        self.subln = di("subln_fm", [2, 128, 1])
        self.na_bias = di("na_bias", [2, 8, 128, N_NA_MASK, 128])
        self.sink = di("sink_rep", [2, 128, 8])
        self.c_ident = di("c_ident", [128, 128])
        self.c_tri = di("c_tri", [4, 128, 128])
        self.c_rope_d = di("c_rope_d", [2, 128, T])
        self.c_rope_w = di("c_rope_w", [2, 128, T])
        self.c_perm = di("c_perm", [2, 128, 128])
        self.c_win = di("c_win_mask", [6, 128, 512])
        self.out = self.nc.dram_tensor("out", [T, D], F32, kind="ExternalOutput").ap()
        ds = self.dscr
        self.hT = ds("hT", [KC, 128, NT], F32)
        self.yT = ds("yT", [KC, 128, NT], BF16)
        self.qT_a = ds("qT_a", [8, 128, NT], BF16)
        self.kT_a = ds("kT_a", [8, 128, NT], BF16)
        self.v_a = ds("v_a", [NT, 1024], BF16)
        self.k_tm = ds("k_tm", [NT, 1024], BF16)
        self.qT_b = ds("qT_b", [8, 128, NT], BF16)
        self.kT_b = ds("kT_b", [8, 128, NT], BF16)
        self.v_b = ds("v_b", [NT, 1024], BF16)
        self.gz = ds("gz", [NT, 1024], BF16)
        self.gab = ds("gab", [NT, 32], F32)
        self.go = ds("go", [2, NT, 1024], F32)

    def setup(self, es):
        nc = self.nc
        sb = lambda n, s, d: self.sb(es, n, s, d)
        self.ps = es.enter_context(nc.psum_tensor("ps", [128, 8, 512], F32))
        self.ident_f = sb("ident_f", [128, 128], F32)
        self.ident_b = sb("ident_b", [128, 128], BF16)
        self.ones_b = sb("ones_b", [128, 128], BF16)
        self.ones_f = sb("ones_f", [128, 128], F32)
        self.eps_col = sb("eps_col", [128, 1], F32)
        self.cact = sb("cact", [128, KC, 2], BF16)
        self.modv_s = [sb("modv", [128, 96, 2], F32) for _ in range(2)]
        self.G1_s = [sb("G1", [128, KC, 2], F32) for _ in range(2)]
        self.G2_s = [sb("G2", [128, KC, 2], F32) for _ in range(2)]
        self.cur = 0
        self.nrm2 = sb("nrm2", [128, 64], F32)
        self.mbias = sb("mbias", [128, 64], F32)
        P = self.P
        with ExitStack() as e1:
            craw = self.sb(e1, "craw", [128, KC, 2], F32)
            self.dma("sp", self.ident_f[:], self.c_ident, [], ["ident_f"])
            self.dma("sp", craw[:], self.c_fm, [], ["craw"])
            self.cp("dve", self.ident_b[:], self.ident_f[:], ["ident_f"], ["ident_b"])
            P.op("pool", lambda e: e.memset(self.ones_b[:], 1.0), [], ["ones_b"])
            P.op("pool", lambda e: e.memset(self.ones_f[:], 1.0), [], ["ones_f"])
            P.op("pool", lambda e: e.memset(self.eps_col[:], EPS), [], ["eps"])
            self.act(self.cact[:], craw[:], AF.Silu, ["craw"], ["cact"])
            P.stage_end()

    @property
    def modv(self):
        return self.modv_s[self.cur]

    @property
    def G1(self):
        return self.G1_s[self.cur]

    @property
    def G2(self):
        return self.G2_s[self.cur]

    def mod_part(self, es, l, part):
        wv = self.ada_w[l].rearrange("(k p) n -> p k n", p=128)
        wbm = [self.sb(es, "wbm", [128, KC, 256], BF16) for _ in range(2)]
        g0, g1 = (0, 24) if part == 0 else (24, 48)

        def load(cg):
            self.dma("pool", wbm[cg % 2][:], wv[:, :, cg * 256:(cg + 1) * 256], [], [("wbm", cg % 2)])
        if part == 1:
            adab = self.sb(es, "adab2", [128, 96], F32)
            nmg = self.sb(es, "nmg2", [128, KC], F32)
            nfg = self.sb(es, "nfg2", [128, KC], F32)
            self.dma("sp", adab[:], self.ada_b[l], [], ["adab2"])
            self.dma("sp", nmg[:], self.nmg[l], [], ["nmg2"])
            self.dma("sp", nfg[:], self.nfg[l], [], ["nfg2"])
        load(g0)
        for cg in range(g0, g1):
            if cg + 1 < g1:
                load(cg + 1)
            for j in range(2):
                ch = cg * 2 + j
                for k in range(KC):
                    self.mm(self.ps[:, 6, ch * 2:ch * 2 + 2], wbm[cg % 2][:, k, j * 128:(j + 1) * 128], self.cact[:, k, :],
                            k == 0, k == KC - 1, [("wbm", cg % 2), "cact"], [("ps", 6)])
            yield
        if part == 1:
            nx = 1 - self.cur
            mv, g1t, g2t = self.modv_s[nx], self.G1_s[nx], self.G2_s[nx]
            self.tt("dve", mv[:], self.ps[:, 6, 0:192].rearrange("p (a b) -> p a b", b=2),
                    adab[:].unsqueeze(2).to_broadcast([128, 96, 2]), ALU.add, [("ps", 6), "adab2"], ["modv_n"])
            self.stt(g1t[:], mv[:, 16:32, :], 1.0, nmg[:].unsqueeze(2).to_broadcast([128, KC, 2]), ALU.add, ALU.mult,
                     ["modv_n", "nmg2"], ["G1n"])
            self.stt(g2t[:], mv[:, 64:80, :], 1.0, nfg[:].unsqueeze(2).to_broadcast([128, KC, 2]), ALU.add, ALU.mult,
                     ["modv_n", "nfg2"], ["G2n"])
        yield

    def stage_load_inputs(self):
        P = self.P
        with ExitStack() as es:
            xin = [self.sb(es, "xin", [128, D], F32) for _ in range(2)]
            hst = [self.sb(es, "hst", [128, KC, 128], F32) for _ in range(2)]
            hv = self.hT.rearrange("k p t -> p k t")
            for m in range(NTILE):
                src = self.x[m * 128:(m + 1) * 128, :] if m < 16 else self.ctx[(m - 16) * 128:(m - 15) * 128, :]
                r = m % 2
                self.dma("sp", xin[r][:], src, [], [("xin", r)])
                for g in range(4):
                    b = (m * 4 + g) % 8
                    for j in range(4):
                        k = g * 4 + j
                        self.mm(self.ps[:, b, j * 128:(j + 1) * 128], xin[r][:, k * 128:(k + 1) * 128], self.ident_f[:],
                                True, True, [("xin", r)], [("ps", b)])
                    dst = hst[r][:, g * 4:(g + 1) * 4, :]
                    srcp = self.ps[:, b, :].rearrange("p (a b) -> p a b", a=4)
                    self.cp("act" if g % 2 else "dve", dst, srcp, [("ps", b)], [("hst", r, g)])
                self.dma("pool", hv[:, :, m * 128:(m + 1) * 128], hst[r][:], [("hst", r, g) for g in range(4)], [("hT", m)])
            P.stage_end()

    def stage_mod(self, l):
        P = self.P
        with ExitStack() as es:
            wb = [self.sb(es, "wb", [128, KC, 512], BF16) for _ in range(2)]
            adab = self.sb(es, "adab", [128, 96], F32)
            nmg = self.sb(es, "nmg", [128, KC], F32)
            nfg = self.sb(es, "nfg", [128, KC], F32)
            self.dma("sp", adab[:], self.ada_b[l], [], ["adab"])
            self.dma("sp", nmg[:], self.nmg[l], [], ["nmg"])
            self.dma("sp", nfg[:], self.nfg[l], [], ["nfg"])
            wv = self.ada_w[l].rearrange("(k p) n -> p k n", p=128)

            def load(cg):
                self.dma("pool", wb[cg % 2][:], wv[:, :, cg * 512:(cg + 1) * 512], [], [("wb", cg % 2)])
            load(0)
            for cg in range(24):
                if cg + 1 < 24:
                    load(cg + 1)
                for j in range(4):
                    ch = cg * 4 + j
                    for k in range(KC):
                        self.mm(self.ps[:, 0, ch * 2:ch * 2 + 2], wb[cg % 2][:, k, j * 128:(j + 1) * 128], self.cact[:, k, :],
                                k == 0, k == KC - 1, [("wb", cg % 2), "cact"], [("ps", 0)])
            self.tt("dve", self.modv[:], self.ps[:, 0, 0:192].rearrange("p (a b) -> p a b", b=2),
                    adab[:].unsqueeze(2).to_broadcast([128, 96, 2]), ALU.add, [("ps", 0), "adab"], ["modv"])
            self.stt(self.G1[:], self.modv[:, 16:32, :], 1.0, nmg[:].unsqueeze(2).to_broadcast([128, KC, 2]), ALU.add, ALU.mult,
                     ["modv", "nmg"], ["G1"])
            self.stt(self.G2[:], self.modv[:, 64:80, :], 1.0, nfg[:].unsqueeze(2).to_broadcast([128, KC, 2]), ALU.add, ALU.mult,
                     ["modv", "nfg"], ["G2"])
            if self.debug:
                dm = self.dscr("dbg_mod%d" % l, [128, 192], F32)
                self.dma("sp", dm, self.modv[:].rearrange("p a b -> p (a b)"), ["modv"], ["dbgm"])
            P.stage_end()

    def norm_tiles(self, es, tiles, which, uT, base, tag):
        G = self.G1 if which == 0 else self.G2
        sh = 0 if which == 0 else 48
        hv = self.hT.rearrange("k p t -> p k t")
        hb = [self.sb(es, "hb", [128, KC, 256], F32) for _ in range(2)]
        sq = [self.sb(es, "sq", [128, KC, 256], BF16) for _ in range(2)]
        rs = [self.sb(es, "rs", [128, 256], F32) for _ in range(2)]
        rt = self.sb(es, "rt", [128, 256], F32)
        sub = []
        for (t0, n) in tiles:
            for o in range(0, n, 256):
                sub.append((t0 + o, min(256, n - o)))
        pend = None
        for i, (t0, n) in enumerate(sub):
            r = i % 2
            s = 1 if t0 >= T else 0
            b = 7
            self.dma("sp", hb[r][:, :, 0:n], hv[:, :, t0:t0 + n], [("hT", "all")], [("hb", r)])
            self.tt("pool", sq[r][:, :, 0:n], hb[r][:, :, 0:n], hb[r][:, :, 0:n], ALU.mult, [("hb", r)], [("sq", r)])
            for k in range(KC):
                self.mm(self.ps[:, b, 0:n], self.ones_b[:], sq[r][:, k, 0:n], k == 0, k == KC - 1, [("sq", r), "ones_b"], [("ps", b)])
            self.rsqrt(rs[r][:, 0:n], self.ps[:, b, 0:n], 1.0 / D, rt[:, 0:n], "rt", [("ps", b), "eps"], [("rs", r)])
            self.tt("dve", hb[r][:, :, 0:n], hb[r][:, :, 0:n], rs[r][:, 0:n].unsqueeze(1).to_broadcast([128, KC, n]), ALU.mult,
                    [("hb", r), ("rs", r)], [("hb", r)])

            def second(r=r, s=s, t0=t0, n=n):
                for k in range(KC):
                    self.act(uT[:, k, t0 - base:t0 - base + n], hb[r][:, k, 0:n], AF.Identity, [("hb", r), "G1", "G2", "modv"],
                             [(tag, k, t0)], scale=G[:, k, s:s + 1], bias=self.modv[:, sh + k, s:s + 1])
            if pend is not None:
                pend()
            pend = second
        if pend is not None:
            pend()

    def resid_update(self, hrow, c, t0, n, ps_ap, gate_chunk_base, rkeys, wkey):
        s = 1 if t0 >= T else 0
        self.stt(hrow[:, t0:t0 + n], ps_ap, self.modv[:, gate_chunk_base + c, s:s + 1], hrow[:, t0:t0 + n], ALU.mult, ALU.add,
                 rkeys + ["modv"], [wkey])

    def stats_row(self, orow, okeys, slot0, split, sqrow, mx, sn="sqrow", presq=False):
        if not presq:
            self.act(sqrow[:], orow[:], AF.Square, okeys, [sn])
        parts = [(0, 64), (64, 128)] if split else [(0, 128)]
        for pi, (p0, p1) in enumerate(parts):
            for ti, (t0, n) in enumerate(TT):
                b = 4 + (self.auxr % 3)
                self.auxr += 1
                self.mm(self.ps[:, b, 0:n], self.ones_b[p0:p1, :], sqrow[p0:p1, t0:t0 + n], True, True, [sn, "ones_b"], [("ps", b)])
                self.P.op("dve", lambda e, b=b, n=n, ti=ti: e.tensor_reduce(out=mx[:, ti:ti + 1], in_=self.ps[:, b, 0:n], axis=AX.X, op=ALU.max),
                          [("ps", b)], [("mx", ti)])
            sl = slot0 + pi
            self.P.op("dve", lambda e, sl=sl: e.tensor_reduce(out=self.nrm2[:, sl:sl + 1], in_=mx[:, 0:5], axis=AX.X, op=ALU.max),
                      [("mx", ti) for ti in range(5)], [("nrm2", sl)])

    def stage_inproj(self, l):
        P = self.P
        even = (l % 2 == 0)
        i2 = l // 2
        w = (self.w_in_even if even else self.w_in_odd)[i2]
        wv = w.rearrange("(k p) n -> p k n", p=128)
        if even:
            jobs = [("fm", "gq", 0, 1024), ("fm", "gk", 1024, 1024), ("fm", "gv", 2048, 1024), ("tm", "z", 3072, 1024),
                    ("tm", "ab", 4096, 32), ("fm", "dq", 4128, 1024), ("fm", "dk", 5152, 1024), ("tm", "dv", 6176, 1024)]
        else:
            jobs = [("fm", "nq", 0, 1024), ("fm", "nk", 1024, 1024), ("tm", "nv", 2048, 1024), ("fm", "wq", 3072, 1024),
                    ("fm", "wk", 4096, 256), ("tm", "wv", 4352, 256)]
        groups = []
        for mode, kind, c0, nc_ in jobs:
            for g0 in range(0, nc_, 512):
                groups.append((mode, kind, c0, g0, min(512, nc_ - g0)))
        with ExitStack() as eo:
            uT = self.sb(eo, "uT", [128, KC, NT], BF16)
            with ExitStack() as es:
                self.norm_tiles(es, TT, 0, uT, 0, "uT")
                P.stage_end()
                if self.debug:
                    du = self.dscr("dbg_uT%d" % l, [128, KC, NT], BF16)
                    self.dma("sp", du, uT[:], [], ["dbgu"])
                    P.stage_end()
            with ExitStack() as es:
                sb = lambda n, s, d: self.sb(es, n, s, d)
                wb = [sb("wb", [128, KC, 512], BF16) for _ in range(2)]
                xrows = [sb("xrow", [128, NT], F32) for _ in range(2)]
                yas = [sb("ya", [128, NT], F32) for _ in range(2)]
                orow0s = [sb("orow0", [128, NT], BF16) for _ in range(2)]
                orow = [sb("orow", [128, NT], BF16) for _ in range(2)]
                sqrows = [sb("sqrow", [128, NT], BF16) for _ in range(2)]
                rsr = sb("rsr", [128, 512], F32)
                rst = sb("rst", [128, 512], F32)
                t1 = sb("t1", [128, 512], F32)
                t2 = sb("t2", [128, 512], F32)
                mx = sb("mx", [128, 8], F32)
                tms = [sb("tms", [128, NTILE, 128], BF16) for _ in range(1)]
                gst = [sb("gst", [128, 512], BF16) for _ in range(3)]
                abst = sb("abst", [128, NTILE, 32], F32)
                rope = sb("rope", [128, 2, T], F32)
                permb = sb("permb", [128, 128], BF16)
                permf = sb("permf", [128, 128], F32)
                cw = sb("cw", [128, 24, 3], F32)
                self.auxr = 0
                self.dma("sp", rope[:], (self.c_rope_d if even else self.c_rope_w).rearrange("a p t -> p a t"), [], ["rope"])
                self.dma("sp", permf[:], self.c_perm[0 if even else 1], [], ["permf"])
                self.cp("dve", permb[:], permf[:], ["permf"], ["permb"])
                if even:
                    self.dma("sp", cw[:], self.conv_fm[i2], [], ["cw"])

                def load(gi):
                    mode, kind, c0, g0, gw = groups[gi]
                    self.dma("pool", wb[gi % 2][:, :, 0:gw], wv[:, :, c0 + g0:c0 + g0 + gw], [], [("wb", gi % 2)])

                load(0)
                mainr = 0
                orr = 0
                self.tmr = 0
                gsr = 0
                self.pipe = []
                self.pipe_depth = 1
                for gi, (mode, kind, c0, g0, gw) in enumerate(groups):
                    if gi + 1 < len(groups):
                        load(gi + 1)
                    wbg = wb[gi % 2]
                    wkey = ("wb", gi % 2)
                    if mode == "tm":
                        if kind == "ab":
                            for m in range(NTILE):
                                b = mainr % 4
                                mainr += 1
                                for k in range(KC):
                                    self.mm(self.ps[:, b, 0:gw], uT[:, k, m * 128:(m + 1) * 128], wbg[:, k, 0:gw], k == 0, k == KC - 1,
                                            [wkey], [("ps", b)])
                                self.cp("act", abst[:, m, :], self.ps[:, b, 0:gw], [("ps", b)], [("abst", m)])
                            self.dma("pool", self.gab.rearrange("(t p) c -> p t c", p=128), abst[:], [("abst", m) for m in range(NTILE)], ["gab"])
                            continue
                        dst = {"z": self.gz, "dv": self.v_b, "nv": self.v_a, "wv": self.v_b}[kind]
                        for m in range(NTILE):
                            b = mainr % 4
                            mainr += 1
                            for k in range(KC):
                                self.mm(self.ps[:, b, 0:gw], uT[:, k, m * 128:(m + 1) * 128], wbg[:, k, 0:gw], k == 0, k == KC - 1,
                                        [wkey], [("ps", b)])
                            r = gsr % 3
                            gsr += 1
                            self.cp("act" if m % 2 else "dve", gst[r][:, 0:gw], self.ps[:, b, 0:gw], [("ps", b)], [("gst", r)])
                            self.dma("pool", dst[m * 128:(m + 1) * 128, g0:g0 + gw], gst[r][:, 0:gw], [("gst", r)], [(kind, m, g0)])
                        continue
                    for j in range(gw // 128):
                        ch = (g0 // 128) + j
                        o_r = orr % 2
                        orr += 1
                        ob = orow[o_r]
                        okey = ("orow", o_r)
                        xrow, ya, sqrow, ob0 = xrows[o_r], yas[o_r], sqrows[o_r], orow0s[o_r]
                        xn, yn, sn, o0n = ("xrow", o_r), ("ya", o_r), ("sqrow", o_r), ("orow0", o_r)
                        plain = kind in ("nq", "nk")
                        roped = kind in ("dq", "dk", "wq", "wk")
                        gdn = kind in ("gq", "gk", "gv")
                        for ti, (t0, n) in enumerate(TT):
                            b = mainr % 4
                            mainr += 1
                            for k in range(KC):
                                self.mm(self.ps[:, b, 0:n], wbg[:, k, j * 128:(j + 1) * 128], uT[:, k, t0:t0 + n], k == 0, k == KC - 1,
                                        [wkey], [("ps", b)])
                            if plain or (roped and t0 >= T):
                                self.cp("act", ob[:, t0:t0 + n], self.ps[:, b, 0:n], [("ps", b)], [(okey, ti)])
                            elif roped:
                                self.cp("act", ob0[:, t0:t0 + n], self.ps[:, b, 0:n], [("ps", b)], [(o0n, ti)])
                            else:
                                self.cp("act", xrow[:, t0:t0 + n], self.ps[:, b, 0:n], [("ps", b)], [(xn, ti)])

                        okeys_now = [(okey, ti) for ti in range(5)]
                        sqst = sqrows[o_r]
                        if gdn:
                            cch = {"gq": 0, "gk": 8, "gv": 16}[kind] + ch
                            xk = [(xn, ti) for ti in range(5)]
                            self.ts("dve", ya[:], xrow[:], cw[:, cch, 1:2], None, ALU.mult, None, xk + ["cw"], [yn])
                            for (s0, s1) in ((0, T), (T, NT)):
                                self.stt(ya[:, s0 + 1:s1], xrow[:, s0:s1 - 1], cw[:, cch, 0:1], ya[:, s0 + 1:s1], ALU.mult, ALU.add,
                                         xk + ["cw", yn], [yn])
                                self.stt(ya[:, s0:s1 - 1], xrow[:, s0 + 1:s1], cw[:, cch, 2:3], ya[:, s0:s1 - 1], ALU.mult, ALU.add,
                                         xk + ["cw", yn], [yn])
                            if kind == "gv":
                                self.act(ob[:], ya[:], AF.Silu, [yn], okeys_now)
                            else:
                                self.act(ya[:], ya[:], AF.Silu, [yn], [yn])
                                self.act(sqrow[:], ya[:], AF.Square, [yn], [sn])
                        elif not roped:
                            self.act(sqrow[:], ob[:], AF.Square, okeys_now, [sn])

                        def epi(kind=kind, ch=ch, ob=ob, okey=okey, xrow=xrow, ya=ya, sqrow=sqrow, ob0=ob0, xn=xn, yn=yn, sn=sn, o0n=o0n,
                                roped=roped, gdn=gdn):
                            okeys = [(okey, ti) for ti in range(5)]
                            if roped:
                                for ti, (t0, n) in enumerate(TT[:4]):
                                    ba = 4 + (self.auxr % 3)
                                    self.auxr += 1
                                    self.mm(self.ps[:, ba, 0:n], permb[:], ob0[:, t0:t0 + n], True, True, [(o0n, ti), "permb"], [("ps", ba)])
                                    self.tt("dve", t1[:, 0:n], ob0[:, t0:t0 + n], rope[:, 0, t0:t0 + n], ALU.mult, [(o0n, ti), "rope"], ["t1"])
                                    self.tt("dve", t2[:, 0:n], self.ps[:, ba, 0:n], rope[:, 1, t0:t0 + n], ALU.mult, [("ps", ba), "rope"], ["t2"])
                                    self.tt("dve", ob[:, t0:t0 + n], t1[:, 0:n], t2[:, 0:n], ALU.add, ["t1", "t2"], [(okey, ti)])
                                self.act(sqrow[:], ob[:], AF.Square, okeys, [sn])
                            if kind in ("gq", "gk"):
                                for ti, (t0, n) in enumerate(TT):
                                    ba = 4 + (self.auxr % 3)
                                    self.auxr += 1
                                    self.mm(self.ps[:, ba, 0:n], self.ones_b[:], sqrow[:, t0:t0 + n], True, True, [sn, "ones_b"], [("ps", ba)])
                                    self.rsqrt(rsr[:, 0:n], self.ps[:, ba, 0:n], 1.0, rst[:, 0:n], "rst", [("ps", ba), "eps"], ["rsr"])
                                    sc = (128.0 ** -0.5) if kind == "gq" else 1.0
                                    self.stt(ob[:, t0:t0 + n], ya[:, t0:t0 + n], sc, rsr[:, 0:n], ALU.mult, ALU.mult, [yn, "rsr"], [(okey, ti)])
                            if kind in ("dq", "dk"):
                                self.stats_row(ob, okeys, (0 if kind == "dq" else 16) + 2 * ch, True, sqrow, mx, sn, presq=True)
                            elif kind in ("nq", "nk"):
                                self.stats_row(ob, okeys, (0 if kind == "nq" else 8) + ch, False, sqrow, mx, sn, presq=True)
                            elif kind in ("wq", "wk"):
                                self.stats_row(ob, okeys, (16 if kind == "wq" else 24) + ch, False, sqrow, mx, sn, presq=True)
                            fm_dst = {"gq": self.qT_a, "gk": self.kT_a, "dq": self.qT_b, "dk": self.kT_b, "nq": self.qT_a, "nk": self.kT_a,
                                      "wq": self.qT_b, "wk": self.kT_b}.get(kind)
                            if fm_dst is not None:
                                self.dma("pool", fm_dst[ch], ob[:], okeys, [(kind, "fm", ch)])
                            if kind in ("gk", "gv"):
                                tr = 0
                                for m4 in range(0, NTILE, 4):
                                    ba = 4 + (self.auxr % 3)
                                    self.auxr += 1
                                    nm = min(4, NTILE - m4)
                                    for mm_ in range(nm):
                                        m = m4 + mm_
                                        self.mm(self.ps[:, ba, mm_ * 128:(mm_ + 1) * 128], ob[:, m * 128:(m + 1) * 128], self.ident_b[:], True, True,
                                                okeys + ["ident_b"], [("ps", ba)])
                                    self.cp("act" if (m4 // 4) % 2 else "dve", tms[tr][:, m4:m4 + nm, :],
                                            self.ps[:, ba, 0:nm * 128].rearrange("p (a b) -> p a b", b=128), [("ps", ba)], [("tms", tr, m4)])
                                tdst = self.k_tm if kind == "gk" else self.v_a
                                self.dma("pool", tdst[:, ch * 128:(ch + 1) * 128].rearrange("(t p) d -> p t d", p=128), tms[tr][:],
                                         [("tms", tr, m4) for m4 in range(0, NTILE, 4)], [(kind, "tm", ch)])
                        self.pipe_push(epi)
                self.pipe_flush()
                P.stage_end()

    def stage_outproj(self, l, tiles):
        P = self.P
        wv = self.w_out[l].rearrange("(k p) n -> p k n", p=128)
        hv = self.hT
        with ExitStack() as es:
            sb = lambda n, s, d: self.sb(es, n, s, d)
            yT = sb("yTs", [128, KC, NT], BF16)
            wb = [sb("wb", [128, KC, 512], BF16) for _ in range(2)]
            hrow = [sb("hrow", [128, NT], F32) for _ in range(3)]
            for k in range(KC):
                self.dma("sp", yT[:, k, :], self.yT[k], [], [("yT", k)])

            def load(g):
                self.dma("pool", wb[g % 2][:], wv[:, :, g * 512:(g + 1) * 512], [], [("wb", g % 2)])
            load(0)
            tmax = max(t0 + n for t0, n in tiles)
            mainr = 0
            for g in range(4):
                if g + 1 < 4:
                    load(g + 1)
                for j in range(4):
                    c = g * 4 + j
                    hr = c % 3
                    self.dma("sp", hrow[hr][:, 0:tmax], hv[c][:, 0:tmax], [], [("hrow", hr)])
                    for (t0, n) in tiles:
                        b = mainr % 8
                        mainr += 1
                        for k in range(KC):
                            self.mm(self.ps[:, b, 0:n], wb[g % 2][:, k, j * 128:(j + 1) * 128], yT[:, k, t0:t0 + n], k == 0, k == KC - 1,
                                    [("wb", g % 2), ("yT", k)], [("ps", b)])
                        self.resid_update(hrow[hr], c, t0, n, self.ps[:, b, 0:n], 32, [("ps", b), ("hrow", hr)], ("hrow", hr))
                    self.dma("pool", hv[c][:, 0:tmax], hrow[hr][:, 0:tmax], [("hrow", hr)], [("hTc", c)])
            P.stage_end()

    def stage_ffn(self, l, tiles, bg_mod=None):
        P = self.P
        wg = self.w_gate[l].rearrange("(k p) n -> p k n", p=128)
        wu = self.w_up[l].rearrange("(k p) n -> p k n", p=128)
        wd = self.w_down[l].rearrange("(f p) n -> p f n", p=128)
        hv = self.hT
        halves = [tiles[:2], tiles[2:]]
        for hi, half in enumerate(halves):
            if not half:
                continue
            base = half[0][0]
            ntok = sum(n for _, n in half)
            with ExitStack() as eo:
                actT = self.sb(eo, "actT", [128, FC, ntok], BF16)
                with ExitStack() as es:
                    sb = lambda n, s, d: self.sb(es, n, s, d)
                    u2 = sb("u2", [128, KC, ntok], BF16)
                    with ExitStack() as en:
                        self.norm_tiles(en, half, 1, u2, base, "u2")
                        P.stage_end()
                    wgb = [sb("wgb", [128, KC, 256], BF16) for _ in range(2)]
                    wub = [sb("wub", [128, KC, 256], BF16) for _ in range(2)]
                    sg = sb("sg", [128, 512], F32)
                    gen = self.mod_part(es, bg_mod, hi) if bg_mod is not None else None

                    def load(g):
                        self.dma("pool", wgb[g % 2][:], wg[:, :, g * 256:(g + 1) * 256], [], [("wgb", g % 2)])
                        self.dma("pool", wub[g % 2][:], wu[:, :, g * 256:(g + 1) * 256], [], [("wub", g % 2)])
                    load(0)
                    r = 0
                    for g in range(FC // 2):
                        if g + 1 < FC // 2:
                            load(g + 1)
                        for j in range(2):
                            f = g * 2 + j
                            for (t0, n) in half:
                                bg = (r % 3) * 2
                                bu = bg + 1
                                r += 1
                                for k in range(KC):
                                    self.mm(self.ps[:, bg, 0:n], wgb[g % 2][:, k, j * 128:(j + 1) * 128], u2[:, k, t0 - base:t0 - base + n],
                                            k == 0, k == KC - 1, [("wgb", g % 2)], [("ps", bg)])
                                for k in range(KC):
                                    self.mm(self.ps[:, bu, 0:n], wub[g % 2][:, k, j * 128:(j + 1) * 128], u2[:, k, t0 - base:t0 - base + n],
                                            k == 0, k == KC - 1, [("wub", g % 2)], [("ps", bu)])
                                self.act(sg[:, 0:n], self.ps[:, bg, 0:n], AF.Silu, [("ps", bg)], ["sg"])
                                self.tt("dve", actT[:, f, t0 - base:t0 - base + n], sg[:, 0:n], self.ps[:, bu, 0:n], ALU.mult,
                                        ["sg", ("ps", bu)], [("actT", f, t0)])
                        if gen is not None:
                            next(gen, None)
                    if gen is not None:
                        for _ in gen:
                            pass
                    P.stage_end()
                with ExitStack() as es:
                    sb = lambda n, s, d: self.sb(es, n, s, d)
                    wdb = [sb("wdb", [128, FC, 256], BF16) for _ in range(2)]
                    hrow = [sb("hrow", [128, ntok], F32) for _ in range(3)]

                    def loadd(g):
                        self.dma("pool", wdb[g % 2][:], wd[:, :, g * 256:(g + 1) * 256], [], [("wdb", g % 2)])
                    loadd(0)
                    r = 0
                    for g in range(8):
                        if g + 1 < 8:
                            loadd(g + 1)
                        for j in range(2):
                            c = g * 2 + j
                            hr = c % 3
                            self.dma("sp", hrow[hr][:], hv[c][:, base:base + ntok], [], [("hrow", hr)])
                            for (t0, n) in half:
                                b = r % 6
                                r += 1
                                for f in range(FC):
                                    self.mm(self.ps[:, b, 0:n], wdb[g % 2][:, f, j * 128:(j + 1) * 128], actT[:, f, t0 - base:t0 - base + n],
                                            f == 0, f == FC - 1, [("wdb", g % 2)], [("ps", b)])
                                s = 1 if t0 >= T else 0
                                self.stt(hrow[hr][:, t0 - base:t0 - base + n], self.ps[:, b, 0:n], self.modv[:, 80 + c, s:s + 1],
                                         hrow[hr][:, t0 - base:t0 - base + n], ALU.mult, ALU.add, [("ps", b), ("hrow", hr)], [("hrow", hr)])
                            self.dma("pool", hv[c][:, base:base + ntok], hrow[hr][:], [("hrow", hr)], [("hTc", c)])
                    P.stage_end()

    def stage_final(self):
        P = self.P
        hv = self.hT.rearrange("k p t -> p k t")
        with ExitStack() as es:
            sb = lambda n, s, d: self.sb(es, n, s, d)
            hb = [sb("hb", [128, KC, 512], F32) for _ in range(2)]
            sq = sb("sq", [128, KC, 512], BF16)
            rs = sb("rs", [128, 512], F32)
            rt = sb("rt", [128, 512], F32)
            fg = sb("fg", [128, KC], F32)
            ot = [sb("ot", [128, D], F32) for _ in range(2)]
            self.dma("sp", fg[:], self.fng, [], ["fg"])
            orr = 0
            self.fbank = 0
            for i, (t0, n) in enumerate(TT[:4]):
                r = i % 2
                self.dma("sp", hb[r][:], hv[:, :, t0:t0 + n], [], [("hb", r)])
                self.act(sq[:], hb[r][:], AF.Square, [("hb", r)], ["sq"])
                for k in range(KC):
                    self.mm(self.ps[:, 7, 0:n], self.ones_b[:], sq[:, k, :], k == 0, k == KC - 1, ["sq", "ones_b"], [("ps", 7)])
                self.rsqrt(rs[:], self.ps[:, 7, 0:n], 1.0 / D, rt[:], "rt", [("ps", 7), "eps"], ["rs"])
                self.tt("dve", hb[r][:], hb[r][:], rs[:].unsqueeze(1).to_broadcast([128, KC, n]), ALU.mult, [("hb", r), "rs"], [("hb", r)])
                self.tt("dve", hb[r][:], hb[r][:], fg[:].unsqueeze(2).to_broadcast([128, KC, n]), ALU.mult, [("hb", r), "fg"], [("hb", r)])
                for m in range(n // 128):
                    o = orr % 2
                    orr += 1
                    for g in range(4):
                        b = self.fbank % 7
                        self.fbank += 1
                        for j in range(4):
                            k = g * 4 + j
                            self.mm(self.ps[:, b, j * 128:(j + 1) * 128], hb[r][:, k, m * 128:(m + 1) * 128], self.ident_f[:], True, True,
                                    [("hb", r), "ident_f"], [("ps", b)])
                        self.cp("act" if g % 2 else "dve", ot[o][:, g * 512:(g + 1) * 512], self.ps[:, b, :], [("ps", b)], [("ot", o, g)])
                    tok = t0 + m * 128
                    self.dma("pool", self.out[tok:tok + 128, :], ot[o][:], [("ot", o, g) for g in range(4)], [("out", tok)])
            P.stage_end()

    def pipe_push(self, fn):
        self.pipe.append(fn)
        while len(self.pipe) > self.pipe_depth:
            self.pipe.pop(0)()

    def pipe_flush(self):
        while self.pipe:
            self.pipe.pop(0)()

    def attn_keys(self, pT, q_ap, n, ktiles, bias_ap, scale, slot, qkeys):
        last = len(ktiles) - 1
        for idx, (k_ap, v_ap, mask_ap, rk) in enumerate(ktiles):
            bs = self.sr % 3
            self.sr += 1
            self.mm(self.ps[:, bs, 0:n], k_ap, q_ap, True, True, qkeys + rk, [("ps", bs)])
            pr = self.pr % 4
            self.pr += 1
            self.act(pT[pr][:, 0:n], self.ps[:, bs, 0:n], AF.Exp, [("ps", bs), "mbias"], [("pT", pr)], bias=bias_ap, scale=scale)
            if mask_ap is not None:
                self.tt("dve", pT[pr][:, 0:n], pT[pr][:, 0:n], mask_ap, ALU.mult, [("pT", pr), "mask"], [("pT", pr)])

            def second(idx=idx, pr=pr, v_ap=v_ap, rk=rk):
                self.mm(self.ps[:, 3 + slot, 0:n], v_ap, pT[pr][:, 0:n], idx == 0, idx == last, [("pT", pr)] + rk, [("ps", 3 + slot)])
                self.mm(self.ps[:, 5 + slot, 0:n], self.ones_b[:], pT[pr][:, 0:n], idx == 0, idx == last, [("pT", pr)], [("ps", 5 + slot)])
            self.pipe_push(second)

    def score_bounds(self, es, pairs, scale):
        tmp = self.sb(es, "mtmp", [128, 64], F32)
        for slot, qs, ks in pairs:
            self.tt("dve", tmp[:, slot:slot + 1], self.nrm2[:, qs:qs + 1], self.nrm2[:, ks:ks + 1], ALU.mult, [], [("mtmp", slot)])
            self.act(tmp[:, slot:slot + 1], tmp[:, slot:slot + 1], AF.Sqrt, [("mtmp", slot)], [("mtmp", slot)])
            self.ts("dve", self.mbias[:, slot:slot + 1], tmp[:, slot:slot + 1], -scale, None, ALU.mult, None, [("mtmp", slot)], ["mbias"])

    def stage_diff(self, l, need_ctx):
        P = self.P
        i2 = l // 2
        lam_init = 0.8 - 0.6 * math.exp(-0.3 * l)
        scale = 64.0 ** -0.5
        with ExitStack() as es:
            sb = lambda n, s, d: self.sb(es, n, s, d)
            qT = [sb("qT", [128, NT], BF16) for _ in range(2)]
            qz = [[sb("qz", [128, NT], BF16) for _ in range(2)] for _ in range(2)]
            kT = [sb("kT", [128, NT], BF16) for _ in range(2)]
            V = [sb("V", [128, NTILE, 128], BF16) for _ in range(2)]
            pT = [sb("pT", [128, 512], BF16) for _ in range(4)]
            ybuf = [sb("ybuf", [128, NT], BF16) for _ in range(2)]
            r0 = sb("r0", [128, 512], F32)
            o0 = sb("o0", [128, 512], F32)
            r1 = sb("r1", [128, 512], F32)
            o1 = sb("o1", [128, 512], F32)
            od = sb("od", [128, 512], F32)
            sq = sb("sqd", [128, 512], BF16)
            rs = sb("rsd", [128, 512], F32)
            rt = sb("rtd", [128, 512], F32)
            lamv = sb("lamv", [128, 4, 64], F32)
            lp = sb("lp", [128, 2, 64], F32)
            le = sb("le", [128, 2], F32)
            lamc = sb("lamc", [128, 1], F32)
            sgc = sb("sgc", [128, 1], F32)
            self.sr = 0
            self.pr = 0
            self.pipe = []
            self.pipe_depth = 2
            self.score_bounds(es, [(h * 2 + c, h * 2 + c, 16 + h * 2 + c) for h in range(8) for c in range(2)], scale)
            self.dma("sp", lamv[:], self.lam[i2], [], ["lamv"])
            self.dma("sp", sgc[:], self.subln[i2], [], ["sgc"])
            self.tt("dve", lp[:, 0, :], lamv[:, 0, :], lamv[:, 1, :], ALU.mult, ["lamv"], ["lp"])
            self.tt("dve", lp[:, 1, :], lamv[:, 2, :], lamv[:, 3, :], ALU.mult, ["lamv", "lp"], ["lp"])
            P.op("dve", lambda e: e.tensor_reduce(out=le[:], in_=lp[:], axis=AX.X, op=ALU.add), ["lp"], ["le"])
            self.act(le[:], le[:], AF.Exp, ["le"], ["le"])
            self.tt("dve", lamc[:], le[:, 0:1], le[:, 1:2], ALU.subtract, ["le"], ["lamc"])
            self.ts("dve", lamc[:], lamc[:], lam_init, None, ALU.add, None, ["lamc"], ["lamc"])
            self.ts("dve", sgc[:], sgc[:], 1.0 - lam_init, None, ALU.mult, None, ["sgc"], ["sgc"])
            for c in range(2):
                for r in range(2):
                    P.op("pool", lambda e, c=c, r=r: e.memset(qz[r][c][:], 0.0), [], [("qz", r, c)])
            qtiles = TT if need_ctx else TT[:4]
            for h in range(8):
                r = h % 2
                self.dma("sp", qT[r][:], self.qT_b[h], [], [("qT", r)])
                self.dma("sp", kT[r][:], self.kT_b[h], [], [("kT", r)])
                self.dma("sp", V[r][:], self.v_b[:, h * 128:(h + 1) * 128].rearrange("(t p) d -> p t d", p=128), [], [("V", r)])
                for c in range(2):
                    self.cp("pool", qz[r][c][c * 64:(c + 1) * 64, :], qT[r][c * 64:(c + 1) * 64, :], [("qT", r)], [("qz", r, c)])
                for (t0, n) in qtiles:
                    kts = list(range(NTILE)) if t0 < T else [16, 17]
                    for c in range(2):
                        kl = [(kT[r][:, m * 128:(m + 1) * 128], V[r][:, m, :], None, [("kT", r), ("V", r)]) for m in kts]
                        self.attn_keys(pT, qz[r][c][:, t0:t0 + n], n, kl, self.mbias[:, h * 2 + c:h * 2 + c + 1], scale, c, [("qz", r, c)])

                    def epi(t0=t0, n=n, r=r):
                        self.recip(r0[:, 0:n], self.ps[:, 5, 0:n], [("ps", 5)], ["r0"])
                        self.tt("dve", o0[:, 0:n], self.ps[:, 3, 0:n], r0[:, 0:n], ALU.mult, [("ps", 3), "r0"], ["o0"])
                        self.recip(r1[:, 0:n], self.ps[:, 6, 0:n], [("ps", 6)], ["r1"])
                        self.ts("dve", r1[:, 0:n], r1[:, 0:n], lamc[:, 0:1], None, ALU.mult, None, ["r1", "lamc"], ["r1"])
                        self.tt("dve", o1[:, 0:n], self.ps[:, 4, 0:n], r1[:, 0:n], ALU.mult, [("ps", 4), "r1"], ["o1"])
                        self.tt("dve", od[:, 0:n], o0[:, 0:n], o1[:, 0:n], ALU.subtract, ["o0", "o1"], ["od"])
                        self.act(sq[:, 0:n], od[:, 0:n], AF.Square, ["od"], ["sqd"])
                        self.mm(self.ps[:, 7, 0:n], self.ones_b[:], sq[:, 0:n], True, True, ["sqd"], [("ps", 7)])
                        self.rsqrt(rs[:, 0:n], self.ps[:, 7, 0:n], 1.0 / 128, rt[:, 0:n], "rtd", [("ps", 7)], ["rsd"])
                        self.stt(ybuf[r][:, t0:t0 + n], od[:, 0:n], sgc[:, 0:1], rs[:, 0:n], ALU.mult, ALU.mult, ["od", "rsd", "sgc"], [("ybuf", r)])
                    self.pipe_push(epi)
                tmax = NT if need_ctx else T

                def store(h=h, r=r, tmax=tmax):
                    self.dma("pool", self.yT[8 + h][:, 0:tmax], ybuf[r][:, 0:tmax], [("ybuf", r)], [("yT", 8 + h)])
                self.pipe_push(store)
            self.pipe_flush()
            P.stage_end()

    def stage_na(self, l, need_ctx):
        P = self.P
        i2 = l // 2
        scale = 128.0 ** -0.5
        with ExitStack() as es:
            sb = lambda n, s, d: self.sb(es, n, s, d)
            qT = [sb("qT", [128, NT], BF16) for _ in range(2)]
            kT = [sb("kT", [128, NT], BF16) for _ in range(2)]
            V = [sb("V", [128, NTILE, 128], BF16) for _ in range(2)]
            Bf = sb("Bf", [128, N_NA_MASK, 128], F32)
            E = [sb("E", [128, N_NA_MASK, 128], BF16) for _ in range(2)]
            pT = [sb("pT", [128, 512], BF16) for _ in range(4)]
            ybuf = [sb("ybuf", [128, NT], BF16) for _ in range(2)]
            rr = sb("rr", [128, 512], F32)
            self.sr = 0
            self.pr = 0
            self.pipe = []
            self.pipe_depth = 2
            self.score_bounds(es, [(h, h, 8 + h) for h in range(8)], scale)
            for h in range(8):
                r = h % 2
                self.dma("sp", qT[r][:], self.qT_a[h], [], [("qT", r)])
                self.dma("sp", kT[r][:], self.kT_a[h], [], [("kT", r)])
                self.dma("sp", V[r][:], self.v_a[:, h * 128:(h + 1) * 128].rearrange("(t p) d -> p t d", p=128), [], [("V", r)])
                self.dma("sp", Bf[:], self.na_bias[i2, h], [], ["Bf"])
                self.act(E[r][:], Bf[:], AF.Exp, ["Bf"], [("E", r)])
                blocks = [(a * 128, 128, a) for a in range(16)]
                if need_ctx:
                    blocks.append((T, C, None))
                for bi, (t0, n, a) in enumerate(blocks):
                    rk = [("kT", r), ("V", r)]
                    if a is None:
                        kl = [(kT[r][:, m * 128:(m + 1) * 128], V[r][:, m, :], None, rk) for m in (16, 17)]
                    else:
                        kl = [(kT[r][:, t * 128:(t + 1) * 128], V[r][:, t, :], E[r][:, mid, :], rk + [("E", r)]) for (t, mid) in _NA_PLAN[a]]
                        kl += [(kT[r][:, m * 128:(m + 1) * 128], V[r][:, m, :], None, rk) for m in (16, 17)]
                    slot = bi % 2
                    self.attn_keys(pT, qT[r][:, t0:t0 + n], n, kl, self.mbias[:, h:h + 1], scale, slot, [("qT", r)])

                    def epi(t0=t0, n=n, r=r, slot=slot):
                        self.recip(rr[:, 0:n], self.ps[:, 5 + slot, 0:n], [("ps", 5 + slot)], ["rr"])
                        self.tt("dve", ybuf[r][:, t0:t0 + n], self.ps[:, 3 + slot, 0:n], rr[:, 0:n], ALU.mult, [("ps", 3 + slot), "rr"], [("ybuf", r)])
                    self.pipe_push(epi)
                tmax = NT if need_ctx else T

                def store(h=h, r=r, tmax=tmax):
                    self.dma("pool", self.yT[h][:, 0:tmax], ybuf[r][:, 0:tmax], [("ybuf", r)], [("yT", h)])
                self.pipe_push(store)
            self.pipe_flush()
            P.stage_end()

    def stage_win(self, l, need_ctx):
        P = self.P
        i2 = l // 2
        scale = 128.0 ** -0.5
        with ExitStack() as es:
            sb = lambda n, s, d: self.sb(es, n, s, d)
            qT = [sb("qT", [128, NT], BF16) for _ in range(2)]
            kT = [sb("kT", [128, NT], BF16) for _ in range(2)]
            V = [sb("V", [128, NTILE, 128], BF16) for _ in range(2)]
            wmf = sb("wmf", [128, 6, 512], F32)
            wm = sb("wm", [128, 6, 512], BF16)
            pT = [sb("pT", [128, 512], BF16) for _ in range(4)]
            ybuf = [sb("ybuf", [128, NT], BF16) for _ in range(2)]
            rr = sb("rr", [128, 512], F32)
            sk = sb("sk", [128, 8], F32)
            esk = sb("esk", [128, 8], F32)
            self.sr = 0
            self.pr = 0
            self.pipe = []
            self.pipe_depth = 2
            self.score_bounds(es, [(16 + h, 16 + h, 24 + h // 4) for h in range(8)], scale)
            self.dma("sp", wmf[:], self.c_win.rearrange("r p q -> p r q"), [], ["wmf"])
            self.cp("dve", wm[:], wmf[:], ["wmf"], ["mask"])
            self.dma("sp", sk[:], self.sink[i2], [], ["sk"])
            for h in range(8):
                r = h % 2
                kv = h // 4
                self.dma("sp", qT[r][:], self.qT_b[h], [], [("qT", r)])
                self.dma("sp", kT[r][:], self.kT_b[kv], [], [("kT", r)])
                self.dma("sp", V[r][:], self.v_b[:, kv * 128:(kv + 1) * 128].rearrange("(t p) d -> p t d", p=128), [], [("V", r)])
                self.act(esk[:, h:h + 1], sk[:, h:h + 1], AF.Exp, ["sk", "mbias"], [("esk", h)], bias=self.mbias[:, 16 + h:17 + h])
                blocks = [(b4 * 512, 512, b4) for b4 in range(4)]
                if need_ctx:
                    blocks.append((T, C, None))
                for bi, (t0, n, b4) in enumerate(blocks):
                    rk = [("kT", r), ("V", r)]
                    kl = []
                    if b4 is not None:
                        for t in range(4 * b4 - 1, 4 * b4 + 5):
                            if 0 <= t < 16:
                                kl.append((kT[r][:, t * 128:(t + 1) * 128], V[r][:, t, :], wm[:, t - 4 * b4 + 1, :], rk))
                    kl += [(kT[r][:, m * 128:(m + 1) * 128], V[r][:, m, :], None, rk) for m in (16, 17)]
                    slot = bi % 2
                    self.attn_keys(pT, qT[r][:, t0:t0 + n], n, kl, self.mbias[:, 16 + h:17 + h], scale, slot, [("qT", r)])

                    def epi(t0=t0, n=n, r=r, slot=slot, h=h):
                        self.ts("dve", rr[:, 0:n], self.ps[:, 5 + slot, 0:n], esk[:, h:h + 1], None, ALU.add, None, [("ps", 5 + slot), ("esk", h)], ["rr"])
                        self.recip(rr[:, 0:n], rr[:, 0:n], ["rr"], ["rr"])
                        self.tt("dve", ybuf[r][:, t0:t0 + n], self.ps[:, 3 + slot, 0:n], rr[:, 0:n], ALU.mult, [("ps", 3 + slot), "rr"], [("ybuf", r)])
                    self.pipe_push(epi)
                tmax = NT if need_ctx else T

                def store(h=h, r=r, tmax=tmax):
                    self.dma("pool", self.yT[8 + h][:, 0:tmax], ybuf[r][:, 0:tmax], [("ybuf", r)], [("yT", 8 + h)])
                self.pipe_push(store)
            self.pipe_flush()
            P.stage_end()

    def stage_gdn(self, l):
        P = self.P
        i2 = l // 2
        NDT = self.ndt
        orders = [[16, 17] + list(range(16)), [17, 16] + list(range(15, -1, -1))]
        with ExitStack() as es:
            sb = lambda n, s, d: self.sb(es, n, s, d)
            tri = sb("tri", [128, 4, 128], F32)
            trin = sb("trin", [128, 4, 128], NDT) if NDT != F32 else tri
            identn = self.ident_f if NDT == F32 else self.ident_b
            ab = sb("ab", [128, NTILE, 32], F32)
            gvec = sb("gvec", [128, 32], F32)
            negA = sb("negA", [128, 16], F32)
            gsb = sb("gsb", [128, NTILE, 16], F32)
            bsb = sb("bsb", [128, NTILE, 16], F32)
            nbs = sb("nbs", [128, NTILE, 16], F32)
            tot = sb("tot", [128, NTILE, 16], F32)
            glast = sb("glast", [128, NTILE, 16], F32)
            gc = sb("gc", [128, NTILE, 16], F32)
            eg = sb("eg", [128, NTILE, 16], F32)
            kd = sb("kd", [128, NTILE, 16], F32)
            bg = sb("bg", [128, NTILE, 16], F32)
            S = [[sb("S", [128, 4, 128], F32) for _ in range(2)] for _ in range(2)]
            Sb = [[sb("Sb", [128, 4, 128], BF16) for _ in range(2)] for _ in range(2)]
            self.dma("sp", tri[:], self.c_tri.rearrange("a p f -> p a f"), [], ["tri"])
            if NDT != F32:
                self.cp("dve", trin[:], tri[:], ["tri"], ["trin"])
            self.dma("sp", ab[:], self.gab.rearrange("(t p) c -> p t c", p=128), [], ["ab"])
            self.dma("sp", gvec[:], self.gdn_vec[i2], [], ["gvec"])
            self.act(negA[:], gvec[:, 0:16], AF.Exp, ["gvec"], ["negA"])
            self.ts("dve", negA[:], negA[:], -1.0, None, ALU.mult, None, ["negA"], ["negA"])
            self.tt("dve", gsb[:], ab[:, :, 0:16], gvec[:, 16:32].unsqueeze(1).to_broadcast([128, NTILE, 16]), ALU.add, ["ab", "gvec"], ["gsb"])
            self.act(gsb[:], gsb[:], AF.Exp, ["gsb"], ["gsb"])
            self.act(gsb[:], gsb[:], AF.Ln, ["gsb"], ["gsb"], bias=1.0, scale=1.0)
            self.tt("dve", gsb[:], gsb[:], negA[:].unsqueeze(1).to_broadcast([128, NTILE, 16]), ALU.mult, ["gsb", "negA"], ["gsb"])
            self.act(bsb[:], ab[:, :, 16:32], AF.Sigmoid, ["ab"], ["bsb"])
            self.ts("dve", nbs[:], bsb[:], -1.0, None, ALU.mult, None, ["bsb"], ["nbs"])
            gflat = gsb[:].rearrange("p t c -> p (t c)")
            self.mm(self.ps[:, 0, 0:NTILE * 16], self.ones_f[:], gflat, True, True, ["gsb", "ones_f"], [("ps", 0)])
            self.cp("dve", tot[:].rearrange("p t c -> p (t c)"), self.ps[:, 0, 0:NTILE * 16], [("ps", 0)], ["tot"])
            self.act(glast[:], tot[:], AF.Exp, ["tot"], ["glast"])
            for d in range(2):
                self.mm(self.ps[:, 1 + d, 0:NTILE * 8], tri[:, d, :], gsb[:, :, d * 8:(d + 1) * 8], True, True, ["gsb", "tri"], [("ps", 1 + d)])
                self.cp("dve", gc[:, :, d * 8:(d + 1) * 8], self.ps[:, 1 + d, 0:NTILE * 8].rearrange("p (t c) -> p t c", c=8), [("ps", 1 + d)], ["gc"])
            self.act(eg[:], gc[:], AF.Exp, ["gc"], ["eg"])
            self.tt("dve", kd[:], tot[:], gc[:], ALU.subtract, ["tot", "gc"], ["kd"])
            self.act(kd[:], kd[:], AF.Exp, ["kd"], ["kd"])
            self.tt("dve", bg[:], bsb[:], eg[:], ALU.mult, ["bsb", "eg"], ["bg"])
            for d in range(2):
                for hg in range(2):
                    P.op("pool", lambda e, d=d, hg=hg: e.memset(S[d][hg][:], 0.0), [], [("S", d, hg)])
                    P.op("pool", lambda e, d=d, hg=hg: e.memset(Sb[d][hg][:], 0.0), [], [("Sb", d, hg)])
            qTt = [[sb("qTt", [128, 8, 128], BF16) for _ in range(2)] for _ in range(2)]
            kTt = [[sb("kTt", [128, 8, 128], BF16) for _ in range(2)] for _ in range(2)]
            ktm = [[sb("ktm", [128, 8, 128], BF16) for _ in range(2)] for _ in range(1)]
            vtm = [[sb("vtm", [128, 8, 128], BF16) for _ in range(2)] for _ in range(1)]
            vb = [[sb("vb", [128, 8, 128], BF16) for _ in range(2)] for _ in range(2)]
            kbg = [[sb("kbg", [128, 8, 128], BF16) for _ in range(2)] for _ in range(2)]
            kdc = [[sb("kdc", [128, 8, 128], BF16) for _ in range(2)] for _ in range(2)]
            Ug = [[sb("Ug", [128, 8, 128], F32) for _ in range(2)] for _ in range(1)]
            def slotbufs(name, dt, nring):
                return [[[sb(name, [128, 4, 128], dt) for _ in range(nring)] for _ in range(2)] for _ in range(2)]
            Eb = slotbufs("Eb", F32, 1)
            ETb = slotbufs("ETb", F32, 1)
            Pb = slotbufs("Pb", NDT, 2)
            PTb = slotbufs("PTb", NDT, 2)
            RTb = slotbufs("RTb", NDT, 2)
            TTb = slotbufs("TTb", BF16, 1)
            aT = slotbufs("aT", BF16, 2)
            ub = slotbufs("ub", F32, 2)
            wT = slotbufs("wT", BF16, 2)
            vn = slotbufs("vn", BF16, 1)
            ot = slotbufs("ot", F32, 1)
            oo = slotbufs("oo", F32, 1)
            qv = self.qT_a.rearrange("h p t -> p h t")
            kv = self.kT_a.rearrange("h p t -> p h t")
            self.bank = 0

            def nb():
                b = self.bank % 8
                self.bank += 1
                return b

            def bc_h(ap2):
                return ap2.unsqueeze(1).to_broadcast([128, 4, 128])

            def bc_e(ap1):
                return ap1.unsqueeze(2).to_broadcast([128, ap1.shape[1], 128])

            def prep_ug(s):
                for d in range(2):
                    c = orders[d][s]
                    cs = slice(d * 8, (d + 1) * 8)
                    self.tt("dve", Ug[0][d][:], tri[:, d, :].unsqueeze(1).to_broadcast([128, 8, 128]), bc_e(gsb[:, c, cs]), ALU.mult,
                            ["tri", "gsb"], [("Ug", d)])

            def precompute(s):
                sr = s % 2
                for d in range(2):
                    c = orders[d][s]
                    tk = (sr, d)
                    self.dma("sp", qTt[sr][d][:], qv[:, :, c * 128:(c + 1) * 128], [], [("qTt",) + tk])
                    self.dma("sp", kTt[sr][d][:], kv[:, :, c * 128:(c + 1) * 128], [], [("kTt",) + tk])
                    self.dma("sp", ktm[0][d][:].rearrange("p h e -> p (h e)"), self.k_tm[c * 128:(c + 1) * 128, :], [], [("ktm", d)])
                    self.dma("sp", vtm[0][d][:].rearrange("p h e -> p (h e)"), self.v_a[c * 128:(c + 1) * 128, :], [], [("vtm", d)])
                    cs = slice(d * 8, (d + 1) * 8)
                    self.tt("dve", vb[sr][d][:], vtm[0][d][:], bc_e(bsb[:, c, cs]), ALU.mult, [("vtm", d), "bsb"], [("vb",) + tk])
                    self.tt("pool", kbg[sr][d][:], ktm[0][d][:], bc_e(bg[:, c, cs]), ALU.mult, [("ktm", d), "bg"], [("kbg",) + tk])
                    self.tt("pool", kdc[sr][d][:], ktm[0][d][:], bc_e(kd[:, c, cs]), ALU.mult, [("ktm", d), "kd"], [("kdc",) + tk])
                slots = [(d, hg) for d in range(2) for hg in range(2)]
                lvl = getattr(self, "gdn_lvl", 9)
                if lvl <= 1:
                    return
                for (d, hg) in slots:
                    c = orders[d][s]
                    tk = (sr, d)
                    sk = (d, hg)
                    hs = range(hg * 4, hg * 4 + 4)
                    bD, bDT, bG, bR = nb(), nb(), nb(), nb()
                    for j, h in enumerate(hs):
                        self.mm(self.ps[:, bD, j * 128:(j + 1) * 128], Ug[0][d][:, h, :], tri[:, 2 + d, :], True, True, [("Ug", d), "tri"], [("ps", bD)])
                    for j, h in enumerate(hs):
                        self.mm(self.ps[:, bDT, j * 128:(j + 1) * 128], tri[:, 2 + d, :], Ug[0][d][:, h, :], True, True, [("Ug", d), "tri"], [("ps", bDT)])
                    for j, h in enumerate(hs):
                        self.mm(self.ps[:, bG, j * 128:(j + 1) * 128], kTt[sr][d][:, h, :], kTt[sr][d][:, h, :], True, True, [("kTt",) + tk], [("ps", bG)])
                    for j, h in enumerate(hs):
                        self.mm(self.ps[:, bR, j * 128:(j + 1) * 128], kTt[sr][d][:, h, :], qTt[sr][d][:, h, :], True, True,
                                [("kTt",) + tk, ("qTt",) + tk], [("ps", bR)])
                    E = Eb[d][hg][0]
                    ET = ETb[d][hg][0]
                    p4 = lambda b: self.ps[:, b, :].rearrange("p (h e) -> p h e", h=4)
                    self.act(E[:], p4(bD), AF.Exp, [("ps", bD)], [("E",) + sk])
                    self.tt("dve", E[:], E[:], bc_h(tri[:, 2 + d, :]), ALU.mult, [("E",) + sk, "tri"], [("E",) + sk])
                    self.tt("dve", E[:], p4(bG), E[:], ALU.mult, [("ps", bG), ("E",) + sk], [("E",) + sk])
                    P0 = Pb[d][hg][0]
                    hsl = slice(d * 8 + hg * 4, d * 8 + hg * 4 + 4)
                    self.tt("dve", P0[:], E[:], bc_e(nbs[:, c, hsl]), ALU.mult, [("E",) + sk, "nbs"], [("P", 0) + sk])
                    self.act(ET[:], p4(bDT), AF.Exp, [("ps", bDT)], [("ET",) + sk])
                    self.tt("dve", ET[:], ET[:], bc_h(tri[:, d, :]), ALU.mult, [("ET",) + sk, "tri"], [("ET",) + sk])
                    self.tt("dve", aT[d][hg][sr][:], p4(bR), ET[:], ALU.mult, [("ps", bR), ("ET",) + sk], [("aT", sr) + sk])
                    bT = nb()
                    for j in range(4):
                        self.mm(self.ps[:, bT, j * 128:(j + 1) * 128], P0[:, j, :], identn[:], True, True, [("P", 0) + sk], [("ps", bT)])
                    self.cp("act", PTb[d][hg][0][:], p4(bT), [("ps", bT)], [("PT", 0) + sk])
                    self.tt("dve", RTb[d][hg][0][:], p4(bT), bc_h(identn[:]), ALU.add, [("ps", bT)], [("RT", 0) + sk])
                if lvl <= 2:
                    return
                for k in range(1, 7):
                    cur, prv = k % 2, (k - 1) % 2
                    for (d, hg) in slots:
                        sk = (d, hg)
                        p4 = lambda b: self.ps[:, b, :].rearrange("p (h e) -> p h e", h=4)
                        Pp, PTp = Pb[d][hg][prv], PTb[d][hg][prv]
                        Pn, PTn = Pb[d][hg][cur], PTb[d][hg][cur]
                        bP = nb()
                        for j in range(4):
                            self.mm(self.ps[:, bP, j * 128:(j + 1) * 128], PTp[:, j, :], Pp[:, j, :], True, True,
                                    [("P", prv) + sk, ("PT", prv) + sk], [("ps", bP)])
                        if k < 6:
                            bPT = nb()
                            for j in range(4):
                                self.mm(self.ps[:, bPT, j * 128:(j + 1) * 128], Pp[:, j, :], PTp[:, j, :], True, True,
                                        [("P", prv) + sk, ("PT", prv) + sk], [("ps", bPT)])
                        self.cp("act", Pn[:], p4(bP), [("ps", bP)], [("P", cur) + sk])
                        if k < 6:
                            self.cp("act", PTn[:], p4(bPT), [("ps", bPT)], [("PT", cur) + sk])
                        bRT = nb()
                        for j in range(4):
                            self.mm(self.ps[:, bRT, j * 128:(j + 1) * 128], Pn[:, j, :], RTb[d][hg][prv][:, j, :], True, True,
                                    [("P", cur) + sk, ("RT", prv) + sk], [("ps", bRT)])
                        self.tt("dve", RTb[d][hg][cur][:], p4(bRT), RTb[d][hg][prv][:], ALU.add, [("ps", bRT), ("RT", prv) + sk], [("RT", cur) + sk])
                if lvl <= 3:
                    return
                for (d, hg) in slots:
                    sk = (d, hg)
                    tk = (sr, d)
                    p4 = lambda b: self.ps[:, b, :].rearrange("p (h e) -> p h e", h=4)
                    self.cp("act", TTb[d][hg][0][:], RTb[d][hg][0][:], [("RT", 0) + sk], [("TT",) + sk])
                    bU, bW = nb(), nb()
                    for j in range(4):
                        h = hg * 4 + j
                        self.mm(self.ps[:, bU, j * 128:(j + 1) * 128], TTb[d][hg][0][:, j, :], vb[sr][d][:, h, :], True, True,
                                [("TT",) + sk, ("vb",) + tk], [("ps", bU)])
                    for j in range(4):
                        h = hg * 4 + j
                        self.mm(self.ps[:, bW, j * 128:(j + 1) * 128], kbg[sr][d][:, h, :], TTb[d][hg][0][:, j, :], True, True,
                                [("TT",) + sk, ("kbg",) + tk], [("ps", bW)])
                    self.cp("act", ub[d][hg][sr][:], p4(bU), [("ps", bU)], [("ub", sr) + sk])
                    self.cp("dve", wT[d][hg][sr][:], p4(bW), [("ps", bW)], [("wT", sr) + sk])

            def recur(s):
                sr = s % 2
                for d in range(2):
                    c = orders[d][s]
                    tk = (sr, d)
                    for hg in range(2):
                        sk = (d, hg)
                        p4 = lambda b: self.ps[:, b, :].rearrange("p (h e) -> p h e", h=4)
                        hsl = slice(d * 8 + hg * 4, d * 8 + hg * 4 + 4)
                        bW, bO1, bO2, bS = nb(), nb(), nb(), nb()
                        for j in range(4):
                            self.mm(self.ps[:, bW, j * 128:(j + 1) * 128], wT[d][hg][sr][:, j, :], Sb[d][hg][:, j, :], True, True,
                                    [("wT", sr) + sk, ("Sb",) + sk], [("ps", bW)])
                        for j in range(4):
                            h = hg * 4 + j
                            self.mm(self.ps[:, bO1, j * 128:(j + 1) * 128], qTt[sr][d][:, h, :], Sb[d][hg][:, j, :], True, True,
                                    [("qTt",) + tk, ("Sb",) + sk], [("ps", bO1)])
                        V_ = vn[d][hg][0]
                        self.tt("dve", V_[:], ub[d][hg][sr][:], p4(bW), ALU.subtract, [("ub", sr) + sk, ("ps", bW)], [("vn",) + sk])
                        for j in range(4):
                            self.mm(self.ps[:, bO2, j * 128:(j + 1) * 128], aT[d][hg][sr][:, j, :], V_[:, j, :], True, True,
                                    [("aT", sr) + sk, ("vn",) + sk], [("ps", bO2)])
                        for j in range(4):
                            h = hg * 4 + j
                            self.mm(self.ps[:, bS, j * 128:(j + 1) * 128], kdc[sr][d][:, h, :], V_[:, j, :], True, True,
                                    [("kdc",) + tk, ("vn",) + sk], [("ps", bS)])
                        O_ = oo[d][hg][0]
                        self.tt("dve", ot[d][hg][0][:], p4(bO1), bc_e(eg[:, c, hsl]), ALU.mult, [("ps", bO1), "eg"], [("ot",) + sk])
                        self.tt("dve", O_[:], ot[d][hg][0][:], p4(bO2), ALU.add, [("ot",) + sk, ("ps", bO2)], [("oo",) + sk])
                        self.dma("pool", self.go[d, c * 128:(c + 1) * 128, hg * 512:(hg + 1) * 512], O_[:].rearrange("p h e -> p (h e)"),
                                 [("oo",) + sk], [("go", d, c, hg)])
                        self.tt("dve", S[d][hg][:], S[d][hg][:], bc_e(glast[:, c, hsl]), ALU.mult, [("S",) + sk, "glast"], [("S",) + sk])
                        self.tt("dve", S[d][hg][:], S[d][hg][:], p4(bS), ALU.add, [("S",) + sk, ("ps", bS)], [("S",) + sk])
                        self.cp("act", Sb[d][hg][:], S[d][hg][:], [("S",) + sk], [("Sb",) + sk])

            stop = getattr(self, "gdn_stop", None)
            if stop != "pre":
                prep_ug(0)
                precompute(0)
                prep_ug(1)
            nsteps = NTILE if stop is None else (0 if stop in ("pre", "pc0") else int(stop))
            for s in range(nsteps):
                if s + 1 < NTILE:
                    precompute(s + 1)
                if s + 2 < NTILE:
                    prep_ug(s + 2)
                recur(s)
            P.stage_end()
        with ExitStack() as es:
            sb = lambda n, s, d: self.sb(es, n, s, d)
            of = [sb("of", [128, 8, 128], F32) for _ in range(2)]
            obk = [sb("obk", [128, 8, 128], F32) for _ in range(2)]
            zb = [sb("zb", [128, 8, 128], BF16) for _ in range(2)]
            sz = sb("sz", [128, 8, 128], F32)
            sq = sb("sqg", [128, 8, 128], F32)
            ss = sb("ss", [128, 8], F32)
            st = sb("sst", [128, 8], F32)
            an = sb("an", [128, 8, 128], BF16)
            gn = sb("gn", [128, 128], F32)
            ybuf = sb("ybuf8", [128, 8, NT], BF16)
            self.dma("sp", gn[:], self.gng[i2], [], ["gn"])
            bank = 0
            for m in range(NTILE):
                r = m % 2
                rows = slice(m * 128, (m + 1) * 128)
                self.dma("sp", of[r][:].rearrange("p h e -> p (h e)"), self.go[0, rows, :], [], [("of", r)])
                self.dma("sp", obk[r][:].rearrange("p h e -> p (h e)"), self.go[1, rows, :], [], [("obk", r)])
                self.dma("sp", zb[r][:].rearrange("p h e -> p (h e)"), self.gz[rows, :], [], [("zb", r)])
                self.tt("dve", of[r][:], of[r][:], obk[r][:], ALU.add, [("of", r), ("obk", r)], [("of", r)])
                self.act(sq[:], of[r][:], AF.Square, [("of", r)], ["sqg"])
                P.op("dve", lambda e: e.tensor_reduce(out=ss[:], in_=sq[:], axis=AX.X, op=ALU.add), ["sqg"], ["ss"])
                self.rsqrt(ss[:], ss[:], 1.0 / 128, st[:], "sst", ["ss"], ["ss"])
                self.tt("dve", of[r][:], of[r][:], ss[:].unsqueeze(2).to_broadcast([128, 8, 128]), ALU.mult, [("of", r), "ss"], [("of", r)])
                self.tt("dve", of[r][:], of[r][:], gn[:].unsqueeze(1).to_broadcast([128, 8, 128]), ALU.mult, [("of", r), "gn"], [("of", r)])
                self.act(sz[:], zb[r][:], AF.Silu, [("zb", r)], ["sz"])
                self.tt("dve", an[:], of[r][:], sz[:], ALU.mult, [("of", r), "sz"], ["an"])
                for hg in range(2):
                    b = bank % 8
                    bank += 1
                    for j in range(4):
                        self.mm(self.ps[:, b, j * 128:(j + 1) * 128], an[:, hg * 4 + j, :], self.ident_b[:], True, True, ["an"], [("ps", b)])
                    self.cp("act", ybuf[:, hg * 4:(hg + 1) * 4, m * 128:(m + 1) * 128], self.ps[:, b, :].rearrange("p (h e) -> p h e", h=4),
                            [("ps", b)], [("ybuf", m, hg)])
            for h in range(8):
                self.dma("pool", self.yT[h], ybuf[:, h, :], [("ybuf", m, h // 4) for m in range(NTILE)], [("yT", h)])
            P.stage_end()


    def build(self, layer_list=None, do_final=True, upto=None):
        self.declare()
        layer_list = list(range(DEPTH)) if layer_list is None else layer_list
        with ExitStack() as es:
            self.setup(es)
            self.stage_load_inputs()
            for li, l in enumerate(layer_list):
                need_ctx = l < DEPTH - 1
                tiles = TT if need_ctx else TT[:4]
                if li == 0:
                    self.stage_mod(l)
                if upto == "mod":
                    break
                self.stage_inproj(l)
                if upto == "inproj":
                    break
                if l % 2 == 0:
                    self.stage_gdn(l)
                    if upto == "gdn":
                        break
                    self.stage_diff(l, need_ctx)
                else:
                    self.stage_na(l, need_ctx)
                    self.stage_win(l, need_ctx)
                if upto == "mixer":
                    break
                self.stage_outproj(l, tiles)
                if upto == "outproj":
                    break
                nxt = layer_list[li + 1] if li + 1 < len(layer_list) else None
                self.stage_ffn(l, tiles, bg_mod=nxt)
                if nxt is not None:
                    self.cur = 1 - self.cur
            if do_final and upto is None:
                self.stage_final()
            self.P.final_wait()
        self.P.close()


def _fm(v):
    return np.ascontiguousarray(np.asarray(v, np.float32).reshape(KC, 128).T)


def _rep(v):
    v = np.asarray(v, np.float32)
    return np.ascontiguousarray(np.broadcast_to(v[None], (128,) + v.shape))


def prep_shared(inp):
    f = lambda a: np.ascontiguousarray(np.asarray(a, np.float32))
    sh = {}
    for k in ("ada_w", "w_in_even", "w_in_odd", "w_out", "ffn_w_gate", "ffn_w_up", "ffn_w_down"):
        sh[k] = f(inp[k])
    sh["ada_b_fm"] = np.ascontiguousarray(f(inp["ada_b"]).reshape(DEPTH, 96, 128).transpose(0, 2, 1))
    sh["nmg_fm"] = np.stack([_fm(inp["norm_mix_g"][l]) for l in range(DEPTH)])
    sh["nfg_fm"] = np.stack([_fm(inp["norm_ffn_g"][l]) for l in range(DEPTH)])
    sh["fng_fm"] = _fm(inp["final_norm_g"])
    cw = f(inp["gdn_conv_w"])
    sh["conv_fm"] = np.ascontiguousarray(cw.reshape(2, 3, 24, 128).transpose(0, 3, 2, 1))
    gv = np.concatenate([f(inp["gdn_a_log"]).reshape(2, 16), f(inp["gdn_dt_bias"]).reshape(2, 16)], axis=1)
    sh["gdn_vec"] = np.stack([_rep(gv[i]) for i in range(2)])
    sh["gng_rep"] = np.stack([_rep(f(inp["gdn_norm_g"])[i]) for i in range(2)])
    lam = np.stack([f(inp["diff_lambda_q1"]), f(inp["diff_lambda_k1"]), f(inp["diff_lambda_q2"]), f(inp["diff_lambda_k2"])], axis=1)
    sh["lam_rep"] = np.stack([_rep(lam[i]) for i in range(2)])
    sh["subln_fm"] = np.ascontiguousarray(f(inp["diff_subln_g"]).reshape(2, 128, 1))
    rpb = f(inp["na_rpb"])
    kr = np.repeat(np.arange(2), 64)
    kc = np.tile(np.arange(64), 2)
    nb = np.full((2, 8, 128, N_NA_MASK, 128), -30000.0, np.float32)
    for mid, (delta, ok) in enumerate(_NA_MASKS):
        dr = np.clip(2 * delta + kr[:, None] - kr[None, :] + 7, 0, 14)
        dc = np.clip(kc[:, None] - kc[None, :] + 15, 0, 30)
        g = rpb[:, :, dr, dc]
        nb[:, :, :, mid, :] = np.where(ok[None, None] > 0, g, np.float32(-30000.0))
    sh["na_bias"] = nb
    sh["sink_rep"] = np.stack([_rep(f(inp["win_sink"])[i]) for i in range(2)])
    c = _consts()
    sh["c_ident"] = c["ident"]
    sh["c_tri"] = c["tri"]
    sh["c_rope_d"] = c["rope_d"]
    sh["c_rope_w"] = c["rope_w"]
    sh["c_perm"] = c["perm"]
    sh["c_win_mask"] = c["win_mask"]
    return sh


def prep_core(inp, b):
    x = np.ascontiguousarray(np.asarray(inp["x"][b], np.float32))
    ctx = np.ascontiguousarray(np.asarray(inp["ctx"][b], np.float32))
    cf = np.stack([_fm(inp["c"][b]), _fm(inp["c_ctx"])], axis=2)
    return {"x": x, "ctx": ctx, "c_fm": np.ascontiguousarray(cf)}


_CACHE = {}


def kernel(**inputs):
    n = 8
    nc = bass.Bass("TRN2", target_bir_lowering=False)
    bld = Builder(nc)
    bld.build()
    shared = prep_shared(inputs)
    in_maps = []
    for b in range(n):
        m = dict(shared)
        m.update(prep_core(inputs, b))
        in_maps.append(m)
    res = run_bass_kernel_spmd(nc, in_maps, core_ids=list(range(n)))
    return np.stack([np.asarray(r["out"], np.float32) for r in res.results], axis=0)
```
